# Optimizing a Trainium2 kernel written in Bass

```python
import jax, jax.numpy as jnp
from jax import lax
import numpy as np

D_MODEL = 1024
BATCH = 2
SEQ = 8192
DEPTH = 2

GRID_W = 64
CTX_LEN = 256
Q_BLOCK = 128
ROPE_THETA = 10000.0
NORM_EPS = 1e-6
N_MOD = 9
D_FF = 2816

SC_WIDTH = 512
SC_KERNEL = 3
MLA_HEADS = 8
MLA_NOPE = 64
MLA_ROPE = 32
MLA_V = 64
MLA_Q_RANK = 256
MLA_KV_RANK = 128
E_IN = 3 * SC_WIDTH + MLA_Q_RANK + MLA_KV_RANK + MLA_ROPE
E_MIX = SC_WIDTH + MLA_HEADS * MLA_V
GQA_HEADS = 8
GQA_KV_HEADS = 2
GQA_HEAD_DIM = 64
CONF_WIDTH = 512
CONF_KERNEL = 31
O_IN = (GQA_HEADS + 2 * GQA_KV_HEADS) * GQA_HEAD_DIM + 2 * CONF_WIDTH
O_MIX = GQA_HEADS * GQA_HEAD_DIM + CONF_WIDTH
N_EVEN = (DEPTH + 1) // 2
N_ODD = DEPTH // 2

kernel_name = 'hybrid_dit_shortconv_mla_gqa_conformer_macaron'


def rms_norm(x, g):
    xf = x.astype(jnp.float32)
    y = xf * lax.rsqrt(jnp.mean(xf * xf, axis=-1, keepdims=True) + NORM_EPS)
    return (y * g.astype(jnp.float32)).astype(x.dtype)


def layer_norm(x, g, b):
    xf = x.astype(jnp.float32)
    mu = jnp.mean(xf, axis=-1, keepdims=True)
    var = jnp.mean(jnp.square(xf - mu), axis=-1, keepdims=True)
    y = (xf - mu) * lax.rsqrt(var + NORM_EPS) * g.astype(jnp.float32) + b.astype(jnp.float32)
    return y.astype(x.dtype)


def modulation(cond, w, b):
    m = jax.nn.silu(cond) @ w + b
    return m.reshape(-1, N_MOD, 1, D_MODEL)


def adaln(t, g, mm, j):
    return rms_norm(t, g) * (1 + mm[:, 3 * j + 1]) + mm[:, 3 * j]


def swiglu(t, w_in, w_out):
    gt, up = jnp.split(t @ w_in, 2, axis=-1)
    return (jax.nn.silu(gt) * up) @ w_out


def depthwise_conv(x, w, b=None):
    k, ch = w.shape
    y = lax.conv_general_dilated(x, w[:, None, :].astype(x.dtype), window_strides=(1,),
                                 padding=[(k // 2, k // 2)],
                                 dimension_numbers=('NWC', 'WIO', 'NWC'),
                                 feature_group_count=ch)
    if b is not None:
        y = y + b
    return y


def axial_rope(n_rows, rot_dim):
    row = jnp.repeat(jnp.arange(n_rows, dtype=jnp.float32), GRID_W)
    col = jnp.tile(jnp.arange(GRID_W, dtype=jnp.float32), n_rows)
    nf = rot_dim // 4
    inv = ROPE_THETA ** (-jnp.arange(nf, dtype=jnp.float32) / nf)
    ang = jnp.concatenate([row[:, None] * inv, col[:, None] * inv], axis=-1)
    return jnp.cos(ang), jnp.sin(ang)


def apply_rope(x, cos, sin):
    xp = x.reshape(x.shape[:-1] + (-1, 2))
    x0, x1 = xp[..., 0], xp[..., 1]
    c = cos[:, None, :].astype(x.dtype)
    s = sin[:, None, :].astype(x.dtype)
    return jnp.stack([x0 * c - x1 * s, x0 * s + x1 * c], axis=-1).reshape(x.shape)


def blocked_attention(q, k, v, scale):
    b, n, hq, dh = q.shape
    hkv, dv = k.shape[2], v.shape[3]
    g = hq // hkv
    nb = n // Q_BLOCK
    qb = q.reshape(b, nb, Q_BLOCK, hkv, g, dh).transpose(1, 0, 2, 3, 4, 5)

    def one_block(qq):
        s = jnp.einsum('btkgd,bskd->bkgts', qq, k).astype(jnp.float32) * scale
        p = jax.nn.softmax(s, axis=-1).astype(v.dtype)
        return jnp.einsum('bkgts,bskd->btkgd', p, v)

    o = lax.map(one_block, qb)
    return o.transpose(1, 0, 2, 3, 4, 5).reshape(b, n, hq, dv)


def _mla_qkv(qa, kva, kr, q_norm, w_q_b, kv_norm, w_kv_b, rope):
    b, n, _ = qa.shape
    q = (rms_norm(qa, q_norm) @ w_q_b).reshape(b, n, MLA_HEADS, MLA_NOPE + MLA_ROPE)
    kv = (rms_norm(kva, kv_norm) @ w_kv_b).reshape(b, n, MLA_HEADS, MLA_NOPE + MLA_V)
    q_nope, q_rope = q[..., :MLA_NOPE], q[..., MLA_NOPE:]
    k_nope, v = kv[..., :MLA_NOPE], kv[..., MLA_NOPE:]
    kr = kr[:, :, None, :]
    if rope is not None:
        q_rope = apply_rope(q_rope, *rope)
        kr = apply_rope(kr, *rope)
    k = jnp.concatenate([k_nope, jnp.broadcast_to(kr, (b, n, MLA_HEADS, MLA_ROPE))], axis=-1)
    q = jnp.concatenate([q_nope, q_rope], axis=-1)
    return q, k, v


def even_mixer(h, hc, rope, w_in, conv_w, q_norm, w_q_b, kv_norm, w_kv_b, w_out, ctx_out):
    cuts = [SC_WIDTH, 2 * SC_WIDTH, 3 * SC_WIDTH, 3 * SC_WIDTH + MLA_Q_RANK,
            3 * SC_WIDTH + MLA_Q_RANK + MLA_KV_RANK]
    sb, sc, sx, qa, kva, kr = jnp.split(h @ w_in, cuts, axis=-1)
    sbc, scc, sxc, qac, kvac, krc = jnp.split(hc @ w_in, cuts, axis=-1)
    q, k, v = _mla_qkv(qa, kva, kr, q_norm, w_q_b, kv_norm, w_kv_b, rope)
    qc, kc, vc = _mla_qkv(qac, kvac, krc, q_norm, w_q_b, kv_norm, w_kv_b, None)
    scale = (MLA_NOPE + MLA_ROPE) ** -0.5
    b, n, _ = h.shape
    o = blocked_attention(q, jnp.concatenate([k, kc], axis=1), jnp.concatenate([v, vc], axis=1), scale)
    y_a = sb * depthwise_conv(sc * sx, conv_w)
    y = jnp.concatenate([y_a, o.reshape(b, n, MLA_HEADS * MLA_V)], axis=-1) @ w_out
    yc = None
    if ctx_out:
        oc = blocked_attention(qc, kc, vc, scale)
        y_ac = sbc * depthwise_conv(scc * sxc, conv_w)
        yc = jnp.concatenate([y_ac, oc.reshape(b, hc.shape[1], MLA_HEADS * MLA_V)], axis=-1) @ w_out
    return y, yc


def odd_mixer(h, hc, rope, w_in, q_norm, k_norm, conv_w, conv_b, ln_g, ln_b, w_out, ctx_out):
    qw = GQA_HEADS * GQA_HEAD_DIM
    kw = GQA_KV_HEADS * GQA_HEAD_DIM
    cuts = [qw, qw + kw, qw + 2 * kw]

    def split(t):
        b, n, _ = t.shape
        q, k, v, u = jnp.split(t @ w_in, cuts, axis=-1)
        q = rms_norm(q.reshape(b, n, GQA_HEADS, GQA_HEAD_DIM), q_norm)
        k = rms_norm(k.reshape(b, n, GQA_KV_HEADS, GQA_HEAD_DIM), k_norm)
        v = v.reshape(b, n, GQA_KV_HEADS, GQA_HEAD_DIM)
        return q, k, v, u

    def conformer_conv(u):
        a, gt = jnp.split(u, 2, axis=-1)
        z = depthwise_conv(a * jax.nn.sigmoid(gt), conv_w, conv_b)
        return jax.nn.silu(layer_norm(z, ln_g, ln_b))

    q, k, v, u = split(h)
    qc, kc, vc, uc = split(hc)
    q = apply_rope(q, *rope)
    k = apply_rope(k, *rope)
    scale = GQA_HEAD_DIM ** -0.5
    b, n, _ = h.shape
    o = blocked_attention(q, jnp.concatenate([k, kc], axis=1), jnp.concatenate([v, vc], axis=1), scale)
    y = jnp.concatenate([o.reshape(b, n, qw), conformer_conv(u)], axis=-1) @ w_out
    yc = None
    if ctx_out:
        oc = blocked_attention(qc, kc, vc, scale)
        yc = jnp.concatenate([oc.reshape(b, hc.shape[1], qw), conformer_conv(uc)], axis=-1) @ w_out
    return y, yc


def setup_inputs(seed: int = 0) -> dict:
    key = jax.random.key(seed)
    ks = iter(jax.random.split(key, 32))
    f32 = jnp.float32

    def nrm(shape, fan_in, gain=1.0):
        return gain * fan_in ** -0.5 * jax.random.normal(next(ks), shape, f32)

    def gains(shape):
        return 1.0 + 0.05 * jax.random.normal(next(ks), shape, f32)

    def bias(shape):
        return 0.02 * jax.random.normal(next(ks), shape, f32)

    return {
        'x': jax.random.normal(next(ks), (BATCH, SEQ, D_MODEL), f32),
        'c': jax.random.normal(next(ks), (BATCH, D_MODEL), f32),
        'ctx': jax.random.normal(next(ks), (BATCH, CTX_LEN, D_MODEL), f32),
        'c_ctx': jax.random.normal(next(ks), (D_MODEL,), f32),
        'w_mod': nrm((DEPTH, D_MODEL, N_MOD * D_MODEL), D_MODEL, 0.5),
        'b_mod': bias((DEPTH, N_MOD * D_MODEL)),
        'norm_g': gains((DEPTH, 3, D_MODEL)),
        'w_ffn_in': nrm((DEPTH, 2, D_MODEL, 2 * D_FF), D_MODEL),
        'w_ffn_out': nrm((DEPTH, 2, D_FF, D_MODEL), D_FF),
        'e_w_in': nrm((N_EVEN, D_MODEL, E_IN), D_MODEL),
        'e_conv_w': nrm((N_EVEN, SC_KERNEL, SC_WIDTH), SC_KERNEL),
        'e_q_norm': gains((N_EVEN, MLA_Q_RANK)),
        'e_w_q_b': nrm((N_EVEN, MLA_Q_RANK, MLA_HEADS * (MLA_NOPE + MLA_ROPE)), MLA_Q_RANK),
        'e_kv_norm': gains((N_EVEN, MLA_KV_RANK)),
        'e_w_kv_b': nrm((N_EVEN, MLA_KV_RANK, MLA_HEADS * (MLA_NOPE + MLA_V)), MLA_KV_RANK),
        'e_w_out': nrm((N_EVEN, E_MIX, D_MODEL), E_MIX),
        'o_w_in': nrm((N_ODD, D_MODEL, O_IN), D_MODEL),
        'o_q_norm': gains((N_ODD, GQA_HEAD_DIM)),
        'o_k_norm': gains((N_ODD, GQA_HEAD_DIM)),
        'o_conv_w': nrm((N_ODD, CONF_KERNEL, CONF_WIDTH), CONF_KERNEL),
        'o_conv_b': bias((N_ODD, CONF_WIDTH)),
        'o_ln_g': gains((N_ODD, CONF_WIDTH)),
        'o_ln_b': bias((N_ODD, CONF_WIDTH)),
        'o_w_out': nrm((N_ODD, O_MIX, D_MODEL), O_MIX),
        'final_g': gains((D_MODEL,)),
    }


def reference(x, c, ctx, c_ctx, w_mod, b_mod, norm_g, w_ffn_in, w_ffn_out,
              e_w_in, e_conv_w, e_q_norm, e_w_q_b, e_kv_norm, e_w_kv_b, e_w_out,
              o_w_in, o_q_norm, o_k_norm, o_conv_w, o_conv_b, o_ln_g, o_ln_b, o_w_out,
              final_g):
    ROWS = x.shape[1] // GRID_W
    rope_mla = axial_rope(ROWS, MLA_ROPE)
    rope_gqa = axial_rope(ROWS, GQA_HEAD_DIM)
    h, hc = x, ctx
    for l in range(DEPTH):
        last = l == DEPTH - 1
        m = modulation(c, w_mod[l], b_mod[l])
        mc = modulation(c_ctx, w_mod[l], b_mod[l])
        h = h + 0.5 * m[:, 2] * swiglu(adaln(h, norm_g[l, 0], m, 0), w_ffn_in[l, 0], w_ffn_out[l, 0])
        hc = hc + 0.5 * mc[:, 2] * swiglu(adaln(hc, norm_g[l, 0], mc, 0), w_ffn_in[l, 0], w_ffn_out[l, 0])
        a = adaln(h, norm_g[l, 1], m, 1)
        ac = adaln(hc, norm_g[l, 1], mc, 1)
        if l % 2 == 0:
            i = l // 2
            y, yc = even_mixer(a, ac, rope_mla, e_w_in[i], e_conv_w[i], e_q_norm[i], e_w_q_b[i],
                               e_kv_norm[i], e_w_kv_b[i], e_w_out[i], not last)
        else:
            i = l // 2
            y, yc = odd_mixer(a, ac, rope_gqa, o_w_in[i], o_q_norm[i], o_k_norm[i], o_conv_w[i],
                              o_conv_b[i], o_ln_g[i], o_ln_b[i], o_w_out[i], not last)
        h = h + m[:, 5] * y
        h = h + 0.5 * m[:, 8] * swiglu(adaln(h, norm_g[l, 2], m, 2), w_ffn_in[l, 1], w_ffn_out[l, 1])
        if not last:
            hc = hc + mc[:, 5] * yc
            hc = hc + 0.5 * mc[:, 8] * swiglu(adaln(hc, norm_g[l, 2], mc, 2), w_ffn_in[l, 1], w_ffn_out[l, 1])
    return rms_norm(h, final_g)
```

```python
import numpy as np
import ml_dtypes
import concourse.bass as bass
import concourse.mybir as mybir
from concourse.bass_utils import run_bass_kernel_spmd

F32 = mybir.dt.float32
BF16 = mybir.dt.bfloat16
AF = mybir.ActivationFunctionType
ALU = mybir.AluOpType

D = 1024
SEQ = 8192
T = 2048
HALO = 16
TT = 2146
LAT_END = 2080
CTX0 = 2082
NKR = 2112
NKEYS = 4 * NKR
TILES = [(0, 512), (512, 1024), (1024, 1536), (1536, 2048), (2048, TT)]
EPS = 1e-6
DFF = 2816
NSLOT = 4
SLOT = 4096

_off = {}
_n = 0
for _name, _w in [("CV", 16), ("BM", 144), ("NG", 48), ("FG", 8), ("ECW", 12), ("EQN", 2), ("EKN", 1),
                  ("OQN", 1), ("OQNS", 1), ("OKN", 1), ("OKNS", 1), ("OCW", 124), ("OCB", 4), ("OLG", 4),
                  ("OLB", 4), ("MKL", 32), ("MKC", 2)]:
    _off[_name] = _n
    _n += _w
NCONST = _n


class Op:
    __slots__ = ("eng", "fn", "deps", "kind", "needed", "sem", "val", "inc", "nobar")

    def __init__(self, eng, fn, kind):
        self.eng = eng
        self.fn = fn
        self.kind = kind
        self.deps = ()
        self.needed = False
        self.sem = None
        self.val = 0
        self.inc = 1


class Prog:
    CE = ("pe", "act", "dve", "pool")

    def __init__(self):
        self.q = {"pe": [], "act": [], "dve": [], "pool": [], "sp": []}
        self.last_w = {}
        self.readers = {}
        self.bar = []
        self.rr = 0
        self.ring_i = 0
        self.bar_pos = {}
        self.bank_lo, self.bank_n = 2, 6

    def add(self, eng, fn, reads=(), writes=(), kind="c", nobar=False):
        op = Op(eng, fn, kind)
        op.nobar = nobar
        deps = set() if nobar else set(self.bar)
        lw = self.last_w
        rd = self.readers
        for r in reads:
            w = lw.get(r)
            if w is not None:
                deps.add(w)
            if type(r) is tuple and r[0] == "ps":
                rs = rd.get(r)
                if rs:
                    for k_, o_ in rs.items():
                        if k_ != eng:
                            deps.add(o_)
        for w_ in writes:
            w = lw.get(w_)
            if w is not None:
                deps.add(w)
            rs = rd.get(w_)
            if rs:
                deps.update(rs.values())
        op.deps = deps
        key = eng if kind == "c" else id(op)
        for r in reads:
            d = rd.get(r)
            if d is None:
                d = rd[r] = {}
            d[key] = op
        for w_ in writes:
            lw[w_] = op
            rd[w_] = {}
        self.q[eng].append(op)
        return op

    def barrier(self):
        bar = []
        for e in self.CE:
            for op in reversed(self.q[e]):
                if op.kind == "c":
                    bar.append(op)
                    break
        for e in ("pool", "sp"):
            for op in self.q[e][self.bar_pos.get(e, 0):]:
                if op.kind == "d" and not op.nobar:
                    bar.append(op)
            self.bar_pos[e] = len(self.q[e])
        self.bar = bar

    def bank(self):
        b = self.bank_lo + self.rr % self.bank_n
        self.rr += 1
        return b

    def ring(self):
        s = self.ring_i % NSLOT
        self.ring_i += 1
        return s


def build_nc(stop=None):
    nc = bass.Bass("TRN2", target_bir_lowering=False)
    P = Prog()

    def din(name, shape, dt=F32):
        return nc.dram_tensor(name, list(shape), dt, kind="ExternalInput").ap()

    xT = din("xT", [128, 8, TT])
    consts_d = din("consts", [128, NCONST])
    ident_d = din("ident", [128, 128], BF16)
    tab0_d = din("tab0", [128, 2, TT], BF16)
    tab1_d = din("tab1", [128, 2, TT], BF16)
    w_mod = din("w_mod", [2, D, 9 * D])
    w_ffn_in = din("w_ffn_in", [2, 2, D, 2 * DFF])
    w_ffn_out = din("w_ffn_out", [2, 2, DFF, D])
    e_w_in = din("e_w_in", [D, 1952])
    e_wkr_pad = din("e_wkr_pad", [D, 2, 96])
    e_wqb = din("e_wqb", [256, 768])
    e_wqb_swp = din("e_wqb_swp", [256, 8, 96])
    e_wkvb = din("e_wkvb", [128, 1024])
    e_w_out = din("e_w_out", [D, D])
    o_wq = din("o_wq", [D, 2, 512])
    o_wk = din("o_wk", [D, 2, 128])
    o_w_in = din("o_w_in", [D, 1792])
    o_w_out = din("o_w_out", [D, D])
    out_d = nc.dram_tensor("out", [128, 8, TT if stop else T], F32, kind="ExternalOutput").ap()
    kvb0 = nc.dram_tensor("kvb0", [160, NKR], BF16)
    kvg0 = nc.dram_tensor("kvg0", [640, NKR], BF16)
    kb1 = nc.dram_tensor("kb1", [128, NKR], BF16)
    kg1 = nc.dram_tensor("kg1", [512, NKR], BF16)
    vb1 = nc.dram_tensor("vb1", [128, 17 * 128], BF16)
    vg1 = nc.dram_tensor("vg1", [512, 17 * 128], BF16)

    H = nc.alloc_sbuf_tensor("H", [128, 8, TT], F32)
    RING = nc.alloc_sbuf_tensor("RING", [128, NSLOT, SLOT], BF16)
    CONST = nc.alloc_sbuf_tensor("CONST", [128, NCONST], F32)
    SCB = nc.alloc_sbuf_tensor("SCB", [128, 16], BF16)
    MODL = nc.alloc_sbuf_tensor("MODL", [128, 72], F32)
    MODC = nc.alloc_sbuf_tensor("MODC", [128, 72], F32)
    MODS = [{"L": MODL, "C": MODC}, {"L": nc.alloc_sbuf_tensor("MODL1", [128, 72], F32), "C": nc.alloc_sbuf_tensor("MODC1", [128, 72], F32)}]
    MOD = dict(MODS[0])
    IDENT = nc.alloc_sbuf_tensor("IDENT", [128, 128], BF16)
    GSS = [{"L": nc.alloc_sbuf_tensor("GSL", [128, 3, 8], F32), "C": nc.alloc_sbuf_tensor("GSC", [128, 3, 8], F32)},
           {"L": nc.alloc_sbuf_tensor("GSL1", [128, 3, 8], F32), "C": nc.alloc_sbuf_tensor("GSC1", [128, 3, 8], F32)}]
    GATES = [{"L": nc.alloc_sbuf_tensor("GTL", [128, 3, 8], F32), "C": nc.alloc_sbuf_tensor("GTC", [128, 3, 8], F32)},
             {"L": nc.alloc_sbuf_tensor("GTL1", [128, 3, 8], F32), "C": nc.alloc_sbuf_tensor("GTC1", [128, 3, 8], F32)}]
    GS = dict(GSS[0])
    GATE = dict(GATES[0])
    LS = [0]

    def mk(name, g):
        return f"{name}{g}{LS[0]}"

    def use_layer(l):
        LS[0] = l
        MOD.update(MODS[l])
        GS.update(GSS[l])
        GATE.update(GATES[l])
    ONES = nc.alloc_sbuf_tensor("ONES", [128, 128], BF16)
    BD = nc.alloc_sbuf_tensor("BD", [128, 128], BF16)
    ONES32 = nc.alloc_sbuf_tensor("ONES32", [128, 128], F32)
    FONE = nc.alloc_sbuf_tensor("FONE", [128, 8], F32)
    FZERO = nc.alloc_sbuf_tensor("FZERO", [128, 8], F32)
    EPSV = nc.alloc_sbuf_tensor("EPSV", [128, 1], F32)
    ABASE = nc.sbuf_base
    ATOP = nc.sbuf_top
    acur = [ABASE]

    def aalloc(name, shape, dt, at=None):
        esz = 4 if dt == F32 else 2
        nbytes = int(np.prod(shape[1:])) * esz
        off = acur[0] if at is None else at
        off = (off + 31) // 32 * 32
        assert off + nbytes <= ATOP, (name, off, nbytes, ATOP)
        t = nc.alloc_sbuf_tensor_at(name, list(shape), dt, offset=off)
        if at is None:
            acur[0] = off + nbytes
        return t, off, off + nbytes

    def areset(to=None):
        acur[0] = ABASE if to is None else to

    PSALL = nc.alloc_psum_tensor("psall", [128, 4096], F32)
    ps = [PSALL[:, i * 512:(i + 1) * 512] for i in range(8)]

    def C(name, w=None, o=0):
        if name == "EPSV":
            return EPSV[:, 0:1]
        a = _off[name] + o
        return CONST[:, a:a + (1 if w is None else w)]

    def mm(out, lhsT, rhs, start=True, stop=True):
        return lambda e: e.matmul(out, lhsT=lhsT, rhs=rhs, start=start, stop=stop)

    def act(out, in_, func, scale=1.0, bias=0.0):
        return lambda e: e.activation(out=out, in_=in_, func=func, bias=bias, scale=scale)

    def tt(out, in0, in1, op):
        return lambda e: e.tensor_tensor(out=out, in0=in0, in1=in1, op=op)

    def ts(out, in0, s1, op0, s2=None, op1=None):
        if op1 is None:
            return lambda e: e.tensor_scalar(out=out, in0=in0, scalar1=s1, scalar2=None, op0=op0)
        return lambda e: e.tensor_scalar(out=out, in0=in0, scalar1=s1, scalar2=s2, op0=op0, op1=op1)

    def stt(out, in0, scalar, in1, op0, op1):
        return lambda e: e.scalar_tensor_tensor(out=out, in0=in0, scalar=scalar, in1=in1, op0=op0, op1=op1)

    def cp(out, in_):
        return lambda e: (e.tensor_copy(out=out, in_=in_) if hasattr(e, "tensor_copy") else e.activation(out=out, in_=in_, func=AF.Copy))

    def dma(out, in_):
        return lambda e: e.dma_start(out=out, in_=in_)

    def segs(ti):
        s, e = TILES[ti]
        if ti < 4:
            return [(s, e, "L")]
        return [(2048, LAT_END, "L"), (LAT_END, TT, "C")]

    def Hres(ti):
        return [("H", c, ti) for c in range(8)]

    def rslot(slot, shape):
        v = RING[:, slot, 0:int(np.prod(shape))]
        if len(shape) == 2:
            return v.rearrange("p (a b) -> p a b", a=shape[0])
        if len(shape) == 3:
            return v.rearrange("p (a b c) -> p a b c", a=shape[0], b=shape[1])
        return v

    for c in range(8):
        P.add("sp", dma(H[:, c, :], xT[:, c, :]), writes=[("H", c, ti) for ti in range(5)], kind="d")
    P.add("sp", dma(CONST[:], consts_d), writes=["CONST"], kind="d")
    P.add("sp", dma(IDENT[:], ident_d), writes=["IDENT"], kind="d")
    P.add("dve", lambda e: e.memset(ONES[:], 1.0), writes=["ONES"])
    P.add("dve", lambda e: e.memset(ONES32[:], 0.0), writes=["ONES32"])
    P.add("dve", lambda e: e.memset(ONES32[64:65, :], 1.0), reads=["ONES32"], writes=["ONES32"])
    P.add("dve", lambda e: e.memset(FONE[:], 1.0), writes=["FONE"])
    P.add("dve", lambda e: e.memset(FZERO[:], 0.0), writes=["FZERO"])
    P.add("dve", lambda e: e.memset(BD[:], 0.0), writes=["BD"])
    P.add("dve", lambda e: e.memset(BD[0:64, 0:64], 1.0), reads=["BD"], writes=["BD"])
    P.add("dve", lambda e: e.memset(BD[64:128, 64:128], 1.0), reads=["BD"], writes=["BD"])
    P.add("act", act(SCB[:], C("CV", 16), AF.Silu), reads=["CONST"], writes=["SCB"])

    MODW = [aalloc(f"MODW{i}", [128, 8, 512], BF16, at=ABASE + 55616 + i * 8192)[0] for i in range(6)]
    modw_i = [0]

    def mod_piece(l, p, ksuf=""):
        i = modw_i[0] % 6
        modw_i[0] += 1
        wv = w_mod[l].rearrange("(c p) f -> p c f", p=128)

        def dma_part():
            xr = [("H", c_, 0) for c_ in range(8)] if (l == 0 and p == 0) else []
            P.add("pool", dma(MODW[i][:], wv[:, :, p * 512:(p + 1) * 512]), reads=xr, writes=[("MODW", i)], kind="d")

        def comp_part():
            bank = P.bank()
            for fc in range(4):
                for d in range(8):
                    P.add("pe", mm(ps[bank][:, fc * 2:fc * 2 + 2], MODW[i][:, d, fc * 128:(fc + 1) * 128], SCB[:, d * 2:d * 2 + 2], d == 0, d == 7),
                          reads=[("MODW", i), "SCB"], writes=[("ps", bank)])
            psv = ps[bank][:, 0:8].rearrange("p (f two) -> p f two", two=2)
            bm = C("BM", 4, 72 * l + p * 4)
            P.add("dve", tt(MODS[l]["L"][:, p * 4:(p + 1) * 4], psv[:, :, 0], bm, ALU.add), reads=[("ps", bank), "CONST"], writes=[f"MODL{l}" + ksuf])
            P.add("dve", tt(MODS[l]["C"][:, p * 4:(p + 1) * 4], psv[:, :, 1], bm, ALU.add), reads=[("ps", bank), "CONST"], writes=[f"MODC{l}" + ksuf])
        return dma_part, comp_part

    def mod_derive(l, j, parts=("gs", "gate"), ksuf=""):
        for g in ("L", "C"):
            ng = C("NG", 8, (l * 3 + j) * 8)
            if "gs" in parts:
                P.add("dve", stt(GSS[l][g][:, j, :], MODS[l][g][:, (3 * j + 1) * 8:(3 * j + 2) * 8], 1.0, ng, ALU.add, ALU.mult),
                      reads=[f"MOD{g}{l}", "CONST"], writes=[f"GS{g}{l}"])
            if "gate" in parts:
                P.add("dve", ts(GATES[l][g][:, j, :], MODS[l][g][:, (3 * j + 2) * 8:(3 * j + 3) * 8], 1.0 if j == 1 else 0.5, ALU.mult),
                      reads=[f"MOD{g}{l}" + ksuf], writes=[f"GATE{g}{l}"])

    def adaln(dest, dres, scale_of, shift_of, tiles=range(5), tmp=None, post=None):
        SQ, RS, LT, TMP = tmp
        kk = [0]
        banks = {}
        tl = list(tiles)

        def stage_a(ti):
            s, e = TILES[ti]
            n = e - s
            b2 = ti % 2
            P.add("act", act(SQ[b2][:, :, 0:n], H[:, :, s:e], AF.Square), reads=Hres(ti), writes=[("SQ", b2)])
            bank = P.bank()
            banks[ti] = bank
            for c in range(8):
                P.add("pe", mm(ps[bank][:, 0:n], ONES[:], SQ[b2][:, c, 0:n], c == 0, c == 7), reads=[("SQ", b2), "ONES"], writes=[("ps", bank)])

        def stage_b(ti):
            s, e = TILES[ti]
            n = e - s
            b2 = ti % 2
            bank = banks[ti]
            P.add("act", act(LT[b2][:, 0:n], ps[bank][:, 0:n], AF.Ln, scale=1.0 / D, bias=C("EPSV")), reads=[("ps", bank), "CONST"], writes=[("LT", b2)])
            P.add("act", act(RS[b2][:, 0:n], LT[b2][:, 0:n], AF.Exp, scale=-0.5), reads=[("LT", b2)], writes=[("RS", b2)])

        def stage_c(ti):
            s, e = TILES[ti]
            b2 = ti % 2
            for c in range(8):
                for (ss, ee, g) in segs(ti):
                    t4 = kk[0] % 4
                    kk[0] += 1
                    P.add("dve", stt(TMP[t4][:, 0:ee - ss], H[:, c, ss:ee], scale_of(g, c), RS[b2][:, ss - s:ee - s], ALU.mult, ALU.mult),
                          reads=[("H", c, ti), ("RS", b2), mk("GS", g)], writes=[("TMP", t4)])
                    if c % 2 == 0:
                        P.add("act", act(dest(c, ss, ee), TMP[t4][:, 0:ee - ss], AF.Identity, bias=shift_of(g, c)),
                              reads=[("TMP", t4), mk("MOD", g)], writes=(dres(c, ti) if callable(dres) else [(dres, c, ti)]))
                    else:
                        P.add("dve", ts(dest(c, ss, ee), TMP[t4][:, 0:ee - ss], shift_of(g, c), ALU.add),
                              reads=[("TMP", t4), mk("MOD", g)], writes=(dres(c, ti) if callable(dres) else [(dres, c, ti)]))

        n_ = len(tl)
        stage_a(tl[0])
        for i_ in range(n_):
            if i_ + 1 < n_:
                stage_a(tl[i_ + 1])
            stage_b(tl[i_])
            stage_c(tl[i_])
            if post is not None:
                post(tl[i_])

    def adaln_tmp(at=None):
        if at is None:
            SQ = [aalloc(f"SQ{i}", [128, 8, 512], BF16)[0] for i in range(2)]
            RS = [aalloc(f"RS{i}", [128, 512], F32)[0] for i in range(2)]
            LT = [aalloc(f"LT{i}", [128, 512], F32)[0] for i in range(2)]
            TMP = [aalloc(f"TMP{i}", [128, 512], F32)[0] for i in range(4)]
            return SQ, RS, LT, TMP
        cur = [at]

        def a3(name, shape, dt):
            t_, _, end = aalloc(name, shape, dt, at=cur[0])
            cur[0] = end
            return t_
        SQ = [a3(f"SQx{i}", [128, 8, 512], BF16) for i in range(2)]
        RS = [a3(f"RSx{i}", [128, 512], F32) for i in range(2)]
        LT = [a3(f"LTx{i}", [128, 512], F32) for i in range(2)]
        TMP = [a3(f"TMPx{i}", [128, 512], F32) for i in range(4)]
        return SQ, RS, LT, TMP

    def ffn(l, k, j, A, HID, SG, tiles=range(5), sched=None):
        win = w_ffn_in[l, k].rearrange("(c p) f -> p c f", p=128)
        wout = w_ffn_out[l, k]
        groups = [[0, 1], [2, 3], [4, 5], [6, 7], [8, 9], [10]]
        sgk = 0
        carry = []
        for gi, grp in enumerate(groups):
            for pi_l, pi in enumerate(grp):
                slot = P.ring()
                rv = rslot(slot, [8, 2, 256])
                P.add("pool", dma(rv[:, :, 0, :], win[:, :, pi * 256:(pi + 1) * 256]), writes=[("ring", slot, 0)], kind="d", nobar=True)
                P.add("pool", dma(rv[:, :, 1, :], win[:, :, DFF + pi * 256:DFF + (pi + 1) * 256]), writes=[("ring", slot, 1)], kind="d", nobar=True)
                for sub in range(2):
                    fl = pi_l * 2 + sub
                    for ti in tiles:
                        s, e = TILES[ti]
                        n = e - s
                        bg = P.bank()
                        for d in range(8):
                            P.add("pe", mm(ps[bg][:, 0:n], rv[:, d, 0, sub * 128:(sub + 1) * 128], A[:, d, s:e], d == 0, d == 7),
                                  reads=[("ring", slot, 0), ("A", d, ti)], writes=[("ps", bg)])
                        bu = P.bank()
                        for d in range(8):
                            P.add("pe", mm(ps[bu][:, 0:n], rv[:, d, 1, sub * 128:(sub + 1) * 128], A[:, d, s:e], d == 0, d == 7),
                                  reads=[("ring", slot, 1), ("A", d, ti)], writes=[("ps", bu)])
                        sg = sgk % 2
                        sgk += 1
                        P.add("act", act(SG[sg][:, 0:n], ps[bg][:, 0:n], AF.Silu), reads=[("ps", bg)], writes=[("SG", sg)])
                        P.add("dve", tt(HID[:, fl, s:e], SG[sg][:, 0:n], ps[bu][:, 0:n], ALU.mult), reads=[("SG", sg), ("ps", bu)], writes=[("HID", fl, ti)])
            oslots = []
            for pi in grp:
                slot = P.ring()
                rv = rslot(slot, [2, 1024])
                P.add("pool", dma(rv, wout[pi * 256:(pi + 1) * 256, :].rearrange("(s p) d -> p s d", p=128)),
                      writes=[("ring", slot, 0), ("ring", slot, 1)], kind="d", nobar=True)
                oslots.append((slot, rv))
            pend = []
            for task in (sched[gi] if sched else []):
                if task[0] == "piece":
                    d_, c_ = mod_piece(task[1], task[2])
                    d_()
                    pend.append(c_)
                else:
                    pend.append((lambda t_=task: mod_derive(t_[1], t_[2])))
            nf = 2 * len(grp)
            for ti in tiles:
                s, e = TILES[ti]
                n = e - s
                for dc in range(8):
                    bo = P.bank()
                    for fl in range(nf):
                        slot, rv = oslots[fl // 2]
                        P.add("pe", mm(ps[bo][:, 0:n], rv[:, fl % 2, dc * 128:(dc + 1) * 128], HID[:, fl, s:e], fl == 0, fl == nf - 1),
                              reads=[("ring", slot, 0), ("ring", slot, 1), ("HID", fl, ti)], writes=[("ps", bo)])
                    for (ss, ee, g) in segs(ti):
                        P.add("dve", stt(H[:, dc, ss:ee], ps[bo][:, ss - s:ee - s], GATE[g][:, j, dc:dc + 1], H[:, dc, ss:ee], ALU.mult, ALU.add),
                              reads=[("ps", bo), mk("GATE", g), ("H", dc, ti)], writes=[("H", dc, ti)])
            for f_ in carry:
                f_()
            carry = pend
        for f_ in carry:
            f_()

    def ffn_block(l, k, j, tiles=range(5), sched=None):
        P.barrier()
        areset()
        A, _, a_end = aalloc("A", [128, 8, TT], BF16)
        tmp = adaln_tmp()
        adaln(lambda c, s, e: A[:, c, s:e], "A", lambda g, c: GS[g][:, j, c:c + 1], lambda g, c: MOD[g][:, 3 * j * 8 + c:3 * j * 8 + c + 1], tiles, tmp)
        P.barrier()
        if stop == "adaln0":
            for c in range(8):
                P.add("act", cp(H[:, c, :], A[:, c, :]), writes=[("H", c, ti) for ti in range(5)])
            return
        areset(a_end)
        HID = aalloc("HID", [128, 4, TT], BF16)[0]
        SG = [aalloc(f"SG{i}", [128, 512], F32)[0] for i in range(2)]
        ffn(l, k, j, A, HID, SG, tiles, sched)

    def dump_H():
        P.barrier()
        for c in range(8):
            P.add("sp", dma(out_d[:, c, :], H[:, c, :]), reads=[("H", c, ti) for ti in range(5)], writes=[("out", c)], kind="d")
        P.add("sp", None, reads=[("out", c) for c in range(8)], kind="w")

    P.add("dve", lambda e: e.memset(EPSV[:], EPS), writes=["CONST2"])

    comps = []
    for p_ in range(6):
        d_, c_ = mod_piece(0, p_, ksuf="" if p_ < 4 else "g")
        d_()
        comps.append(c_)
    for p_ in range(4):
        comps[p_]()
    mod_derive(0, 0, parts=("gs",))
    for p_ in range(4, 6):
        comps[p_]()
    mod_derive(0, 0, parts=("gate",), ksuf="g")
    sched0 = [[("piece", 0, 6 + 2 * gi), ("piece", 0, 7 + 2 * gi)] for gi in range(6)]
    sched0[2].append(("derive", 0, 1))
    sched0[5].append(("derive", 0, 2))
    ffn_block(0, 0, 0, sched=sched0)
    if stop in ("ffn1_0", "adaln0"):
        dump_H()
        return nc, P

    RG = [[0, 1, 2, 3], [4, 5, 6, 7]]
    NT = 66
    KEYT = [(j * 128, 128, j >= 64) for j in range(NT)]

    def ring_load(dmas):
        slot = P.ring()
        for i, (dst, src) in enumerate(dmas):
            keys = [("ring", slot, i)]
            if i == 0:
                keys += [("ring", slot, k) for k in range(len(dmas), 3)]
            P.add("pool", dma(dst(slot), src), writes=keys, kind="d", nobar=True)
        return slot

    def rkeys(slot):
        return [("ring", slot, k) for k in range(3)]

    def h_update(bank, ti, dc, j, s):
        for (ss, ee, g) in segs(ti):
            P.add("dve", stt(H[:, dc, ss:ee], ps[bank][:, ss - s:ee - s], GATE[g][:, j, dc:dc + 1], H[:, dc, ss:ee], ALU.mult, ALU.add),
                  reads=[("ps", bank), mk("GATE", g), ("H", dc, ti)], writes=[("H", dc, ti)])

    def mixer0():
        P.barrier()
        areset()
        TAB0 = aalloc("TAB0", [128, 2, TT], BF16)[0]
        QAN, _, pers_end = aalloc("QAN", [128, 2, TT], BF16)
        P.add("sp", dma(TAB0[:], tab0_d), writes=["TAB0"], kind="d")
        A, _, a_end = aalloc("A0", [128, 8, TT], BF16)
        tmp = adaln_tmp()
        adaln(lambda c, s, e: A[:, c, s:e], "A", lambda g, c: GS[g][:, 1, c:c + 1], lambda g, c: MOD[g][:, 24 + c:25 + c], range(5), tmp)
        P.barrier()
        areset(a_end)
        CKVN = aalloc("CKVN", [128, TT], BF16)[0]
        KRR = aalloc("KRR", [128, TT], BF16)[0]
        SQ2 = [aalloc(f"SQ2{i}", [128, 3, 512], BF16)[0] for i in range(1)]
        LT2 = [aalloc(f"LT2{i}", [128, 512], F32)[0] for i in range(2)]
        RS2 = [aalloc(f"RS2{i}", [128, 512], F32)[0] for i in range(2)]
        TS = [aalloc(f"TS{i}", [128, 512], F32)[0] for i in range(4)]
        CIN = aalloc("CIN", [128, 2080], BF16)[0]
        CINC = aalloc("CINC", [128, 66], BF16)[0]
        SBB = aalloc("SBB", [128, TT], BF16)[0]
        DG3 = aalloc("DG3", [128, 3, 128], BF16)[0]
        YA = aalloc("YA", [128, 4, TT], BF16)[0]
        P.add("dve", lambda e: e.memset(YA[:, :, 2048:CTX0], 0.0), writes=[("YA", cc_, 4) for cc_ in range(4)])
        ewv = e_w_in.rearrange("(c p) f -> p c f", p=128)
        s_qk = ring_load([(lambda sl: rslot(sl, [8, 384]), ewv[:, :, 1536:1920])])
        rv_qk = rslot(s_qk, [8, 384])
        s_kr = ring_load([(lambda sl: rslot(sl, [8, 2, 96]), e_wkr_pad.rearrange("(c p) s f -> p c s f", p=128))])
        rv_kr = rslot(s_kr, [8, 2, 96])
        tsk = 0
        for ti, (s, e) in enumerate(TILES):
            n = e - s
            b2 = 0
            bq = [P.bank(), P.bank()]
            for cq in range(2):
                for d in range(8):
                    P.add("pe", mm(ps[bq[cq]][:, 0:n], rv_qk[:, d, cq * 128:(cq + 1) * 128], A[:, d, s:e], d == 0, d == 7),
                          reads=rkeys(s_qk) + [("A", d, ti)], writes=[("ps", bq[cq])])
                P.add("act", act(SQ2[b2][:, cq, 0:n], ps[bq[cq]][:, 0:n], AF.Square), reads=[("ps", bq[cq])], writes=[("SQ2", b2, cq)])
            bs = P.bank()
            for cq in range(2):
                P.add("pe", mm(ps[bs][:, 0:n], ONES[:], SQ2[b2][:, cq, 0:n], cq == 0, cq == 1), reads=[("SQ2", b2, cq)], writes=[("ps", bs)])
            P.add("act", act(LT2[0][:, 0:n], ps[bs][:, 0:n], AF.Ln, scale=1.0 / 256, bias=C("EPSV")), reads=[("ps", bs)], writes=[("LT2", 0)])
            P.add("act", act(RS2[0][:, 0:n], LT2[0][:, 0:n], AF.Exp, scale=-0.5), reads=[("LT2", 0)], writes=[("RS2", 0)])
            for cq in range(2):
                P.add("dve", stt(QAN[:, cq, s:e], ps[bq[cq]][:, 0:n], C("EQN", 1, cq), RS2[0][:, 0:n], ALU.mult, ALU.mult),
                      reads=[("ps", bq[cq]), ("RS2", 0), "CONST"], writes=[("QAN", ti)])
            bk = P.bank()
            for d in range(8):
                P.add("pe", mm(ps[bk][:, 0:n], rv_qk[:, d, 256:384], A[:, d, s:e], d == 0, d == 7), reads=rkeys(s_qk) + [("A", d, ti)], writes=[("ps", bk)])
            P.add("act", act(SQ2[b2][:, 2, 0:n], ps[bk][:, 0:n], AF.Square), reads=[("ps", bk)], writes=[("SQ2", b2, 2)])
            bs2 = P.bank()
            P.add("pe", mm(ps[bs2][:, 0:n], ONES[:], SQ2[b2][:, 2, 0:n]), reads=[("SQ2", b2, 2)], writes=[("ps", bs2)])
            P.add("act", act(LT2[1][:, 0:n], ps[bs2][:, 0:n], AF.Ln, scale=1.0 / 128, bias=C("EPSV")), reads=[("ps", bs2)], writes=[("LT2", 1)])
            P.add("act", act(RS2[1][:, 0:n], LT2[1][:, 0:n], AF.Exp, scale=-0.5), reads=[("LT2", 1)], writes=[("RS2", 1)])
            P.add("dve", stt(CKVN[:, s:e], ps[bk][:, 0:n], C("EKN"), RS2[1][:, 0:n], ALU.mult, ALU.mult),
                  reads=[("ps", bk), ("RS2", 1), "CONST"], writes=[("CKVN", ti)])
            br = [P.bank(), P.bank()]
            for w_ in range(2):
                for d in range(8):
                    P.add("pe", mm(ps[br[w_]][0:96, 0:n], rv_kr[:, d, w_, :], A[:, d, s:e], d == 0, d == 7), reads=rkeys(s_kr) + [("A", d, ti)], writes=[("ps", br[w_])])
            ta, tb = TS[tsk % 4], TS[(tsk + 1) % 4]
            ka, kb_ = ("TS", tsk % 4), ("TS", (tsk + 1) % 4)
            tsk += 2
            P.add("dve", tt(ta[64:96, 0:n], ps[br[0]][64:96, 0:n], TAB0[64:96, 0, s:e], ALU.mult), reads=[("ps", br[0]), "TAB0"], writes=[ka])
            P.add("dve", tt(tb[64:96, 0:n], ps[br[1]][64:96, 0:n], TAB0[64:96, 1, s:e], ALU.mult), reads=[("ps", br[1]), "TAB0"], writes=[kb_])
            P.add("dve", tt(KRR[64:96, s:e], ta[64:96, 0:n], tb[64:96, 0:n], ALU.add), reads=[ka, kb_], writes=[("KRR", ti)])
        P.add("sp", dma(kvb0[0:128, 0:T], CKVN[:, 0:T]), reads=[("CKVN", t_) for t_ in range(4)], writes=[("kvb0", 0)], kind="d")
        P.add("sp", dma(kvb0[0:128, T:NKR], CKVN[:, CTX0:TT]), reads=[("CKVN", 4)], writes=[("kvb0", 1)], kind="d")
        P.add("sp", dma(kvb0[128:160, 0:T], KRR[64:96, 0:T]), reads=[("KRR", t_) for t_ in range(4)], writes=[("kvb0", 2)], kind="d")
        P.add("sp", dma(kvb0[128:160, T:NKR], KRR[64:96, CTX0:TT]), reads=[("KRR", 4)], writes=[("kvb0", 3)], kind="d")
        def conv_load(cc):
            return ring_load([(lambda sl, k=k: rslot(sl, [8, 3, 128])[:, :, k, :], ewv[:, :, k * 512 + cc * 128:k * 512 + (cc + 1) * 128]) for k in range(3)])
        conv_slots = {cc_: conv_load(cc_) for cc_ in (0, 1)}
        P.add("pool", lambda e: e.collective_compute("AllGather", ALU.bypass, replica_groups=RG, ins=[kvb0.ap().opt()], outs=[kvg0.ap().opt()]),
              reads=[("kvb0", i) for i in range(4)], writes=["kvg0"], kind="cc")

        for cc in range(4):
            s_c = conv_slots.pop(cc) if cc in conv_slots else conv_load(cc)
            rv = rslot(s_c, [8, 3, 128])
            for ti, (s, e) in enumerate(TILES):
                n = e - s
                bb = [P.bank(), P.bank(), P.bank()]
                for k in range(3):
                    for d in range(8):
                        P.add("pe", mm(ps[bb[k]][:, 0:n], rv[:, d, k, :], A[:, d, s:e], d == 0, d == 7), reads=[("ring", s_c, k), ("A", d, ti)], writes=[("ps", bb[k])])
                ta = TS[tsk % 4]
                ka = ("TS", tsk % 4)
                tsk += 1
                P.add("act", act(ta[:, 0:n], ps[bb[1]][:, 0:n], AF.Copy), reads=[("ps", bb[1])], writes=[ka])
                if ti < 4:
                    pieces = [(CIN[:, 16 + s:16 + e], 0, n, None)]
                else:
                    pieces = [(CIN[:, 0:16], 0, 16, C("MKL", 16, 0)), (CIN[:, 2064:2080], 16, 32, C("MKL", 16, 16)),
                              (CINC[:, 0:1], 32, 33, C("MKC", 1, 0)), (CINC[:, 65:66], 33, 34, C("MKC", 1, 1)), (CINC[:, 1:65], 34, 98, None)]
                for (dst, a0, a1, mk) in pieces:
                    P.add("dve", tt(dst, ta[:, a0:a1], ps[bb[2]][:, a0:a1], ALU.mult), reads=[ka, ("ps", bb[2])], writes=[("CIN", ti)])
                    if mk is not None:
                        P.add("dve", tt(dst, dst, mk, ALU.mult), reads=[("CIN", ti), "CONST"], writes=[("CIN", ti)])
                P.add("act", act(SBB[:, s:e], ps[bb[0]][:, 0:n], AF.Copy), reads=[("ps", bb[0])], writes=[("SBB", ti)])
            w = [C("ECW", 1, cc * 3 + k) for k in range(3)]
            cin_all = [("CIN", t_) for t_ in range(5)]
            for k in range(3):
                P.add("dve", ts(DG3[:, k, :], IDENT[:], w[k], ALU.mult), reads=["IDENT", "CONST"], writes=["DG3"])
            for ti in range(4):
                s, e = TILES[ti]
                bz = P.bank()
                for k in range(3):
                    P.add("pe", mm(ps[bz][:, 0:512], DG3[:, k, :], CIN[:, 15 + s + k:15 + s + k + 512], k == 0, k == 2), reads=["DG3"] + cin_all, writes=[("ps", bz)])
                P.add("dve", tt(YA[:, cc, s:e], SBB[:, s:e], ps[bz][:, 0:512], ALU.mult), reads=[("SBB", ti), ("ps", bz)], writes=[("YA", cc, ti)])
            bz = P.bank()
            for (c0, n_, srcf) in [(0, 15, lambda k: CIN[:, k:k + 15]), (15, 15, lambda k: CIN[:, 2063 + k:2063 + k + 15]), (30, 64, lambda k: CINC[:, k:k + 64])]:
                for k in range(3):
                    P.add("pe", mm(ps[bz][:, c0:c0 + n_], DG3[:, k, :], srcf(k), k == 0, k == 2), reads=["DG3"] + cin_all, writes=[("ps", bz)])
            P.add("dve", tt(YA[:, cc, 2049:2064], SBB[:, 2049:2064], ps[bz][:, 0:15], ALU.mult), reads=[("SBB", 4), ("ps", bz)], writes=[("YA", cc, 4)])
            P.add("dve", tt(YA[:, cc, 2064:2079], SBB[:, 2064:2079], ps[bz][:, 15:30], ALU.mult), reads=[("SBB", 4), ("ps", bz), ("YA", cc, 4)], writes=[("YA", cc, 4)])
            P.add("dve", tt(YA[:, cc, CTX0:TT], SBB[:, CTX0:TT], ps[bz][:, 30:94], ALU.mult), reads=[("SBB", 4), ("ps", bz), ("YA", cc, 4)], writes=[("YA", cc, 4)])
        s_o = ring_load([(lambda sl: rslot(sl, [4, 1024]), e_w_out[0:512, :].rearrange("(c p) d -> p c d", p=128))])
        rvo = rslot(s_o, [4, 1024])
        for ti, (s, e) in enumerate(TILES):
            n = e - s
            for dc in range(8):
                bo = P.bank()
                for cc in range(4):
                    P.add("pe", mm(ps[bo][:, 0:n], rvo[:, cc, dc * 128:(dc + 1) * 128], YA[:, cc, s:e], cc == 0, cc == 3), reads=rkeys(s_o) + [("YA", cc, ti)], writes=[("ps", bo)])
                h_update(bo, ti, dc, 1, s)
        if stop == "conv0":
            return
        P.barrier()
        areset(pers_end)
        CKVf = aalloc("CKV", [128, NKEYS], BF16)[0]
        KT = aalloc("KT", [128, NKEYS], BF16)[0]
        V = aalloc("V", [128, NT, 65], BF16)[0]
        Q = [aalloc(f"Q{i}", [128, TT], BF16)[0] for i in range(2)]
        E = [aalloc(f"E{i}", [128, 2, 512], BF16)[0] for i in range(4)]
        O = [aalloc(f"O{i}", [128, TT], BF16)[0] for i in range(2)]
        REC = [aalloc(f"REC{i}", [128, 512], F32)[0] for i in range(1)]
        OU = [aalloc(f"OU{i}", [128, 512], F32)[0] for i in range(2)]
        TS2 = [aalloc(f"TSB{i}", [128, 512], F32)[0] for i in range(1)]
        WQB = aalloc("WQB", [128, 2, 768], BF16)[0]
        WQBS = aalloc("WQBS", [128, 2, 8, 96], BF16)[0]
        WKVB = aalloc("WKVB", [128, 1024], BF16)[0]
        WO = [aalloc(f"WO{i}", [128, 1024], BF16)[0] for i in range(2)]
        kview = kvg0.ap().rearrange("(r q) t -> q r t", q=160)
        P.add("sp", dma(CKVf[:, 0:4 * T].rearrange("p (r t) -> p r t", r=4), kview[0:128, :, 0:T]), reads=["kvg0"], writes=[("CKV", 0)], kind="d")
        P.add("sp", dma(CKVf[:, 4 * T:NKEYS].rearrange("p (r t) -> p r t", r=4), kview[0:128, :, T:NKR]), reads=["kvg0"], writes=[("CKV", 1)], kind="d")
        P.add("sp", dma(KT[64:96, 0:4 * T].rearrange("p (r t) -> p r t", r=4), kview[128:160, :, 0:T]), reads=["kvg0"], writes=[("KTR", 0)], kind="d")
        P.add("sp", dma(KT[64:96, 4 * T:NKEYS].rearrange("p (r t) -> p r t", r=4), kview[128:160, :, T:NKR]), reads=["kvg0"], writes=[("KTR", 1)], kind="d")
        for i in range(2):
            P.add("dve", (lambda t: (lambda e: e.memset(t[:], 0.0)))(WO[i]), writes=[("WO", i)])
        P.add("pool", dma(WQB[:], e_wqb.rearrange("(c p) f -> p c f", p=128)), writes=["WQB"], kind="d")
        P.add("pool", dma(WQBS[:], e_wqb_swp.rearrange("(c p) h f -> p c h f", p=128)), writes=["WQBS"], kind="d")
        P.add("pool", dma(WKVB[:], e_wkvb), writes=["WKVB"], kind="d")
        P.add("dve", lambda e: e.memset(V[:, :, 64:65], 1.0), writes=["VONE"])
        for i in range(2):
            P.add("dve", (lambda t: (lambda e: e.memset(t[:], 0.0)))(O[i]), writes=[("O", i, t_) for t_ in range(5)])
        P.add("dve", lambda e: e.memset(REC[0][:], 0.0), writes=[("REC", 0)])
        attention(8, 96, 96 ** -0.5,
                  kgen=lambda h: kgen0(h, KT, CKVf, WKVB), vgen=lambda h: vgen0(h, V, CKVf, WKVB),
                  qgen=lambda h, qb: qgen0(h, qb, Q[qb], QAN, WQB, WQBS, TAB0, TS2),
                  kt_of=lambda h: (KT, 0, 96), v_of=lambda h, j: V[:, j, :], q_of=lambda h, qb, s, e: Q[qb][0:96, s:e],
                  qkeys=lambda h, qb, ti: [("Q", qb, ti)], kkeys=lambda ks, nk: [("KT", ks // 512), ("KT", (ks + nk - 1) // 512)],
                  vkeys=lambda j: [("V", j // 8)],
                  E=E, O=O, REC=REC, OU=OU, WO=WO, wo_src=lambda h: e_w_out[512 + h * 64:512 + (h + 1) * 64, :],
                  qtiles=[(0, 512, 0, False), (512, 1024, 1, False), (1024, 1536, 2, False), (1536, 2048, 3, False), (2048, LAT_END, 4, False), (CTX0, TT, 4, True)],
                  out_tiles=range(5), extra_reads=[("KTR", 0), ("KTR", 1), "VONE"])

    def kgen0(h, KT, CKVf, WKVB):
        for g in range(17):
            ks, ke = g * 512, min(NKEYS, (g + 1) * 512)
            n = ke - ks
            b = P.bank()
            P.add("pe", mm(ps[b][:, 0:n], WKVB[:, h * 128:(h + 1) * 128], CKVf[:, ks:ke]), reads=["WKVB", ("CKV", 0), ("CKV", 1)], writes=[("ps", b)])
            if g % 2 == 0:
                P.add("act", act(KT[0:64, ks:ke], ps[b][0:64, 0:n], AF.Copy), reads=[("ps", b)], writes=[("KT", g)])
            else:
                P.add("dve", cp(KT[0:64, ks:ke], ps[b][0:64, 0:n]), reads=[("ps", b)], writes=[("KT", g)])

    def vgen0(h, V, CKVf, WKVB):
        for j0 in range(0, NT, 8):
            cnt = min(8, NT - j0)
            b = P.bank()
            for j in range(j0, j0 + cnt):
                ks, nk, _ = KEYT[j]
                P.add("pe", mm(ps[b][0:nk, (j - j0) * 64:(j - j0 + 1) * 64], CKVf[:, ks:ks + nk], WKVB[:, h * 128 + 64:(h + 1) * 128]),
                      reads=["WKVB", ("CKV", 0), ("CKV", 1)], writes=[("ps", b)])
            src = ps[b][:, 0:cnt * 64].rearrange("p (j f) -> p j f", f=64)
            if (j0 // 8) % 2 == 0:
                P.add("dve", cp(V[:, j0:j0 + cnt, 0:64], src), reads=[("ps", b)], writes=[("V", j0 // 8)])
            else:
                P.add("act", act(V[:, j0:j0 + cnt, 0:64], src, AF.Copy), reads=[("ps", b)], writes=[("V", j0 // 8)])

    def qgen0(h, qb, Qb, QAN, WQB, WQBS, TAB0, TS2):
        for ti, (s, e) in enumerate(TILES):
            n = e - s
            b0, b1 = P.bank(), P.bank()
            for c in range(2):
                P.add("pe", mm(ps[b0][0:96, 0:n], WQB[:, c, h * 96:(h + 1) * 96], QAN[:, c, s:e], c == 0, c == 1), reads=["WQB", ("QAN", ti)], writes=[("ps", b0)])
            for c in range(2):
                P.add("pe", mm(ps[b1][0:96, 0:n], WQBS[:, c, h, :], QAN[:, c, s:e], c == 0, c == 1), reads=["WQBS", ("QAN", ti)], writes=[("ps", b1)])
            qk = ("Q", qb, ti)
            P.add("act", act(Qb[0:64, s:e], ps[b0][0:64, 0:n], AF.Copy), reads=[("ps", b0)], writes=[qk])
            P.add("dve", tt(TS2[0][64:96, 0:n], ps[b0][64:96, 0:n], TAB0[64:96, 0, s:e], ALU.mult), reads=[("ps", b0), "TAB0"], writes=[("TSB", 0)])
            P.add("dve", tt(Qb[64:96, s:e], ps[b1][64:96, 0:n], TAB0[64:96, 1, s:e], ALU.mult), reads=[("ps", b1), "TAB0", qk], writes=[qk])
            P.add("dve", tt(Qb[64:96, s:e], TS2[0][64:96, 0:n], Qb[64:96, s:e], ALU.add), reads=[("TSB", 0), qk], writes=[qk])


    def qgen1(h, qb, QP, Q1):
        g = h // 4
        if h == 4 or h == 5:
            P.add("dve", (lambda t: (lambda e: e.memset(t[0:64, :], 0.0)))(QP[qb]), reads=[("QP", qb)], writes=[("QP", qb)])
        src = Q1[g * 64:(g + 1) * 64, h % 4, :]
        dst = QP[qb][g * 64:(g + 1) * 64, :]
        if h % 2 == 0:
            P.add("dve", cp(dst, src), reads=[("Q1", h % 4, t_) for t_ in range(4)] + [("QP", qb)], writes=[("QP", qb)])
        else:
            P.add("act", act(dst, src, AF.Copy), reads=[("Q1", h % 4, t_) for t_ in range(4)] + [("QP", qb)], writes=[("QP", qb)])

    def attention(nheads, kdim, scale, kgen, vgen, qgen, kt_of, v_of, q_of, qkeys, kkeys, vkeys, E, O, REC, OU, WO, wo_src, qtiles, out_tiles, extra_reads):
        ek = 0
        npair = 0
        NE = len(E)
        LA = max(1, NE - 2)
        NSP = LA + 1
        ob = 0
        pend_fin = []
        pend_out = {}
        nfin = [0]

        pend_fin2 = []

        def finalize(h, qb, s, e, ti):
            nq = e - s

            def run():
                ou = OU[nfin[0] % len(OU)]
                ouk = ("OU", nfin[0] % len(OU))
                rc = REC[nfin[0] % len(REC)]
                rck = ("REC", nfin[0] % len(REC))
                nfin[0] += 1
                P.add("act", act(ou[0:65, 0:nq], ps[ob][0:65, 0:nq], AF.Copy), reads=[("ps", ob)], writes=[ouk])
                P.add("dve", lambda e_: e_.reciprocal(out=rc[64:65, 0:nq], in_=ou[64:65, 0:nq]), reads=[ouk], writes=[rck])

                def run2():
                    bb = P.bank()
                    P.add("pe", mm(ps[bb][:, 0:nq], ONES32[:], rc[:, 0:nq]), reads=[rck, "ONES32"], writes=[("ps", bb)])
                    P.add("dve", tt(O[qb][0:64, s:e], ou[0:64, 0:nq], ps[bb][0:64, 0:nq], ALU.mult), reads=[ouk, ("ps", bb)], writes=[("O", qb, ti)])
                pend_fin2.append(run2)
            return run

        def outproj(qb, ti):
            s, e = TILES[ti]
            n = e - s

            def one(dc):
                def run():
                    bo = P.bank()
                    P.add("pe", mm(ps[bo][:, 0:n], WO[qb][:, dc * 128:(dc + 1) * 128], O[qb][:, s:e]), reads=[("WO", qb), ("O", qb, ti)], writes=[("ps", bo)])
                    h_update(bo, ti, dc, 1, s)
                return run
            return [one(dc) for dc in range(8)]

        for h in range(nheads):
            qb = h % 2
            P.bank_lo, P.bank_n = 2, 6
            if kgen is not None:
                kgen(h)
                vgen(h)
            qgen(h, qb)
            P.bank_lo, P.bank_n = 1, 1
            P.add("pool", dma(WO[qb][0:64, :], wo_src(h)), writes=[("WO", qb)], kind="d")
            KTt, kp0, kp1 = kt_of(h)
            for qi, (s, e, ti, ctx_only) in enumerate(qtiles):
                nq = e - s
                keys = [(j, kt) for j, kt in enumerate(KEYT) if (kt[2] or not ctx_only)]
                pairs = [keys[i_:i_ + 2] for i_ in range(0, len(keys), 2)]
                slots = {}
                nk_tot = len(keys)
                done = 0
                for pi in range(len(pairs) + LA):
                    if pi < len(pairs):
                        pr = pairs[pi]
                        pb = 2 + 2 * (npair % NSP)
                        npair += 1
                        eb = ek % NE
                        ek += 1
                        rows = max(kt[1] for (_, kt) in pr)
                        for t_, (j, (ks, nk, _)) in enumerate(pr):
                            P.add("pe", mm(ps[pb + t_][0:nk, 0:nq], KTt[kp0:kp1, ks:ks + nk], q_of(h, qb, s, e)),
                                  reads=kkeys(ks, nk) + qkeys(h, qb, ti) + extra_reads, writes=[("ps", pb + t_)])
                        np_ = len(pr)
                        src = PSALL[0:rows, pb * 512:(pb + np_) * 512].rearrange("p (t c) -> p t c", t=np_)[:, :, 0:nq]
                        P.add("act", act(E[eb][0:rows, 0:np_, 0:nq], src, AF.Exp, scale=scale),
                              reads=[("ps", pb + t_) for t_ in range(np_)], writes=[("E", eb)])
                        slots[pi] = eb
                    if pi >= LA:
                        pr = pairs[pi - LA]
                        eb = slots[pi - LA]
                        for t_, (j, (ks, nk, _)) in enumerate(pr):
                            P.add("pe", mm(ps[ob][0:65, 0:nq], v_of(h, j)[0:nk, :], E[eb][0:nk, t_, 0:nq], done == 0, done == nk_tot - 1),
                                  reads=[("E", eb)] + vkeys(j) + extra_reads, writes=[("ps", ob)])
                            done += 1
                    last = pi == len(pairs) + LA - 1
                    if pi == 0 or last:
                        while pend_fin:
                            pend_fin.pop(0)()
                    if pi == 5 or last:
                        while pend_fin2:
                            pend_fin2.pop(0)()
                    fl = pend_out.get(qi)
                    if fl and (pi >= 6 and pi % 3 == 0 or last):
                        fl.pop(0)()
                        if last:
                            while fl:
                                fl.pop(0)()
                pend_fin.append(finalize(h, qb, s, e, ti))
            for k_ in sorted(pend_out):
                for f in pend_out.pop(k_):
                    f()
            for qi, ti in enumerate(out_tiles):
                pend_out[qi] = outproj(qb, ti)
        P.bank_lo, P.bank_n = 2, 6
        while pend_fin:
            pend_fin.pop(0)()
        while pend_fin2:
            pend_fin2.pop(0)()
        for k_ in sorted(pend_out):
            for f in pend_out.pop(k_):
                f()

    mixer0()
    if stop in ("conv0", "mix0"):
        dump_H()
        return nc, P
    sched1 = [[("piece", 1, 3 * gi + t_) for t_ in range(3)] for gi in range(6)]
    sched1[1].append(("derive", 1, 0))
    sched1[3].append(("derive", 1, 1))
    sched1[5].append(("derive", 1, 2))
    ffn_block(0, 1, 2, sched=sched1)
    if stop == "l0":
        dump_H()
        return nc, P

    def mixer1():
        areset()
        A, _, a_end = aalloc("A1", [128, 8, TT], BF16)
        tmp = adaln_tmp(at=ABASE + 55616)
        adaln(lambda c, s, e: A[:, c, s:e], "A", lambda g, c: GS[g][:, 1, c:c + 1], lambda g, c: MOD[g][:, 24 + c:25 + c], range(5), tmp)
        P.barrier()
        areset(a_end)
        owv = o_w_in.rearrange("(c p) f -> p c f", p=128)
        Q1, _, q1_end = aalloc("Q1", [128, 4, T], BF16)
        TAB1 = aalloc("TAB1", [128, 2, TT], BF16)[0]
        K1O = aalloc("K1O", [128, NKR], BF16)[0]
        VOWN = aalloc("VOWN", [128, 17, 128], BF16)[0]
        TS = [aalloc(f"TS2{i}", [128, 512], F32)[0] for i in range(4)]
        SQ = [aalloc(f"SQB{i}", [128, 512], BF16)[0] for i in range(2)]
        LTb = [aalloc(f"LTB{i}", [128, 512], F32)[0] for i in range(2)]
        RSb = [aalloc(f"RSB{i}", [128, 512], F32)[0] for i in range(2)]
        P.add("sp", dma(TAB1[:], tab1_d), writes=["TAB1"], kind="d")
        wqv = o_wq.rearrange("(c p) w f -> p c w f", p=128)
        wkv = o_wk.rearrange("(c p) w f -> p c w f", p=128)
        it = 0
        work = [("q", jq) for jq in range(4)] + [("k", 0)]
        for (kind, jq) in work:
            src = wqv[:, :, :, jq * 128:(jq + 1) * 128] if kind == "q" else wkv
            s_w = ring_load([(lambda sl, w_=w_: rslot(sl, [8, 2, 128])[:, :, w_, :], src[:, :, w_, :]) for w_ in range(2)])
            rv = rslot(s_w, [8, 2, 128])
            gn, gns = ("OQN", "OQNS") if kind == "q" else ("OKN", "OKNS")
            cols = [(s, e, ti, s) for ti, (s, e) in enumerate(TILES[:4])]
            if kind == "k":
                cols.append((CTX0, TT, 4, T))
            for (s, e, ti, dcol) in cols:
                n = e - s
                b2 = it % 2
                it += 1
                bb = [P.bank(), P.bank()]
                for w_ in range(2):
                    for d in range(8):
                        P.add("pe", mm(ps[bb[w_]][:, 0:n], rv[:, d, w_, :], A[:, d, s:e], d == 0, d == 7), reads=rkeys(s_w) + [("A", d, ti)], writes=[("ps", bb[w_])])
                P.add("act", act(SQ[b2][:, 0:n], ps[bb[0]][:, 0:n], AF.Square), reads=[("ps", bb[0])], writes=[("SQB", b2)])
                bs = P.bank()
                P.add("pe", mm(ps[bs][:, 0:n], BD[:], SQ[b2][:, 0:n]), reads=[("SQB", b2), "BD"], writes=[("ps", bs)])
                P.add("act", act(LTb[b2][:, 0:n], ps[bs][:, 0:n], AF.Ln, scale=1.0 / 64, bias=C("EPSV")), reads=[("ps", bs)], writes=[("LTB", b2)])
                P.add("act", act(RSb[b2][:, 0:n], LTb[b2][:, 0:n], AF.Exp, scale=-0.5), reads=[("LTB", b2)], writes=[("RSB", b2)])
                t1, t2 = TS[(2 * it) % 4], TS[(2 * it + 1) % 4]
                k1, k2 = ("TS", (2 * it) % 4), ("TS", (2 * it + 1) % 4)
                P.add("dve", stt(t1[:, 0:n], ps[bb[0]][:, 0:n], C(gn), TAB1[:, 0, s:e], ALU.mult, ALU.mult), reads=[("ps", bb[0]), "TAB1", "CONST"], writes=[k1])
                P.add("dve", stt(t2[:, 0:n], ps[bb[1]][:, 0:n], C(gns), TAB1[:, 1, s:e], ALU.mult, ALU.mult), reads=[("ps", bb[1]), "TAB1", "CONST"], writes=[k2])
                P.add("dve", tt(t1[:, 0:n], t1[:, 0:n], t2[:, 0:n], ALU.add), reads=[k1, k2], writes=[k1])
                if kind == "q":
                    P.add("dve", tt(Q1[:, jq, s:e], t1[:, 0:n], RSb[b2][:, 0:n], ALU.mult), reads=[k1, ("RSB", b2)], writes=[("Q1", jq, ti)])
                else:
                    P.add("dve", tt(K1O[:, dcol:dcol + n], t1[:, 0:n], RSb[b2][:, 0:n], ALU.mult), reads=[k1, ("RSB", b2)], writes=[("K1O", ti)])
        if stop in ("pb_q", "pb_only"):
            return
        s_v = ring_load([(lambda sl: rslot(sl, [8, 128]), owv[:, :, 640:768])])
        rvv = rslot(s_v, [8, 128])
        for t0_ in range(0, 17, 4):
            cnt = min(4, 17 - t0_)
            b = P.bank()
            for tq in range(t0_, t0_ + cnt):
                cs, nk = (tq * 128, 128) if tq < 16 else (CTX0, 64)
                ti = min(tq // 4, 4)
                for d in range(8):
                    P.add("pe", mm(ps[b][0:nk, (tq - t0_) * 128:(tq - t0_ + 1) * 128], A[:, d, cs:cs + nk], rvv[:, d, :], d == 0, d == 7),
                          reads=rkeys(s_v) + [("A", d, ti)], writes=[("ps", b)])
            P.add("act", act(VOWN[:, t0_:t0_ + cnt, :], ps[b][:, 0:cnt * 128].rearrange("p (j f) -> p j f", f=128), AF.Copy), reads=[("ps", b)], writes=[("VOWN", t0_ // 4)])
        if stop == "pb_v":
            return
        P.add("sp", dma(kb1[:, :], K1O[:]), reads=[("K1O", t_) for t_ in range(5)], writes=["kb1"], kind="d")
        P.add("sp", dma(vb1[:, :], VOWN[:].rearrange("p t f -> p (t f)")), reads=[("VOWN", t_) for t_ in range(5)], writes=[("vb1", 0), ("vb1", 1)], kind="d")
        if stop == "qkv1":
            return
        def u_load(cc):
            return ring_load([(lambda sl, k=k: rslot(sl, [8, 2, 128])[:, :, k, :], owv[:, :, 768 + k * 512 + cc * 128:768 + k * 512 + (cc + 1) * 128]) for k in range(2)])
        u_slots = {cc_: u_load(cc_) for cc_ in (0, 1)}
        P.add("pool", lambda e: e.collective_compute("AllGather", ALU.bypass, replica_groups=RG, ins=[kb1.ap().opt()], outs=[kg1.ap().opt()]),
              reads=["kb1"], writes=["kg1"], kind="cc")
        P.add("pool", lambda e: e.collective_compute("AllGather", ALU.bypass, replica_groups=RG, ins=[vb1.ap().opt()], outs=[vg1.ap().opt()]),
              reads=[("vb1", 0), ("vb1", 1)], writes=["vg1"], kind="cc")
        P.barrier()
        areset(q1_end)
        CIN1 = [aalloc(f"CIN1{i}", [128, 2080], BF16)[0] for i in range(2)]
        Z = aalloc("Z", [128, 4, T], BF16)[0]
        DG = [aalloc(f"DG{i}", [128, 31, 128], BF16)[0] for i in range(2)]
        TS = [aalloc(f"TS1{i}", [128, 512], F32)[0] for i in range(2)]
        ZSQ = aalloc("ZSQ", [128, 4, 512], BF16)[0]
        MEAN = aalloc("MEAN", [128, 512], F32)[0]
        VAR = aalloc("VAR", [128, 512], F32)[0]
        RS = aalloc("RS1", [128, 512], F32)[0]
        owv = o_w_in.rearrange("(c p) f -> p c f", p=128)
        tsk = [0]

        def uproj(cc):
            cb = cc % 2
            s_u = u_slots.pop(cc) if cc in u_slots else u_load(cc)
            rv = rslot(s_u, [8, 2, 128])
            for ti, (s, e) in enumerate(TILES):
                if ti == 4:
                    e = LAT_END
                n = e - s
                bb = [P.bank(), P.bank()]
                for k in range(2):
                    for d in range(8):
                        P.add("pe", mm(ps[bb[k]][:, 0:n], rv[:, d, k, :], A[:, d, s:e], d == 0, d == 7), reads=[("ring", s_u, k), ("A", d, ti)], writes=[("ps", bb[k])])
                ta = TS[tsk[0] % 2]
                ka = ("TS", tsk[0] % 2)
                tsk[0] += 1
                P.add("act", act(ta[:, 0:n], ps[bb[1]][:, 0:n], AF.Sigmoid), reads=[("ps", bb[1])], writes=[ka])
                if ti < 4:
                    pieces = [(CIN1[cb][:, 16 + s:16 + e], 0, n, None)]
                else:
                    pieces = [(CIN1[cb][:, 0:16], 0, 16, C("MKL", 16, 0)), (CIN1[cb][:, 2064:2080], 16, 32, C("MKL", 16, 16))]
                for (dst, a0, a1, mk) in pieces:
                    P.add("dve", tt(dst, ta[:, a0:a1], ps[bb[0]][:, a0:a1], ALU.mult), reads=[ka, ("ps", bb[0])], writes=[("CIN1", cb, ti)])
                    if mk is not None:
                        P.add("dve", tt(dst, dst, mk, ALU.mult), reads=[("CIN1", cb, ti), "CONST"], writes=[("CIN1", cb, ti)])

        def dgbuild(cc):
            cb = cc % 2
            for k in range(31):
                P.add("dve", ts(DG[cb][:, k, :], IDENT[:], C("OCW", 1, cc * 31 + k), ALU.mult), reads=["IDENT", "CONST"], writes=[("DG", cb)])

        def conv(cc):
            cb = cc % 2
            cin_all = [("CIN1", cb, t_) for t_ in range(5)]
            for ti in range(4):
                s, e = TILES[ti]
                bz = P.bank()
                for k in range(31):
                    P.add("pe", mm(ps[bz][:, 0:512], DG[cb][:, k, :], CIN1[cb][:, 1 + k + s:1 + k + s + 512], k == 0, k == 30), reads=[("DG", cb)] + cin_all, writes=[("ps", bz)])
                P.add("act", act(Z[:, cc, s:e], ps[bz][:, 0:512], AF.Identity, bias=C("OCB", 1, cc)), reads=[("ps", bz), "CONST"], writes=[("Z", cc, ti)])

        dgbuild(0)
        uproj(0)
        for cc in range(4):
            if cc + 1 < 4:
                dgbuild(cc + 1)
                uproj(cc + 1)
            conv(cc)
        s_o = ring_load([(lambda sl: rslot(sl, [4, 1024]), o_w_out[512:1024, :].rearrange("(c p) d -> p c d", p=128))])
        rvo = rslot(s_o, [4, 1024])
        lnb = {}

        def ln1(ti):
            s, e = TILES[ti]
            n = 512
            for cc in range(4):
                P.add("act", act(ZSQ[:, cc, :], Z[:, cc, s:e], AF.Square), reads=[("Z", cc, ti)], writes=[("ZSQ", cc)])
            bm, bv = (0, 1) if ti % 2 == 0 else (2, 3)
            lnb[ti] = (bm, bv)
            for cc in range(4):
                P.add("pe", mm(ps[bm][:, 0:n], ONES[:], Z[:, cc, s:e], cc == 0, cc == 3), reads=[("Z", cc, ti), "ONES"], writes=[("ps", bm)])
            for cc in range(4):
                P.add("pe", mm(ps[bv][:, 0:n], ONES[:], ZSQ[:, cc, :], cc == 0, cc == 3), reads=[("ZSQ", cc), "ONES"], writes=[("ps", bv)])

        def ln2(ti):
            n = 512
            bm, bv = lnb[ti]
            P.add("act", act(MEAN[:, 0:n], ps[bm][:, 0:n], AF.Copy, scale=1.0 / 512), reads=[("ps", bm)], writes=["MEAN"])
            P.add("dve", tt(VAR[:, 0:n], MEAN[:, 0:n], MEAN[:, 0:n], ALU.mult), reads=["MEAN"], writes=["VAR"])
            P.add("dve", stt(VAR[:, 0:n], ps[bv][:, 0:n], 1.0 / 512, VAR[:, 0:n], ALU.mult, ALU.subtract), reads=[("ps", bv), "VAR"], writes=["VAR"])
            P.add("act", act(VAR[:, 0:n], VAR[:, 0:n], AF.Ln, bias=C("EPSV")), reads=["VAR"], writes=["VAR"])
            P.add("act", act(RS[:, 0:n], VAR[:, 0:n], AF.Exp, scale=-0.5), reads=["VAR"], writes=["RS1"])

        def ln3(ti):
            s, e = TILES[ti]
            n = 512
            for cc in range(4):
                ta = TS[tsk[0] % 2]
                ka = ("TS", tsk[0] % 2)
                tsk[0] += 1
                P.add("dve", tt(ta[:, 0:n], Z[:, cc, s:e], MEAN[:, 0:n], ALU.subtract), reads=[("Z", cc, ti), "MEAN"], writes=[ka])
                P.add("dve", tt(ta[:, 0:n], ta[:, 0:n], RS[:, 0:n], ALU.mult), reads=[ka, "RS1"], writes=[ka])
                P.add("act", act(Z[:, cc, s:e], ta[:, 0:n], AF.Silu, scale=C("OLG", 1, cc), bias=C("OLB", 1, cc)), reads=[ka, "CONST"], writes=[("Z", cc, ti)])
            for dc in range(8):
                bo = P.bank()
                for cc in range(4):
                    P.add("pe", mm(ps[bo][:, 0:n], rvo[:, cc, dc * 128:(dc + 1) * 128], Z[:, cc, s:e], cc == 0, cc == 3), reads=rkeys(s_o) + [("Z", cc, ti)], writes=[("ps", bo)])
                h_update(bo, ti, dc, 1, s)

        P.bank_lo, P.bank_n = 4, 4
        ln1(0)
        for ti in range(4):
            if ti + 1 < 4:
                ln1(ti + 1)
            ln2(ti)
            ln3(ti)
        P.bank_lo, P.bank_n = 2, 6
        P.barrier()
        areset(q1_end)
        K1f = aalloc("K1", [128, NKEYS], BF16)[0]
        V1 = aalloc("V1", [128, NT, 2, 65], BF16)[0]
        QP = [aalloc(f"QP{i}", [128, T], BF16)[0] for i in range(2)]
        O = [aalloc(f"O1{i}", [128, T], BF16)[0] for i in range(2)]
        WO = [aalloc(f"WO1{i}", [128, 1024], BF16)[0] for i in range(2)]
        c2 = [ABASE]

        def a2(name, shape, dt):
            t_, _, end = aalloc(name, shape, dt, at=c2[0])
            c2[0] = end
            assert end <= a_end
            return t_
        VST = a2("VST", [128, NT, 128], BF16)
        E = [a2(f"E1{i}", [128, 2, 512], BF16) for i in range(4)]
        REC = [a2(f"REC1{i}", [128, 512], F32) for i in range(2)]
        OU = [a2(f"OU1{i}", [128, 512], F32) for i in range(2)]
        kv1 = kg1.ap().rearrange("(r q) t -> q r t", q=128)
        P.add("sp", dma(K1f[:, 0:4 * T].rearrange("p (r t) -> p r t", r=4), kv1[:, :, 0:T]), reads=["kg1"], writes=[("K1", 0)], kind="d")
        P.add("sp", dma(K1f[:, 4 * T:NKEYS].rearrange("p (r t) -> p r t", r=4), kv1[:, :, T:NKR]), reads=["kg1"], writes=[("K1", 1)], kind="d")
        vv1 = vg1.ap().rearrange("(r p) x -> p r x", p=128)
        P.add("sp", dma(VST[:, 0:64, :].rearrange("p (r t) f -> p r (t f)", r=4), vv1[:, :, 0:16 * 128]), reads=["vg1"], writes=[("VST", 0)], kind="d")
        for r in range(4):
            P.add("sp", dma(VST[(r % 2) * 64:(r % 2) * 64 + 64, 64 + r // 2, :], vg1[r * 128:r * 128 + 64, 16 * 128:17 * 128]), reads=["vg1"], writes=[("VST", 1 + r)], kind="d")
        for i in range(2):
            P.add("dve", (lambda t: (lambda e: e.memset(t[:], 0.0)))(WO[i]), writes=[("WO", i)])
            P.add("dve", (lambda t: (lambda e: e.memset(t[:], 0.0)))(O[i]), writes=[("O", i, t_) for t_ in range(5)])
            P.add("dve", (lambda t: (lambda e: e.memset(t[:], 0.0)))(QP[i]), writes=[("QP", i)])
            P.add("dve", (lambda t: (lambda e: e.memset(t[:], 0.0)))(REC[i]), writes=[("REC", i)])
        vst_all = [("VST", i) for i in range(5)]
        P.add("dve", cp(V1[:, :, 0, 0:64], VST[:, :, 0:64]), reads=vst_all, writes=[("V1", 0)])
        P.add("act", act(V1[:, :, 1, 0:64], VST[:, :, 64:128], AF.Copy), reads=vst_all, writes=[("V1", 1)])
        if stop == "gath1":
            return
        P.add("dve", lambda e: e.memset(V1[:, :, :, 64:65], 1.0), writes=["VONE1"])
        attention(8, 64, 0.125, None, None, lambda h, qb: qgen1(h, qb, QP, Q1),
                  kt_of=lambda h: (K1f, 0, 128), v_of=lambda h, j: V1[:, j, h // 4, :],
                  q_of=lambda h, qb, s, e: QP[qb][:, s:e],
                  qkeys=lambda h, qb, ti: [("QP", qb)], kkeys=lambda ks, nk: [("K1", 0), ("K1", 1)],
                  vkeys=lambda j: [("V1", 0), ("V1", 1)],
                  E=E, O=O, REC=REC, OU=OU, WO=WO, wo_src=lambda h: o_w_out[h * 64:(h + 1) * 64, :],
                  qtiles=[(s, e, ti, False) for ti, (s, e) in enumerate(TILES[:4])], out_tiles=range(4), extra_reads=["VONE1"])

    use_layer(1)
    ffn_block(1, 0, 0)
    if stop == "ffn1_1":
        dump_H()
        return nc, P
    mixer1()
    if stop in ("conf1", "mix1", "qkv1", "gath1", "pb_q", "pb_v", "pb_only"):
        dump_H()
        return nc, P
    ffn_block(1, 1, 2, tiles=range(4))
    areset()
    OUTB = [aalloc(f"OUTB{i}", [128, 8, 512], F32)[0] for i in range(2)]
    tmp = adaln_tmp(at=ABASE + 55616)
    a_keys = [("A", c_, t_) for c_ in range(8) for t_ in range(5)]
    def store_tile(ti):
        s, e = TILES[ti]
        P.add("sp", dma(out_d[:, :, s:e], OUTB[ti % 2][:]), reads=[("OUTB", c, ti % 2) for c in range(8)], writes=[("out", ti)], kind="d")
    adaln(lambda c, s, e: OUTB[(s // 512) % 2][:, c, 0:e - s], lambda c, ti: [("OUTB", c, ti % 2)] + a_keys, lambda g, c: C("FG", 1, c), lambda g, c: FZERO[:, 0:1], range(4), tmp, post=store_tile)
    P.add("sp", None, reads=[("out", ti) for ti in range(4)], kind="w")
    return nc, P


NDS = 16


def emit(nc, P):
    from contextlib import ExitStack
    for q, ops in P.q.items():
        for op in ops:
            for d in op.deps:
                if d.kind == "c" and d.eng == "pe" and q == "pe":
                    continue
                d.needed = True
    idx = {}
    ncc = 0
    for q, ops in P.q.items():
        nd = 0
        for op in ops:
            if op.kind == "c":
                if op.needed:
                    idx[q] = idx.get(q, 0) + 1
                    op.sem = q
                    op.val = idx[q]
            elif op.kind == "d":
                op.sem = (q, nd % NDS)
                op.val = 16 * (nd // NDS + 1)
                nd += 1
            elif op.kind == "cc":
                op.sem = ("cc", ncc)
                op.val = 1
                ncc += 1
    with ExitStack() as st:
        semobj = {}
        for q in ("pe", "act", "dve", "pool"):
            semobj[q] = st.enter_context(nc.semaphore("s_" + q))
        for q in ("pool", "sp"):
            for i in range(NDS):
                semobj[(q, i)] = st.enter_context(nc.semaphore(f"d_{q}{i}"))
        for i in range(ncc):
            semobj[("cc", i)] = st.enter_context(nc.semaphore(f"cc{i}"))
        block = st.enter_context(nc.Block())

        def run(q):
            def body(e):
                known = {}
                for op in P.q[q]:
                    waits = {}
                    for d in op.deps:
                        if d.kind == "c" and d.eng == "pe" and q == "pe":
                            continue
                        if d.sem is None:
                            continue
                        if waits.get(d.sem, 0) < d.val:
                            waits[d.sem] = d.val
                    if op.kind == "d" and op.val > 16:
                        if waits.get(op.sem, 0) < op.val - 16:
                            waits[op.sem] = op.val - 16
                    for s_, v in waits.items():
                        if known.get(s_, 0) < v:
                            e.wait_ge(semobj[s_], v)
                            known[s_] = v
                    if op.kind == "w":
                        continue
                    ins = op.fn(e)
                    if op.kind == "c":
                        if op.needed:
                            ins.then_inc(semobj[op.sem], 1)
                    elif op.kind == "d":
                        ins.then_inc(semobj[op.sem], 16)
                    else:
                        ins.then_inc(semobj[op.sem])
            return body

        block.tensor(run("pe"))
        block.scalar(run("act"))
        block.vector(run("dve"))
        block.gpsimd(run("pool"))
        block.sync(run("sp"))
    return nc


def _fm(v):
    v = np.asarray(v, np.float32)
    return v.reshape(-1, 128).T.copy()


def _rope_tab(rot_dim, pos_valid, rows, cols):
    nf = rot_dim // 4
    inv = (np.float32(10000.0) ** (-np.arange(nf, dtype=np.float32) / np.float32(nf))).astype(np.float32)
    ang = np.concatenate([rows[:, None].astype(np.float32) * inv, cols[:, None].astype(np.float32) * inv], axis=-1)
    cos = np.cos(ang).astype(np.float32)
    sin = np.sin(ang).astype(np.float32)
    cos = np.where(pos_valid[:, None], cos, 1.0)
    sin = np.where(pos_valid[:, None], sin, 0.0)
    cf = np.repeat(cos, 2, axis=1)
    sf = np.repeat(sin, 2, axis=1)
    sf[:, 0::2] *= -1.0
    return cf.T, sf.T


def _swap_pairs(w):
    idx = np.arange(w.shape[-1]) ^ 1
    return w[..., idx]


def host_prep(inp):
    g = {k: np.asarray(v) for k, v in inp.items()}
    x, ctx, c, c_ctx = g["x"], g["ctx"], g["c"], g["c_ctx"]
    shared = {}
    shared["w_mod"] = g["w_mod"]
    shared["w_ffn_in"] = g["w_ffn_in"]
    shared["w_ffn_out"] = g["w_ffn_out"]
    ewi = g["e_w_in"][0]
    shared["e_w_in"] = ewi
    kr = ewi[:, 1920:1952]
    pad = np.zeros((D, 2, 96), np.float32)
    pad[:, 0, 64:] = kr
    pad[:, 1, 64:] = _swap_pairs(kr)
    shared["e_wkr_pad"] = pad
    wqb = g["e_w_q_b"][0]
    shared["e_wqb"] = wqb
    sw = np.zeros((256, 8, 96), np.float32)
    for h in range(8):
        sw[:, h, 64:] = _swap_pairs(wqb[:, h * 96 + 64:(h + 1) * 96])
    shared["e_wqb_swp"] = sw
    shared["e_wkvb"] = g["e_w_kv_b"][0]
    shared["e_w_out"] = g["e_w_out"][0]
    owi = g["o_w_in"][0]
    shared["o_w_in"] = owi
    order = [0, 4, 1, 5, 2, 6, 3, 7]
    oq = np.zeros((D, 2, 512), np.float32)
    for s_, h in enumerate(order):
        blk = owi[:, h * 64:(h + 1) * 64]
        oq[:, 0, s_ * 64:(s_ + 1) * 64] = blk
        oq[:, 1, s_ * 64:(s_ + 1) * 64] = _swap_pairs(blk)
    shared["o_wq"] = oq
    ok = np.zeros((D, 2, 128), np.float32)
    for h in range(2):
        blk = owi[:, 512 + h * 64:512 + (h + 1) * 64]
        ok[:, 0, h * 64:(h + 1) * 64] = blk
        ok[:, 1, h * 64:(h + 1) * 64] = _swap_pairs(blk)
    shared["o_wk"] = ok
    shared["o_w_out"] = g["o_w_out"][0]
    shared = {k: np.ascontiguousarray(v, dtype=np.float32) for k, v in shared.items()}

    base = np.zeros((128, NCONST), np.float32)

    def put(name, arr):
        arr = np.asarray(arr, np.float32)
        base[:, _off[name]:_off[name] + arr.shape[1]] = arr
    put("BM", np.concatenate([_fm(g["b_mod"][l]) for l in range(2)], axis=1))
    put("NG", np.concatenate([_fm(g["norm_g"][l, j]) for l in range(2) for j in range(3)], axis=1))
    put("FG", _fm(g["final_g"]))
    put("ECW", np.stack([g["e_conv_w"][0][:, cc * 128:(cc + 1) * 128].T for cc in range(4)], axis=1).reshape(128, 12))
    put("EQN", _fm(g["e_q_norm"][0]))
    put("EKN", _fm(g["e_kv_norm"][0]))
    p64 = np.arange(128) % 64
    put("OQN", g["o_q_norm"][0][p64][:, None])
    put("OQNS", g["o_q_norm"][0][p64 ^ 1][:, None])
    put("OKN", g["o_k_norm"][0][p64][:, None])
    put("OKNS", g["o_k_norm"][0][p64 ^ 1][:, None])
    put("OCW", np.stack([g["o_conv_w"][0][:, cc * 128:(cc + 1) * 128].T for cc in range(4)], axis=1).reshape(128, 124))
    put("OCB", _fm(g["o_conv_b"][0]))
    put("OLG", _fm(g["o_ln_g"][0]))
    put("OLB", _fm(g["o_ln_b"][0]))

    maps = []
    for r in range(8):
        b, qd = r // 4, r % 4
        t0, c0 = qd * T, qd * 64
        tok = np.concatenate([np.arange(t0, t0 + T), np.arange(t0 - HALO, t0), np.arange(t0 + T, t0 + T + HALO)])
        valid = (tok >= 0) & (tok < SEQ)
        xs = np.zeros((TT, D), np.float32)
        xs[:LAT_END][valid] = x[b][tok[valid]]
        ctok = np.array([c0 - 1, c0 + 64] + list(range(c0, c0 + 64)))
        cvalid = (ctok >= 0) & (ctok < 256)
        xs[LAT_END:][cvalid] = ctx[b][ctok[cvalid]]
        xTh = np.ascontiguousarray(xs.reshape(TT, 8, 128).transpose(2, 1, 0))
        cs = base.copy()
        cv = np.stack([_fm(c[b]), _fm(c_ctx)], axis=2).reshape(128, 16)
        cs[:, _off["CV"]:_off["CV"] + 16] = cv
        cs[:, _off["MKL"]:_off["MKL"] + 32] = valid[T:].astype(np.float32)[None, :]
        cs[:, _off["MKC"]:_off["MKC"] + 2] = cvalid[:2].astype(np.float32)[None, :]
        tokc = np.clip(tok, 0, SEQ - 1)
        rows, cols = tokc // 64, tokc % 64
        tabs = []
        for rot, rowsel in ((32, lambda p: (p >= 64) & (p < 96)), (64, None)):
            cf, sf = _rope_tab(rot, valid, rows, cols)
            tab = np.zeros((128, 2, TT), np.float32)
            tab[:, 0, :] = 1.0
            if rot == 32:
                tab[64:96, 0, :LAT_END] = cf
                tab[64:96, 1, :LAT_END] = sf
            else:
                tab[:, 0, :LAT_END] = np.concatenate([cf, cf], axis=0)
                tab[:, 1, :LAT_END] = np.concatenate([sf, sf], axis=0)
            tabs.append(tab.astype(ml_dtypes.bfloat16))
        m = dict(shared)
        m["xT"] = xTh
        m["consts"] = cs
        m["ident"] = np.eye(128, dtype=np.float32).astype(ml_dtypes.bfloat16)
        m["tab0"] = tabs[0]
        m["tab1"] = tabs[1]
        maps.append(m)
    return maps


_CACHE = {}


def kernel(**inputs):
    maps = host_prep(inputs)
    if "nc" not in _CACHE:
        nc, P = build_nc()
        emit(nc, P)
        _CACHE["nc"] = nc
    nc = _CACHE["nc"]
    res = run_bass_kernel_spmd(nc, maps, core_ids=list(range(8)))
    out = np.zeros((2, SEQ, D), np.float32)
    for r in range(8):
        b, qd = r // 4, r % 4
        o = np.asarray(res.results[r]["out"])
        out[b, qd * T:(qd + 1) * T, :] = o.transpose(2, 1, 0).reshape(T, D)
    return out
```

```python
import numpy as np
import ml_dtypes
import concourse.bass as bass
import concourse.mybir as mybir
from concourse.bass_utils import run_bass_kernel_spmd

F32 = mybir.dt.float32
BF16 = mybir.dt.bfloat16
AF = mybir.ActivationFunctionType
ALU = mybir.AluOpType

D = 1024
SEQ = 8192
T = 2048
HALO = 16
TT = 2146
LAT_END = 2080
CTX0 = 2082
NKR = 2112
NKEYS = 4 * NKR
TILES = [(0, 512), (512, 1024), (1024, 1536), (1536, 2048), (2048, TT)]
EPS = 1e-6
DFF = 2816
NSLOT = 4
SLOT = 4096

_off = {}
_n = 0
for _name, _w in [("CV", 16), ("BM", 144), ("NG", 48), ("FG", 8), ("ECW", 12), ("EQN", 2), ("EKN", 1),
                  ("OQN", 1), ("OQNS", 1), ("OKN", 1), ("OKNS", 1), ("OCW", 124), ("OCB", 4), ("OLG", 4),
                  ("OLB", 4), ("MKL", 32), ("MKC", 2)]:
    _off[_name] = _n
    _n += _w
NCONST = _n


class Op:
    __slots__ = ("eng", "fn", "deps", "kind", "needed", "sem", "val", "inc", "nobar")

    def __init__(self, eng, fn, kind):
        self.eng = eng
        self.fn = fn
        self.kind = kind
        self.deps = ()
        self.needed = False
        self.sem = None
        self.val = 0
        self.inc = 1


class Prog:
    CE = ("pe", "act", "dve", "pool")

    def __init__(self):
        self.q = {"pe": [], "act": [], "dve": [], "pool": [], "sp": []}
        self.last_w = {}
        self.readers = {}
        self.bar = []
        self.rr = 0
        self.ring_i = 0
        self.bar_pos = {}
        self.bank_lo, self.bank_n = 2, 6

    def add(self, eng, fn, reads=(), writes=(), kind="c", nobar=False):
        op = Op(eng, fn, kind)
        op.nobar = nobar
        deps = set() if nobar else set(self.bar)
        lw = self.last_w
        rd = self.readers
        for r in reads:
            w = lw.get(r)
            if w is not None:
                deps.add(w)
            if type(r) is tuple and r[0] == "ps":
                rs = rd.get(r)
                if rs:
                    for k_, o_ in rs.items():
                        if k_ != eng:
                            deps.add(o_)
        for w_ in writes:
            w = lw.get(w_)
            if w is not None:
                deps.add(w)
            rs = rd.get(w_)
            if rs:
                deps.update(rs.values())
        op.deps = deps
        key = eng if kind == "c" else id(op)
        for r in reads:
            d = rd.get(r)
            if d is None:
                d = rd[r] = {}
            d[key] = op
        for w_ in writes:
            lw[w_] = op
            rd[w_] = {}
        self.q[eng].append(op)
        return op

    def barrier(self):
        bar = []
        for e in self.CE:
            for op in reversed(self.q[e]):
                if op.kind == "c":
                    bar.append(op)
                    break
        for e in ("pool", "sp"):
            for op in self.q[e][self.bar_pos.get(e, 0):]:
                if op.kind == "d" and not op.nobar:
                    bar.append(op)
            self.bar_pos[e] = len(self.q[e])
        self.bar = bar

    def bank(self):
        b = self.bank_lo + self.rr % self.bank_n
        self.rr += 1
        return b

    def ring(self):
        s = self.ring_i % NSLOT
        self.ring_i += 1
        return s


def build_nc(stop=None):
    nc = bass.Bass("TRN2", target_bir_lowering=False)
    P = Prog()

    def din(name, shape, dt=F32):
        return nc.dram_tensor(name, list(shape), dt, kind="ExternalInput").ap()

    xT = din("xT", [128, 8, TT])
    consts_d = din("consts", [128, NCONST])
    ident_d = din("ident", [128, 128], BF16)
    tab0_d = din("tab0", [128, 2, TT], BF16)
    tab1_d = din("tab1", [128, 2, TT], BF16)
    w_mod = din("w_mod", [2, D, 9 * D])
    w_ffn_in = din("w_ffn_in", [2, 2, D, 2 * DFF])
    w_ffn_out = din("w_ffn_out", [2, 2, DFF, D])
    e_w_in = din("e_w_in", [D, 1952])
    e_wkr_pad = din("e_wkr_pad", [D, 2, 96])
    e_wqb = din("e_wqb", [256, 768])
    e_wqb_swp = din("e_wqb_swp", [256, 8, 96])
    e_wkvb = din("e_wkvb", [128, 1024])
    e_w_out = din("e_w_out", [D, D])
    o_wq = din("o_wq", [D, 2, 512])
    o_wk = din("o_wk", [D, 2, 128])
    o_w_in = din("o_w_in", [D, 1792])
    o_w_out = din("o_w_out", [D, D])
    out_d = nc.dram_tensor("out", [128, 8, TT if stop else T], F32, kind="ExternalOutput").ap()
    kvb0 = nc.dram_tensor("kvb0", [160, NKR], BF16)
    kvg0 = nc.dram_tensor("kvg0", [640, NKR], BF16)
    kb1 = nc.dram_tensor("kb1", [128, NKR], BF16)
    kg1 = nc.dram_tensor("kg1", [512, NKR], BF16)
    vb1 = nc.dram_tensor("vb1", [128, 17 * 128], BF16)
    vg1 = nc.dram_tensor("vg1", [512, 17 * 128], BF16)

    H = nc.alloc_sbuf_tensor("H", [128, 8, TT], F32)
    RING = nc.alloc_sbuf_tensor("RING", [128, NSLOT, SLOT], BF16)
    CONST = nc.alloc_sbuf_tensor("CONST", [128, NCONST], F32)
    SCB = nc.alloc_sbuf_tensor("SCB", [128, 16], BF16)
    MODL = nc.alloc_sbuf_tensor("MODL", [128, 72], F32)
    MODC = nc.alloc_sbuf_tensor("MODC", [128, 72], F32)
    MODS = [{"L": MODL, "C": MODC}, {"L": nc.alloc_sbuf_tensor("MODL1", [128, 72], F32), "C": nc.alloc_sbuf_tensor("MODC1", [128, 72], F32)}]
    MOD = dict(MODS[0])
    IDENT = nc.alloc_sbuf_tensor("IDENT", [128, 128], BF16)
    GSS = [{"L": nc.alloc_sbuf_tensor("GSL", [128, 3, 8], F32), "C": nc.alloc_sbuf_tensor("GSC", [128, 3, 8], F32)},
           {"L": nc.alloc_sbuf_tensor("GSL1", [128, 3, 8], F32), "C": nc.alloc_sbuf_tensor("GSC1", [128, 3, 8], F32)}]
    GATES = [{"L": nc.alloc_sbuf_tensor("GTL", [128, 3, 8], F32), "C": nc.alloc_sbuf_tensor("GTC", [128, 3, 8], F32)},
             {"L": nc.alloc_sbuf_tensor("GTL1", [128, 3, 8], F32), "C": nc.alloc_sbuf_tensor("GTC1", [128, 3, 8], F32)}]
    GS = dict(GSS[0])
    GATE = dict(GATES[0])
    LS = [0]

    def mk(name, g):
        return f"{name}{g}{LS[0]}"

    def use_layer(l):
        LS[0] = l
        MOD.update(MODS[l])
        GS.update(GSS[l])
        GATE.update(GATES[l])
    ONES = nc.alloc_sbuf_tensor("ONES", [128, 128], BF16)
    BD = nc.alloc_sbuf_tensor("BD", [128, 128], BF16)
    ONES32 = nc.alloc_sbuf_tensor("ONES32", [128, 128], F32)
    FONE = nc.alloc_sbuf_tensor("FONE", [128, 8], F32)
    FZERO = nc.alloc_sbuf_tensor("FZERO", [128, 8], F32)
    EPSV = nc.alloc_sbuf_tensor("EPSV", [128, 1], F32)
    ABASE = nc.sbuf_base
    ATOP = nc.sbuf_top
    acur = [ABASE]

    def aalloc(name, shape, dt, at=None):
        esz = 4 if dt == F32 else 2
        nbytes = int(np.prod(shape[1:])) * esz
        off = acur[0] if at is None else at
        off = (off + 31) // 32 * 32
        assert off + nbytes <= ATOP, (name, off, nbytes, ATOP)
        t = nc.alloc_sbuf_tensor_at(name, list(shape), dt, offset=off)
        if at is None:
            acur[0] = off + nbytes
        return t, off, off + nbytes

    def areset(to=None):
        acur[0] = ABASE if to is None else to

    PSALL = nc.alloc_psum_tensor("psall", [128, 4096], F32)
    ps = [PSALL[:, i * 512:(i + 1) * 512] for i in range(8)]

    def C(name, w=None, o=0):
        if name == "EPSV":
            return EPSV[:, 0:1]
        a = _off[name] + o
        return CONST[:, a:a + (1 if w is None else w)]

    def mm(out, lhsT, rhs, start=True, stop=True):
        return lambda e: e.matmul(out, lhsT=lhsT, rhs=rhs, start=start, stop=stop)

    def act(out, in_, func, scale=1.0, bias=0.0):
        return lambda e: e.activation(out=out, in_=in_, func=func, bias=bias, scale=scale)

    def tt(out, in0, in1, op):
        return lambda e: e.tensor_tensor(out=out, in0=in0, in1=in1, op=op)

    def ts(out, in0, s1, op0, s2=None, op1=None):
        if op1 is None:
            return lambda e: e.tensor_scalar(out=out, in0=in0, scalar1=s1, scalar2=None, op0=op0)
        return lambda e: e.tensor_scalar(out=out, in0=in0, scalar1=s1, scalar2=s2, op0=op0, op1=op1)

    def stt(out, in0, scalar, in1, op0, op1):
        return lambda e: e.scalar_tensor_tensor(out=out, in0=in0, scalar=scalar, in1=in1, op0=op0, op1=op1)

    def cp(out, in_):
        return lambda e: (e.tensor_copy(out=out, in_=in_) if hasattr(e, "tensor_copy") else e.activation(out=out, in_=in_, func=AF.Copy))

    def dma(out, in_):
        return lambda e: e.dma_start(out=out, in_=in_)

    def segs(ti):
        s, e = TILES[ti]
        if ti < 4:
            return [(s, e, "L")]
        return [(2048, LAT_END, "L"), (LAT_END, TT, "C")]

    def Hres(ti):
        return [("H", c, ti) for c in range(8)]

    def rslot(slot, shape):
        v = RING[:, slot, 0:int(np.prod(shape))]
        if len(shape) == 2:
            return v.rearrange("p (a b) -> p a b", a=shape[0])
        if len(shape) == 3:
            return v.rearrange("p (a b c) -> p a b c", a=shape[0], b=shape[1])
        return v

    for c in range(8):
        P.add("sp", dma(H[:, c, :], xT[:, c, :]), writes=[("H", c, ti) for ti in range(5)], kind="d")
    P.add("sp", dma(CONST[:], consts_d), writes=["CONST"], kind="d")
    P.add("sp", dma(IDENT[:], ident_d), writes=["IDENT"], kind="d")
    P.add("dve", lambda e: e.memset(ONES[:], 1.0), writes=["ONES"])
    P.add("dve", lambda e: e.memset(ONES32[:], 0.0), writes=["ONES32"])
    P.add("dve", lambda e: e.memset(ONES32[64:65, :], 1.0), reads=["ONES32"], writes=["ONES32"])
    P.add("dve", lambda e: e.memset(FONE[:], 1.0), writes=["FONE"])
    P.add("dve", lambda e: e.memset(FZERO[:], 0.0), writes=["FZERO"])
    P.add("dve", lambda e: e.memset(BD[:], 0.0), writes=["BD"])
    P.add("dve", lambda e: e.memset(BD[0:64, 0:64], 1.0), reads=["BD"], writes=["BD"])
    P.add("dve", lambda e: e.memset(BD[64:128, 64:128], 1.0), reads=["BD"], writes=["BD"])
    P.add("act", act(SCB[:], C("CV", 16), AF.Silu), reads=["CONST"], writes=["SCB"])

    MODW = [aalloc(f"MODW{i}", [128, 8, 512], BF16, at=ABASE + 55616 + i * 8192)[0] for i in range(6)]
    modw_i = [0]

    def mod_piece(l, p, ksuf=""):
        i = modw_i[0] % 6
        modw_i[0] += 1
        wv = w_mod[l].rearrange("(c p) f -> p c f", p=128)

        def dma_part():
            xr = [("MODW", 3)] if (l == 0 and p == 4) else []
            P.add("pool", dma(MODW[i][:], wv[:, :, p * 512:(p + 1) * 512]), reads=xr, writes=[("MODW", i)], kind="d")

        def comp_part():
            bank = P.bank()
            for fc in range(4):
                for d in range(8):
                    P.add("pe", mm(ps[bank][:, fc * 2:fc * 2 + 2], MODW[i][:, d, fc * 128:(fc + 1) * 128], SCB[:, d * 2:d * 2 + 2], d == 0, d == 7),
                          reads=[("MODW", i), "SCB"], writes=[("ps", bank)])
            psv = ps[bank][:, 0:8].rearrange("p (f two) -> p f two", two=2)
            bm = C("BM", 4, 72 * l + p * 4)
            P.add("dve", tt(MODS[l]["L"][:, p * 4:(p + 1) * 4], psv[:, :, 0], bm, ALU.add), reads=[("ps", bank), "CONST"], writes=[f"MODL{l}" + ksuf])
            P.add("dve", tt(MODS[l]["C"][:, p * 4:(p + 1) * 4], psv[:, :, 1], bm, ALU.add), reads=[("ps", bank), "CONST"], writes=[f"MODC{l}" + ksuf])
        return dma_part, comp_part

    def mod_derive(l, j, parts=("gs", "gate"), ksuf=""):
        for g in ("L", "C"):
            ng = C("NG", 8, (l * 3 + j) * 8)
            if "gs" in parts:
                P.add("dve", stt(GSS[l][g][:, j, :], MODS[l][g][:, (3 * j + 1) * 8:(3 * j + 2) * 8], 1.0, ng, ALU.add, ALU.mult),
                      reads=[f"MOD{g}{l}", "CONST"], writes=[f"GS{g}{l}"])
            if "gate" in parts:
                P.add("dve", ts(GATES[l][g][:, j, :], MODS[l][g][:, (3 * j + 2) * 8:(3 * j + 3) * 8], 1.0 if j == 1 else 0.5, ALU.mult),
                      reads=[f"MOD{g}{l}" + ksuf], writes=[f"GATE{g}{l}"])

    def adaln(dest, dres, scale_of, shift_of, tiles=range(5), tmp=None, post=None):
        SQ, RS, LT, TMP = tmp
        kk = [0]
        banks = {}
        tl = list(tiles)

        def stage_a(ti):
            s, e = TILES[ti]
            n = e - s
            b2 = ti % 2
            P.add("act", act(SQ[b2][:, :, 0:n], H[:, :, s:e], AF.Square), reads=Hres(ti), writes=[("SQ", b2)])
            bank = P.bank()
            banks[ti] = bank
            for c in range(8):
                P.add("pe", mm(ps[bank][:, 0:n], ONES[:], SQ[b2][:, c, 0:n], c == 0, c == 7), reads=[("SQ", b2), "ONES"], writes=[("ps", bank)])

        def stage_b(ti):
            s, e = TILES[ti]
            n = e - s
            b2 = ti % 2
            bank = banks[ti]
            P.add("act", act(LT[b2][:, 0:n], ps[bank][:, 0:n], AF.Ln, scale=1.0 / D, bias=C("EPSV")), reads=[("ps", bank), "CONST"], writes=[("LT", b2)])
            P.add("act", act(RS[b2][:, 0:n], LT[b2][:, 0:n], AF.Exp, scale=-0.5), reads=[("LT", b2)], writes=[("RS", b2)])

        def stage_c(ti):
            s, e = TILES[ti]
            b2 = ti % 2
            for c in range(8):
                for (ss, ee, g) in segs(ti):
                    t4 = kk[0] % 4
                    kk[0] += 1
                    P.add("dve", stt(TMP[t4][:, 0:ee - ss], H[:, c, ss:ee], scale_of(g, c), RS[b2][:, ss - s:ee - s], ALU.mult, ALU.mult),
                          reads=[("H", c, ti), ("RS", b2), mk("GS", g)], writes=[("TMP", t4)])
                    if c % 2 == 0:
                        P.add("act", act(dest(c, ss, ee), TMP[t4][:, 0:ee - ss], AF.Identity, bias=shift_of(g, c)),
                              reads=[("TMP", t4), mk("MOD", g)], writes=(dres(c, ti) if callable(dres) else [(dres, c, ti)]))
                    else:
                        P.add("dve", ts(dest(c, ss, ee), TMP[t4][:, 0:ee - ss], shift_of(g, c), ALU.add),
                              reads=[("TMP", t4), mk("MOD", g)], writes=(dres(c, ti) if callable(dres) else [(dres, c, ti)]))

        n_ = len(tl)
        stage_a(tl[0])
        for i_ in range(n_):
            if i_ + 1 < n_:
                stage_a(tl[i_ + 1])
            stage_b(tl[i_])
            stage_c(tl[i_])
            if post is not None:
                post(tl[i_])

    def adaln_tmp(at=None):
        if at is None:
            SQ = [aalloc(f"SQ{i}", [128, 8, 512], BF16)[0] for i in range(2)]
            RS = [aalloc(f"RS{i}", [128, 512], F32)[0] for i in range(2)]
            LT = [aalloc(f"LT{i}", [128, 512], F32)[0] for i in range(2)]
            TMP = [aalloc(f"TMP{i}", [128, 512], F32)[0] for i in range(4)]
            return SQ, RS, LT, TMP
        cur = [at]

        def a3(name, shape, dt):
            t_, _, end = aalloc(name, shape, dt, at=cur[0])
            cur[0] = end
            return t_
        SQ = [a3(f"SQx{i}", [128, 8, 512], BF16) for i in range(2)]
        RS = [a3(f"RSx{i}", [128, 512], F32) for i in range(2)]
        LT = [a3(f"LTx{i}", [128, 512], F32) for i in range(2)]
        TMP = [a3(f"TMPx{i}", [128, 512], F32) for i in range(4)]
        return SQ, RS, LT, TMP

    def ffn(l, k, j, A, HID, SG, tiles=range(5), sched=None):
        win = w_ffn_in[l, k].rearrange("(c p) f -> p c f", p=128)
        wout = w_ffn_out[l, k]
        groups = [[0, 1], [2, 3], [4, 5], [6, 7], [8, 9], [10]]
        sgk = 0
        carry = []
        for gi, grp in enumerate(groups):
            for pi_l, pi in enumerate(grp):
                slot = P.ring()
                rv = rslot(slot, [8, 2, 256])
                P.add("pool", dma(rv[:, :, 0, :], win[:, :, pi * 256:(pi + 1) * 256]), writes=[("ring", slot, 0)], kind="d", nobar=True)
                P.add("pool", dma(rv[:, :, 1, :], win[:, :, DFF + pi * 256:DFF + (pi + 1) * 256]), writes=[("ring", slot, 1)], kind="d", nobar=True)
                for sub in range(2):
                    fl = pi_l * 2 + sub
                    for ti in tiles:
                        s, e = TILES[ti]
                        n = e - s
                        bg = P.bank()
                        for d in range(8):
                            P.add("pe", mm(ps[bg][:, 0:n], rv[:, d, 0, sub * 128:(sub + 1) * 128], A[:, d, s:e], d == 0, d == 7),
                                  reads=[("ring", slot, 0), ("A", d, ti)], writes=[("ps", bg)])
                        bu = P.bank()
                        for d in range(8):
                            P.add("pe", mm(ps[bu][:, 0:n], rv[:, d, 1, sub * 128:(sub + 1) * 128], A[:, d, s:e], d == 0, d == 7),
                                  reads=[("ring", slot, 1), ("A", d, ti)], writes=[("ps", bu)])
                        sg = sgk % 2
                        sgk += 1
                        P.add("act", act(SG[sg][:, 0:n], ps[bg][:, 0:n], AF.Silu), reads=[("ps", bg)], writes=[("SG", sg)])
                        P.add("dve", tt(HID[:, fl, s:e], SG[sg][:, 0:n], ps[bu][:, 0:n], ALU.mult), reads=[("SG", sg), ("ps", bu)], writes=[("HID", fl, ti)])
            oslots = []
            for pi in grp:
                slot = P.ring()
                rv = rslot(slot, [2, 1024])
                P.add("pool", dma(rv, wout[pi * 256:(pi + 1) * 256, :].rearrange("(s p) d -> p s d", p=128)),
                      writes=[("ring", slot, 0), ("ring", slot, 1)], kind="d", nobar=True)
                oslots.append((slot, rv))
            pend = []
            for task in (sched[gi] if sched else []):
                if task[0] == "piece":
                    d_, c_ = mod_piece(task[1], task[2])
                    d_()
                    pend.append(c_)
                else:
                    pend.append((lambda t_=task: mod_derive(t_[1], t_[2])))
            nf = 2 * len(grp)
            for ti in tiles:
                s, e = TILES[ti]
                n = e - s
                for dc in range(8):
                    bo = P.bank()
                    for fl in range(nf):
                        slot, rv = oslots[fl // 2]
                        P.add("pe", mm(ps[bo][:, 0:n], rv[:, fl % 2, dc * 128:(dc + 1) * 128], HID[:, fl, s:e], fl == 0, fl == nf - 1),
                              reads=[("ring", slot, 0), ("ring", slot, 1), ("HID", fl, ti)], writes=[("ps", bo)])
                    for (ss, ee, g) in segs(ti):
                        P.add("dve", stt(H[:, dc, ss:ee], ps[bo][:, ss - s:ee - s], GATE[g][:, j, dc:dc + 1], H[:, dc, ss:ee], ALU.mult, ALU.add),
                              reads=[("ps", bo), mk("GATE", g), ("H", dc, ti)], writes=[("H", dc, ti)])
            for f_ in carry:
                f_()
            carry = pend
        for f_ in carry:
            f_()

    def ffn_block(l, k, j, tiles=range(5), sched=None):
        P.barrier()
        areset()
        A, _, a_end = aalloc("A", [128, 8, TT], BF16)
        tmp = adaln_tmp()
        adaln(lambda c, s, e: A[:, c, s:e], "A", lambda g, c: GS[g][:, j, c:c + 1], lambda g, c: MOD[g][:, 3 * j * 8 + c:3 * j * 8 + c + 1], tiles, tmp)
        P.barrier()
        if stop == "adaln0":
            for c in range(8):
                P.add("act", cp(H[:, c, :], A[:, c, :]), writes=[("H", c, ti) for ti in range(5)])
            return
        areset(a_end)
        HID = aalloc("HID", [128, 4, TT], BF16)[0]
        SG = [aalloc(f"SG{i}", [128, 512], F32)[0] for i in range(2)]
        ffn(l, k, j, A, HID, SG, tiles, sched)

    def dump_H():
        P.barrier()
        for c in range(8):
            P.add("sp", dma(out_d[:, c, :], H[:, c, :]), reads=[("H", c, ti) for ti in range(5)], writes=[("out", c)], kind="d")
        P.add("sp", None, reads=[("out", c) for c in range(8)], kind="w")

    P.add("dve", lambda e: e.memset(EPSV[:], EPS), writes=["CONST2"])

    comps = []
    for p_ in range(6):
        d_, c_ = mod_piece(0, p_, ksuf="" if p_ < 4 else "g")
        d_()
        comps.append(c_)
    for p_ in range(4):
        comps[p_]()
    mod_derive(0, 0, parts=("gs",))
    for p_ in range(4, 6):
        comps[p_]()
    mod_derive(0, 0, parts=("gate",), ksuf="g")
    sched0 = [[("piece", 0, 6 + 2 * gi), ("piece", 0, 7 + 2 * gi)] for gi in range(6)]
    sched0[2].append(("derive", 0, 1))
    sched0[5].append(("derive", 0, 2))
    ffn_block(0, 0, 0, sched=sched0)
    if stop in ("ffn1_0", "adaln0"):
        dump_H()
        return nc, P

    RG = [[0, 1, 2, 3], [4, 5, 6, 7]]
    NT = 66
    KEYT = [(j * 128, 128, j >= 64) for j in range(NT)]

    def ring_load(dmas):
        slot = P.ring()
        for i, (dst, src) in enumerate(dmas):
            keys = [("ring", slot, i)]
            if i == 0:
                keys += [("ring", slot, k) for k in range(len(dmas), 3)]
            P.add("pool", dma(dst(slot), src), writes=keys, kind="d", nobar=True)
        return slot

    def rkeys(slot):
        return [("ring", slot, k) for k in range(3)]

    def h_update(bank, ti, dc, j, s):
        for (ss, ee, g) in segs(ti):
            P.add("dve", stt(H[:, dc, ss:ee], ps[bank][:, ss - s:ee - s], GATE[g][:, j, dc:dc + 1], H[:, dc, ss:ee], ALU.mult, ALU.add),
                  reads=[("ps", bank), mk("GATE", g), ("H", dc, ti)], writes=[("H", dc, ti)])

    def mixer0():
        P.barrier()
        areset()
        TAB0 = aalloc("TAB0", [128, 2, TT], BF16)[0]
        QAN, _, pers_end = aalloc("QAN", [128, 2, TT], BF16)
        P.add("sp", dma(TAB0[:], tab0_d), writes=["TAB0"], kind="d")
        A, _, a_end = aalloc("A0", [128, 8, TT], BF16)
        tmp = adaln_tmp()
        adaln(lambda c, s, e: A[:, c, s:e], "A", lambda g, c: GS[g][:, 1, c:c + 1], lambda g, c: MOD[g][:, 24 + c:25 + c], range(5), tmp)
        P.barrier()
        areset(a_end)
        CKVN = aalloc("CKVN", [128, TT], BF16)[0]
        KRR = aalloc("KRR", [128, TT], BF16)[0]
        SQ2 = [aalloc(f"SQ2{i}", [128, 3, 512], BF16)[0] for i in range(1)]
        LT2 = [aalloc(f"LT2{i}", [128, 512], F32)[0] for i in range(2)]
        RS2 = [aalloc(f"RS2{i}", [128, 512], F32)[0] for i in range(2)]
        TS = [aalloc(f"TS{i}", [128, 512], F32)[0] for i in range(4)]
        CIN = aalloc("CIN", [128, 2080], BF16)[0]
        CINC = aalloc("CINC", [128, 66], BF16)[0]
        SBB = aalloc("SBB", [128, TT], BF16)[0]
        DG3 = aalloc("DG3", [128, 3, 128], BF16)[0]
        YA = aalloc("YA", [128, 4, TT], BF16)[0]
        P.add("dve", lambda e: e.memset(YA[:, :, 2048:CTX0], 0.0), writes=[("YA", cc_, 4) for cc_ in range(4)])
        ewv = e_w_in.rearrange("(c p) f -> p c f", p=128)
        s_qk = ring_load([(lambda sl: rslot(sl, [8, 384]), ewv[:, :, 1536:1920])])
        rv_qk = rslot(s_qk, [8, 384])
        s_kr = ring_load([(lambda sl: rslot(sl, [8, 2, 96]), e_wkr_pad.rearrange("(c p) s f -> p c s f", p=128))])
        rv_kr = rslot(s_kr, [8, 2, 96])
        tsk = 0
        for ti, (s, e) in enumerate(TILES):
            n = e - s
            b2 = 0
            bq = [P.bank(), P.bank()]
            for cq in range(2):
                for d in range(8):
                    P.add("pe", mm(ps[bq[cq]][:, 0:n], rv_qk[:, d, cq * 128:(cq + 1) * 128], A[:, d, s:e], d == 0, d == 7),
                          reads=rkeys(s_qk) + [("A", d, ti)], writes=[("ps", bq[cq])])
                P.add("act", act(SQ2[b2][:, cq, 0:n], ps[bq[cq]][:, 0:n], AF.Square), reads=[("ps", bq[cq])], writes=[("SQ2", b2, cq)])
            bs = P.bank()
            for cq in range(2):
                P.add("pe", mm(ps[bs][:, 0:n], ONES[:], SQ2[b2][:, cq, 0:n], cq == 0, cq == 1), reads=[("SQ2", b2, cq)], writes=[("ps", bs)])
            P.add("act", act(LT2[0][:, 0:n], ps[bs][:, 0:n], AF.Ln, scale=1.0 / 256, bias=C("EPSV")), reads=[("ps", bs)], writes=[("LT2", 0)])
            P.add("act", act(RS2[0][:, 0:n], LT2[0][:, 0:n], AF.Exp, scale=-0.5), reads=[("LT2", 0)], writes=[("RS2", 0)])
            for cq in range(2):
                P.add("dve", stt(QAN[:, cq, s:e], ps[bq[cq]][:, 0:n], C("EQN", 1, cq), RS2[0][:, 0:n], ALU.mult, ALU.mult),
                      reads=[("ps", bq[cq]), ("RS2", 0), "CONST"], writes=[("QAN", ti)])
            bk = P.bank()
            for d in range(8):
                P.add("pe", mm(ps[bk][:, 0:n], rv_qk[:, d, 256:384], A[:, d, s:e], d == 0, d == 7), reads=rkeys(s_qk) + [("A", d, ti)], writes=[("ps", bk)])
            P.add("act", act(SQ2[b2][:, 2, 0:n], ps[bk][:, 0:n], AF.Square), reads=[("ps", bk)], writes=[("SQ2", b2, 2)])
            bs2 = P.bank()
            P.add("pe", mm(ps[bs2][:, 0:n], ONES[:], SQ2[b2][:, 2, 0:n]), reads=[("SQ2", b2, 2)], writes=[("ps", bs2)])
            P.add("act", act(LT2[1][:, 0:n], ps[bs2][:, 0:n], AF.Ln, scale=1.0 / 128, bias=C("EPSV")), reads=[("ps", bs2)], writes=[("LT2", 1)])
            P.add("act", act(RS2[1][:, 0:n], LT2[1][:, 0:n], AF.Exp, scale=-0.5), reads=[("LT2", 1)], writes=[("RS2", 1)])
            P.add("dve", stt(CKVN[:, s:e], ps[bk][:, 0:n], C("EKN"), RS2[1][:, 0:n], ALU.mult, ALU.mult),
                  reads=[("ps", bk), ("RS2", 1), "CONST"], writes=[("CKVN", ti)])
            br = [P.bank(), P.bank()]
            for w_ in range(2):
                for d in range(8):
                    P.add("pe", mm(ps[br[w_]][0:96, 0:n], rv_kr[:, d, w_, :], A[:, d, s:e], d == 0, d == 7), reads=rkeys(s_kr) + [("A", d, ti)], writes=[("ps", br[w_])])
            ta, tb = TS[tsk % 4], TS[(tsk + 1) % 4]
            ka, kb_ = ("TS", tsk % 4), ("TS", (tsk + 1) % 4)
            tsk += 2
            P.add("dve", tt(ta[64:96, 0:n], ps[br[0]][64:96, 0:n], TAB0[64:96, 0, s:e], ALU.mult), reads=[("ps", br[0]), "TAB0"], writes=[ka])
            P.add("dve", tt(tb[64:96, 0:n], ps[br[1]][64:96, 0:n], TAB0[64:96, 1, s:e], ALU.mult), reads=[("ps", br[1]), "TAB0"], writes=[kb_])
            P.add("dve", tt(KRR[64:96, s:e], ta[64:96, 0:n], tb[64:96, 0:n], ALU.add), reads=[ka, kb_], writes=[("KRR", ti)])
        P.add("sp", dma(kvb0[0:128, 0:T], CKVN[:, 0:T]), reads=[("CKVN", t_) for t_ in range(4)], writes=[("kvb0", 0)], kind="d")
        P.add("sp", dma(kvb0[0:128, T:NKR], CKVN[:, CTX0:TT]), reads=[("CKVN", 4)], writes=[("kvb0", 1)], kind="d")
        P.add("sp", dma(kvb0[128:160, 0:T], KRR[64:96, 0:T]), reads=[("KRR", t_) for t_ in range(4)], writes=[("kvb0", 2)], kind="d")
        P.add("sp", dma(kvb0[128:160, T:NKR], KRR[64:96, CTX0:TT]), reads=[("KRR", 4)], writes=[("kvb0", 3)], kind="d")
        def conv_load(cc):
            return ring_load([(lambda sl, k=k: rslot(sl, [8, 3, 128])[:, :, k, :], ewv[:, :, k * 512 + cc * 128:k * 512 + (cc + 1) * 128]) for k in range(3)])
        conv_slots = {cc_: conv_load(cc_) for cc_ in (0, 1)}
        P.add("pool", lambda e: e.collective_compute("AllGather", ALU.bypass, replica_groups=RG, ins=[kvb0.ap().opt()], outs=[kvg0.ap().opt()]),
              reads=[("kvb0", i) for i in range(4)], writes=["kvg0"], kind="cc")

        for cc in range(4):
            s_c = conv_slots.pop(cc) if cc in conv_slots else conv_load(cc)
            rv = rslot(s_c, [8, 3, 128])
            for ti, (s, e) in enumerate(TILES):
                n = e - s
                bb = [P.bank(), P.bank(), P.bank()]
                for k in range(3):
                    for d in range(8):
                        P.add("pe", mm(ps[bb[k]][:, 0:n], rv[:, d, k, :], A[:, d, s:e], d == 0, d == 7), reads=[("ring", s_c, k), ("A", d, ti)], writes=[("ps", bb[k])])
                ta = TS[tsk % 4]
                ka = ("TS", tsk % 4)
                tsk += 1
                P.add("act", act(ta[:, 0:n], ps[bb[1]][:, 0:n], AF.Copy), reads=[("ps", bb[1])], writes=[ka])
                if ti < 4:
                    pieces = [(CIN[:, 16 + s:16 + e], 0, n, None)]
                else:
                    pieces = [(CIN[:, 0:16], 0, 16, C("MKL", 16, 0)), (CIN[:, 2064:2080], 16, 32, C("MKL", 16, 16)),
                              (CINC[:, 0:1], 32, 33, C("MKC", 1, 0)), (CINC[:, 65:66], 33, 34, C("MKC", 1, 1)), (CINC[:, 1:65], 34, 98, None)]
                for (dst, a0, a1, mk) in pieces:
                    P.add("dve", tt(dst, ta[:, a0:a1], ps[bb[2]][:, a0:a1], ALU.mult), reads=[ka, ("ps", bb[2])], writes=[("CIN", ti)])
                    if mk is not None:
                        P.add("dve", tt(dst, dst, mk, ALU.mult), reads=[("CIN", ti), "CONST"], writes=[("CIN", ti)])
                P.add("act", act(SBB[:, s:e], ps[bb[0]][:, 0:n], AF.Copy), reads=[("ps", bb[0])], writes=[("SBB", ti)])
            w = [C("ECW", 1, cc * 3 + k) for k in range(3)]
            cin_all = [("CIN", t_) for t_ in range(5)]
            for k in range(3):
                P.add("dve", ts(DG3[:, k, :], IDENT[:], w[k], ALU.mult), reads=["IDENT", "CONST"], writes=["DG3"])
            for ti in range(4):
                s, e = TILES[ti]
                bz = P.bank()
                for k in range(3):
                    P.add("pe", mm(ps[bz][:, 0:512], DG3[:, k, :], CIN[:, 15 + s + k:15 + s + k + 512], k == 0, k == 2), reads=["DG3"] + cin_all, writes=[("ps", bz)])
                P.add("dve", tt(YA[:, cc, s:e], SBB[:, s:e], ps[bz][:, 0:512], ALU.mult), reads=[("SBB", ti), ("ps", bz)], writes=[("YA", cc, ti)])
            bz = P.bank()
            for (c0, n_, srcf) in [(0, 15, lambda k: CIN[:, k:k + 15]), (15, 15, lambda k: CIN[:, 2063 + k:2063 + k + 15]), (30, 64, lambda k: CINC[:, k:k + 64])]:
                for k in range(3):
                    P.add("pe", mm(ps[bz][:, c0:c0 + n_], DG3[:, k, :], srcf(k), k == 0, k == 2), reads=["DG3"] + cin_all, writes=[("ps", bz)])
            P.add("dve", tt(YA[:, cc, 2049:2064], SBB[:, 2049:2064], ps[bz][:, 0:15], ALU.mult), reads=[("SBB", 4), ("ps", bz)], writes=[("YA", cc, 4)])
            P.add("dve", tt(YA[:, cc, 2064:2079], SBB[:, 2064:2079], ps[bz][:, 15:30], ALU.mult), reads=[("SBB", 4), ("ps", bz), ("YA", cc, 4)], writes=[("YA", cc, 4)])
            P.add("dve", tt(YA[:, cc, CTX0:TT], SBB[:, CTX0:TT], ps[bz][:, 30:94], ALU.mult), reads=[("SBB", 4), ("ps", bz), ("YA", cc, 4)], writes=[("YA", cc, 4)])
        s_o = ring_load([(lambda sl: rslot(sl, [4, 1024]), e_w_out[0:512, :].rearrange("(c p) d -> p c d", p=128))])
        rvo = rslot(s_o, [4, 1024])
        for ti, (s, e) in enumerate(TILES):
            n = e - s
            for dc in range(8):
                bo = P.bank()
                for cc in range(4):
                    P.add("pe", mm(ps[bo][:, 0:n], rvo[:, cc, dc * 128:(dc + 1) * 128], YA[:, cc, s:e], cc == 0, cc == 3), reads=rkeys(s_o) + [("YA", cc, ti)], writes=[("ps", bo)])
                h_update(bo, ti, dc, 1, s)
        if stop == "conv0":
            return
        P.barrier()
        areset(pers_end)
        CKVf = aalloc("CKV", [128, NKEYS], BF16)[0]
        KT = aalloc("KT", [128, NKEYS], BF16)[0]
        V = aalloc("V", [128, NT, 65], BF16)[0]
        Q = [aalloc(f"Q{i}", [128, TT], BF16)[0] for i in range(2)]
        E = [aalloc(f"E{i}", [128, 2, 512], BF16)[0] for i in range(4)]
        O = [aalloc(f"O{i}", [128, TT], BF16)[0] for i in range(2)]
        REC = [aalloc(f"REC{i}", [128, 512], F32)[0] for i in range(1)]
        OU = [aalloc(f"OU{i}", [128, 512], F32)[0] for i in range(2)]
        TS2 = [aalloc(f"TSB{i}", [128, 512], F32)[0] for i in range(1)]
        WQB = aalloc("WQB", [128, 2, 768], BF16)[0]
        WQBS = aalloc("WQBS", [128, 2, 8, 96], BF16)[0]
        WKVB = aalloc("WKVB", [128, 1024], BF16)[0]
        WO = [aalloc(f"WO{i}", [128, 1024], BF16)[0] for i in range(2)]
        kview = kvg0.ap().rearrange("(r q) t -> q r t", q=160)
        P.add("sp", dma(CKVf[:, 0:4 * T].rearrange("p (r t) -> p r t", r=4), kview[0:128, :, 0:T]), reads=["kvg0"], writes=[("CKV", 0)], kind="d")
        P.add("sp", dma(CKVf[:, 4 * T:NKEYS].rearrange("p (r t) -> p r t", r=4), kview[0:128, :, T:NKR]), reads=["kvg0"], writes=[("CKV", 1)], kind="d")
        P.add("sp", dma(KT[64:96, 0:4 * T].rearrange("p (r t) -> p r t", r=4), kview[128:160, :, 0:T]), reads=["kvg0"], writes=[("KTR", 0)], kind="d")
        P.add("sp", dma(KT[64:96, 4 * T:NKEYS].rearrange("p (r t) -> p r t", r=4), kview[128:160, :, T:NKR]), reads=["kvg0"], writes=[("KTR", 1)], kind="d")
        for i in range(2):
            P.add("dve", (lambda t: (lambda e: e.memset(t[:], 0.0)))(WO[i]), writes=[("WO", i)])
        P.add("pool", dma(WQB[:], e_wqb.rearrange("(c p) f -> p c f", p=128)), writes=["WQB"], kind="d")
        P.add("pool", dma(WQBS[:], e_wqb_swp.rearrange("(c p) h f -> p c h f", p=128)), writes=["WQBS"], kind="d")
        P.add("pool", dma(WKVB[:], e_wkvb), writes=["WKVB"], kind="d")
        P.add("dve", lambda e: e.memset(V[:, :, 64:65], 1.0), writes=["VONE"])
        for i in range(2):
            P.add("dve", (lambda t: (lambda e: e.memset(t[:], 0.0)))(O[i]), writes=[("O", i, t_) for t_ in range(5)])
        P.add("dve", lambda e: e.memset(REC[0][:], 0.0), writes=[("REC", 0)])
        attention(8, 96, 96 ** -0.5,
                  kgen=lambda h: kgen0(h, KT, CKVf, WKVB), vgen=lambda h: vgen0(h, V, CKVf, WKVB),
                  qgen=lambda h, qb: qgen0(h, qb, Q[qb], QAN, WQB, WQBS, TAB0, TS2),
                  kt_of=lambda h: (KT, 0, 96), v_of=lambda h, j: V[:, j, :], q_of=lambda h, qb, s, e: Q[qb][0:96, s:e],
                  qkeys=lambda h, qb, ti: [("Q", qb, ti)], kkeys=lambda ks, nk: [("KT", ks // 512), ("KT", (ks + nk - 1) // 512)],
                  vkeys=lambda j: [("V", j // 8)],
                  E=E, O=O, REC=REC, OU=OU, WO=WO, wo_src=lambda h: e_w_out[512 + h * 64:512 + (h + 1) * 64, :],
                  qtiles=[(0, 512, 0, False), (512, 1024, 1, False), (1024, 1536, 2, False), (1536, 2048, 3, False), (2048, LAT_END, 4, False), (CTX0, TT, 4, True)],
                  out_tiles=range(5), extra_reads=[("KTR", 0), ("KTR", 1), "VONE"])

    def kgen0(h, KT, CKVf, WKVB):
        for g in range(17):
            ks, ke = g * 512, min(NKEYS, (g + 1) * 512)
            n = ke - ks
            b = P.bank()
            P.add("pe", mm(ps[b][:, 0:n], WKVB[:, h * 128:(h + 1) * 128], CKVf[:, ks:ke]), reads=["WKVB", ("CKV", 0), ("CKV", 1)], writes=[("ps", b)])
            if g % 2 == 0:
                P.add("act", act(KT[0:64, ks:ke], ps[b][0:64, 0:n], AF.Copy), reads=[("ps", b)], writes=[("KT", g)])
            else:
                P.add("dve", cp(KT[0:64, ks:ke], ps[b][0:64, 0:n]), reads=[("ps", b)], writes=[("KT", g)])

    def vgen0(h, V, CKVf, WKVB):
        for j0 in range(0, NT, 8):
            cnt = min(8, NT - j0)
            b = P.bank()
            for j in range(j0, j0 + cnt):
                ks, nk, _ = KEYT[j]
                P.add("pe", mm(ps[b][0:nk, (j - j0) * 64:(j - j0 + 1) * 64], CKVf[:, ks:ks + nk], WKVB[:, h * 128 + 64:(h + 1) * 128]),
                      reads=["WKVB", ("CKV", 0), ("CKV", 1)], writes=[("ps", b)])
            src = ps[b][:, 0:cnt * 64].rearrange("p (j f) -> p j f", f=64)
            if (j0 // 8) % 2 == 0:
                P.add("dve", cp(V[:, j0:j0 + cnt, 0:64], src), reads=[("ps", b)], writes=[("V", j0 // 8)])
            else:
                P.add("act", act(V[:, j0:j0 + cnt, 0:64], src, AF.Copy), reads=[("ps", b)], writes=[("V", j0 // 8)])

    def qgen0(h, qb, Qb, QAN, WQB, WQBS, TAB0, TS2):
        for ti, (s, e) in enumerate(TILES):
            n = e - s
            b0, b1 = P.bank(), P.bank()
            for c in range(2):
                P.add("pe", mm(ps[b0][0:96, 0:n], WQB[:, c, h * 96:(h + 1) * 96], QAN[:, c, s:e], c == 0, c == 1), reads=["WQB", ("QAN", ti)], writes=[("ps", b0)])
            for c in range(2):
                P.add("pe", mm(ps[b1][0:96, 0:n], WQBS[:, c, h, :], QAN[:, c, s:e], c == 0, c == 1), reads=["WQBS", ("QAN", ti)], writes=[("ps", b1)])
            qk = ("Q", qb, ti)
            P.add("act", act(Qb[0:64, s:e], ps[b0][0:64, 0:n], AF.Copy), reads=[("ps", b0)], writes=[qk])
            P.add("dve", tt(TS2[0][64:96, 0:n], ps[b0][64:96, 0:n], TAB0[64:96, 0, s:e], ALU.mult), reads=[("ps", b0), "TAB0"], writes=[("TSB", 0)])
            P.add("dve", tt(Qb[64:96, s:e], ps[b1][64:96, 0:n], TAB0[64:96, 1, s:e], ALU.mult), reads=[("ps", b1), "TAB0", qk], writes=[qk])
            P.add("dve", tt(Qb[64:96, s:e], TS2[0][64:96, 0:n], Qb[64:96, s:e], ALU.add), reads=[("TSB", 0), qk], writes=[qk])


    def qgen1(h, qb, QP, Q1):
        g = h // 4
        if h == 4 or h == 5:
            P.add("dve", (lambda t: (lambda e: e.memset(t[0:64, :], 0.0)))(QP[qb]), reads=[("QP", qb)], writes=[("QP", qb)])
        src = Q1[g * 64:(g + 1) * 64, h % 4, :]
        dst = QP[qb][g * 64:(g + 1) * 64, :]
        if h % 2 == 0:
            P.add("dve", cp(dst, src), reads=[("Q1", h % 4, t_) for t_ in range(4)] + [("QP", qb)], writes=[("QP", qb)])
        else:
            P.add("act", act(dst, src, AF.Copy), reads=[("Q1", h % 4, t_) for t_ in range(4)] + [("QP", qb)], writes=[("QP", qb)])

    def attention(nheads, kdim, scale, kgen, vgen, qgen, kt_of, v_of, q_of, qkeys, kkeys, vkeys, E, O, REC, OU, WO, wo_src, qtiles, out_tiles, extra_reads):
        ek = 0
        npair = 0
        NE = len(E)
        LA = max(1, NE - 2)
        NSP = LA + 1
        ob = 0
        pend_fin = []
        pend_out = {}
        nfin = [0]

        pend_fin2 = []

        def finalize(h, qb, s, e, ti):
            nq = e - s

            def run():
                ou = OU[nfin[0] % len(OU)]
                ouk = ("OU", nfin[0] % len(OU))
                rc = REC[nfin[0] % len(REC)]
                rck = ("REC", nfin[0] % len(REC))
                nfin[0] += 1
                P.add("act", act(ou[0:65, 0:nq], ps[ob][0:65, 0:nq], AF.Copy), reads=[("ps", ob)], writes=[ouk])
                P.add("dve", lambda e_: e_.reciprocal(out=rc[64:65, 0:nq], in_=ou[64:65, 0:nq]), reads=[ouk], writes=[rck])

                def run2():
                    bb = P.bank()
                    P.add("pe", mm(ps[bb][:, 0:nq], ONES32[:], rc[:, 0:nq]), reads=[rck, "ONES32"], writes=[("ps", bb)])
                    P.add("dve", tt(O[qb][0:64, s:e], ou[0:64, 0:nq], ps[bb][0:64, 0:nq], ALU.mult), reads=[ouk, ("ps", bb)], writes=[("O", qb, ti)])
                pend_fin2.append(run2)
            return run

        def outproj(qb, ti):
            s, e = TILES[ti]
            n = e - s

            def one(dc):
                def run():
                    bo = P.bank()
                    P.add("pe", mm(ps[bo][:, 0:n], WO[qb][:, dc * 128:(dc + 1) * 128], O[qb][:, s:e]), reads=[("WO", qb), ("O", qb, ti)], writes=[("ps", bo)])
                    h_update(bo, ti, dc, 1, s)
                return run
            return [one(dc) for dc in range(8)]

        for h in range(nheads):
            qb = h % 2
            P.bank_lo, P.bank_n = 2, 6
            if kgen is not None:
                kgen(h)
                vgen(h)
            qgen(h, qb)
            P.bank_lo, P.bank_n = 1, 1
            P.add("pool", dma(WO[qb][0:64, :], wo_src(h)), writes=[("WO", qb)], kind="d")
            KTt, kp0, kp1 = kt_of(h)
            for qi, (s, e, ti, ctx_only) in enumerate(qtiles):
                nq = e - s
                keys = [(j, kt) for j, kt in enumerate(KEYT) if (kt[2] or not ctx_only)]
                pairs = [keys[i_:i_ + 2] for i_ in range(0, len(keys), 2)]
                slots = {}
                nk_tot = len(keys)
                done = 0
                for pi in range(len(pairs) + LA):
                    if pi < len(pairs):
                        pr = pairs[pi]
                        pb = 2 + 2 * (npair % NSP)
                        npair += 1
                        eb = ek % NE
                        ek += 1
                        rows = max(kt[1] for (_, kt) in pr)
                        for t_, (j, (ks, nk, _)) in enumerate(pr):
                            P.add("pe", mm(ps[pb + t_][0:nk, 0:nq], KTt[kp0:kp1, ks:ks + nk], q_of(h, qb, s, e)),
                                  reads=kkeys(ks, nk) + qkeys(h, qb, ti) + extra_reads, writes=[("ps", pb + t_)])
                        np_ = len(pr)
                        src = PSALL[0:rows, pb * 512:(pb + np_) * 512].rearrange("p (t c) -> p t c", t=np_)[:, :, 0:nq]
                        P.add("act", act(E[eb][0:rows, 0:np_, 0:nq], src, AF.Exp, scale=scale),
                              reads=[("ps", pb + t_) for t_ in range(np_)], writes=[("E", eb)])
                        slots[pi] = eb
                    if pi >= LA:
                        pr = pairs[pi - LA]
                        eb = slots[pi - LA]
                        for t_, (j, (ks, nk, _)) in enumerate(pr):
                            P.add("pe", mm(ps[ob][0:65, 0:nq], v_of(h, j)[0:nk, :], E[eb][0:nk, t_, 0:nq], done == 0, done == nk_tot - 1),
                                  reads=[("E", eb)] + vkeys(j) + extra_reads, writes=[("ps", ob)])
                            done += 1
                    last = pi == len(pairs) + LA - 1
                    if pi == 0 or last:
                        while pend_fin:
                            pend_fin.pop(0)()
                    if pi == 5 or last:
                        while pend_fin2:
                            pend_fin2.pop(0)()
                    fl = pend_out.get(qi)
                    if fl and (pi >= 6 and pi % 3 == 0 or last):
                        fl.pop(0)()
                        if last:
                            while fl:
                                fl.pop(0)()
                pend_fin.append(finalize(h, qb, s, e, ti))
            for k_ in sorted(pend_out):
                for f in pend_out.pop(k_):
                    f()
            for qi, ti in enumerate(out_tiles):
                pend_out[qi] = outproj(qb, ti)
        P.bank_lo, P.bank_n = 2, 6
        while pend_fin:
            pend_fin.pop(0)()
        while pend_fin2:
            pend_fin2.pop(0)()
        for k_ in sorted(pend_out):
            for f in pend_out.pop(k_):
                f()

    mixer0()
    if stop in ("conv0", "mix0"):
        dump_H()
        return nc, P
    sched1 = [[("piece", 1, 3 * gi + t_) for t_ in range(3)] for gi in range(6)]
    sched1[1].append(("derive", 1, 0))
    sched1[3].append(("derive", 1, 1))
    sched1[5].append(("derive", 1, 2))
    ffn_block(0, 1, 2, sched=sched1)
    if stop == "l0":
        dump_H()
        return nc, P

    def mixer1():
        areset()
        A, _, a_end = aalloc("A1", [128, 8, TT], BF16)
        tmp = adaln_tmp(at=ABASE + 55616)
        adaln(lambda c, s, e: A[:, c, s:e], "A", lambda g, c: GS[g][:, 1, c:c + 1], lambda g, c: MOD[g][:, 24 + c:25 + c], range(5), tmp)
        P.barrier()
        areset(a_end)
        owv = o_w_in.rearrange("(c p) f -> p c f", p=128)
        Q1, _, q1_end = aalloc("Q1", [128, 4, T], BF16)
        TAB1 = aalloc("TAB1", [128, 2, TT], BF16)[0]
        K1O = aalloc("K1O", [128, NKR], BF16)[0]
        VOWN = aalloc("VOWN", [128, 17, 128], BF16)[0]
        TS = [aalloc(f"TS2{i}", [128, 512], F32)[0] for i in range(4)]
        SQ = [aalloc(f"SQB{i}", [128, 512], BF16)[0] for i in range(2)]
        LTb = [aalloc(f"LTB{i}", [128, 512], F32)[0] for i in range(2)]
        RSb = [aalloc(f"RSB{i}", [128, 512], F32)[0] for i in range(2)]
        P.add("sp", dma(TAB1[:], tab1_d), writes=["TAB1"], kind="d")
        wqv = o_wq.rearrange("(c p) w f -> p c w f", p=128)
        wkv = o_wk.rearrange("(c p) w f -> p c w f", p=128)
        it = 0
        work = [("q", jq) for jq in range(4)] + [("k", 0)]
        for (kind, jq) in work:
            src = wqv[:, :, :, jq * 128:(jq + 1) * 128] if kind == "q" else wkv
            s_w = ring_load([(lambda sl, w_=w_: rslot(sl, [8, 2, 128])[:, :, w_, :], src[:, :, w_, :]) for w_ in range(2)])
            rv = rslot(s_w, [8, 2, 128])
            gn, gns = ("OQN", "OQNS") if kind == "q" else ("OKN", "OKNS")
            cols = [(s, e, ti, s) for ti, (s, e) in enumerate(TILES[:4])]
            if kind == "k":
                cols.append((CTX0, TT, 4, T))
            for (s, e, ti, dcol) in cols:
                n = e - s
                b2 = it % 2
                it += 1
                bb = [P.bank(), P.bank()]
                for w_ in range(2):
                    for d in range(8):
                        P.add("pe", mm(ps[bb[w_]][:, 0:n], rv[:, d, w_, :], A[:, d, s:e], d == 0, d == 7), reads=rkeys(s_w) + [("A", d, ti)], writes=[("ps", bb[w_])])
                P.add("act", act(SQ[b2][:, 0:n], ps[bb[0]][:, 0:n], AF.Square), reads=[("ps", bb[0])], writes=[("SQB", b2)])
                bs = P.bank()
                P.add("pe", mm(ps[bs][:, 0:n], BD[:], SQ[b2][:, 0:n]), reads=[("SQB", b2), "BD"], writes=[("ps", bs)])
                P.add("act", act(LTb[b2][:, 0:n], ps[bs][:, 0:n], AF.Ln, scale=1.0 / 64, bias=C("EPSV")), reads=[("ps", bs)], writes=[("LTB", b2)])
                P.add("act", act(RSb[b2][:, 0:n], LTb[b2][:, 0:n], AF.Exp, scale=-0.5), reads=[("LTB", b2)], writes=[("RSB", b2)])
                t1, t2 = TS[(2 * it) % 4], TS[(2 * it + 1) % 4]
                k1, k2 = ("TS", (2 * it) % 4), ("TS", (2 * it + 1) % 4)
                P.add("dve", stt(t1[:, 0:n], ps[bb[0]][:, 0:n], C(gn), TAB1[:, 0, s:e], ALU.mult, ALU.mult), reads=[("ps", bb[0]), "TAB1", "CONST"], writes=[k1])
                P.add("dve", stt(t2[:, 0:n], ps[bb[1]][:, 0:n], C(gns), TAB1[:, 1, s:e], ALU.mult, ALU.mult), reads=[("ps", bb[1]), "TAB1", "CONST"], writes=[k2])
                P.add("dve", tt(t1[:, 0:n], t1[:, 0:n], t2[:, 0:n], ALU.add), reads=[k1, k2], writes=[k1])
                if kind == "q":
                    P.add("dve", tt(Q1[:, jq, s:e], t1[:, 0:n], RSb[b2][:, 0:n], ALU.mult), reads=[k1, ("RSB", b2)], writes=[("Q1", jq, ti)])
                else:
                    P.add("dve", tt(K1O[:, dcol:dcol + n], t1[:, 0:n], RSb[b2][:, 0:n], ALU.mult), reads=[k1, ("RSB", b2)], writes=[("K1O", ti)])
        if stop in ("pb_q", "pb_only"):
            return
        s_v = ring_load([(lambda sl: rslot(sl, [8, 128]), owv[:, :, 640:768])])
        rvv = rslot(s_v, [8, 128])
        for t0_ in range(0, 17, 4):
            cnt = min(4, 17 - t0_)
            b = P.bank()
            for tq in range(t0_, t0_ + cnt):
                cs, nk = (tq * 128, 128) if tq < 16 else (CTX0, 64)
                ti = min(tq // 4, 4)
                for d in range(8):
                    P.add("pe", mm(ps[b][0:nk, (tq - t0_) * 128:(tq - t0_ + 1) * 128], A[:, d, cs:cs + nk], rvv[:, d, :], d == 0, d == 7),
                          reads=rkeys(s_v) + [("A", d, ti)], writes=[("ps", b)])
            P.add("act", act(VOWN[:, t0_:t0_ + cnt, :], ps[b][:, 0:cnt * 128].rearrange("p (j f) -> p j f", f=128), AF.Copy), reads=[("ps", b)], writes=[("VOWN", t0_ // 4)])
        if stop == "pb_v":
            return
        P.add("sp", dma(kb1[:, :], K1O[:]), reads=[("K1O", t_) for t_ in range(5)], writes=["kb1"], kind="d")
        P.add("sp", dma(vb1[:, :], VOWN[:].rearrange("p t f -> p (t f)")), reads=[("VOWN", t_) for t_ in range(5)], writes=[("vb1", 0), ("vb1", 1)], kind="d")
        if stop == "qkv1":
            return
        def u_load(cc):
            return ring_load([(lambda sl, k=k: rslot(sl, [8, 2, 128])[:, :, k, :], owv[:, :, 768 + k * 512 + cc * 128:768 + k * 512 + (cc + 1) * 128]) for k in range(2)])
        u_slots = {cc_: u_load(cc_) for cc_ in (0, 1)}
        P.add("pool", lambda e: e.collective_compute("AllGather", ALU.bypass, replica_groups=RG, ins=[kb1.ap().opt()], outs=[kg1.ap().opt()]),
              reads=["kb1"], writes=["kg1"], kind="cc")
        P.add("pool", lambda e: e.collective_compute("AllGather", ALU.bypass, replica_groups=RG, ins=[vb1.ap().opt()], outs=[vg1.ap().opt()]),
              reads=[("vb1", 0), ("vb1", 1)], writes=["vg1"], kind="cc")
        P.barrier()
        areset(q1_end)
        CIN1 = [aalloc(f"CIN1{i}", [128, 2080], BF16)[0] for i in range(2)]
        Z = aalloc("Z", [128, 4, T], BF16)[0]
        DG = [aalloc(f"DG{i}", [128, 31, 128], BF16)[0] for i in range(2)]
        TS = [aalloc(f"TS1{i}", [128, 512], F32)[0] for i in range(2)]
        ZSQ = aalloc("ZSQ", [128, 4, 512], BF16)[0]
        MEAN = aalloc("MEAN", [128, 512], F32)[0]
        VAR = aalloc("VAR", [128, 512], F32)[0]
        RS = aalloc("RS1", [128, 512], F32)[0]
        owv = o_w_in.rearrange("(c p) f -> p c f", p=128)
        tsk = [0]

        def uproj(cc):
            cb = cc % 2
            s_u = u_slots.pop(cc) if cc in u_slots else u_load(cc)
            rv = rslot(s_u, [8, 2, 128])
            for ti, (s, e) in enumerate(TILES):
                if ti == 4:
                    e = LAT_END
                n = e - s
                bb = [P.bank(), P.bank()]
                for k in range(2):
                    for d in range(8):
                        P.add("pe", mm(ps[bb[k]][:, 0:n], rv[:, d, k, :], A[:, d, s:e], d == 0, d == 7), reads=[("ring", s_u, k), ("A", d, ti)], writes=[("ps", bb[k])])
                ta = TS[tsk[0] % 2]
                ka = ("TS", tsk[0] % 2)
                tsk[0] += 1
                P.add("act", act(ta[:, 0:n], ps[bb[1]][:, 0:n], AF.Sigmoid), reads=[("ps", bb[1])], writes=[ka])
                if ti < 4:
                    pieces = [(CIN1[cb][:, 16 + s:16 + e], 0, n, None)]
                else:
                    pieces = [(CIN1[cb][:, 0:16], 0, 16, C("MKL", 16, 0)), (CIN1[cb][:, 2064:2080], 16, 32, C("MKL", 16, 16))]
                for (dst, a0, a1, mk) in pieces:
                    P.add("dve", tt(dst, ta[:, a0:a1], ps[bb[0]][:, a0:a1], ALU.mult), reads=[ka, ("ps", bb[0])], writes=[("CIN1", cb, ti)])
                    if mk is not None:
                        P.add("dve", tt(dst, dst, mk, ALU.mult), reads=[("CIN1", cb, ti), "CONST"], writes=[("CIN1", cb, ti)])

        def dgbuild(cc):
            cb = cc % 2
            for k in range(31):
                P.add("dve", ts(DG[cb][:, k, :], IDENT[:], C("OCW", 1, cc * 31 + k), ALU.mult), reads=["IDENT", "CONST"], writes=[("DG", cb)])

        def conv(cc):
            cb = cc % 2
            cin_all = [("CIN1", cb, t_) for t_ in range(5)]
            for ti in range(4):
                s, e = TILES[ti]
                bz = P.bank()
                for k in range(31):
                    P.add("pe", mm(ps[bz][:, 0:512], DG[cb][:, k, :], CIN1[cb][:, 1 + k + s:1 + k + s + 512], k == 0, k == 30), reads=[("DG", cb)] + cin_all, writes=[("ps", bz)])
                P.add("act", act(Z[:, cc, s:e], ps[bz][:, 0:512], AF.Identity, bias=C("OCB", 1, cc)), reads=[("ps", bz), "CONST"], writes=[("Z", cc, ti)])

        dgbuild(0)
        uproj(0)
        for cc in range(4):
            if cc + 1 < 4:
                dgbuild(cc + 1)
                uproj(cc + 1)
            conv(cc)
        s_o = ring_load([(lambda sl: rslot(sl, [4, 1024]), o_w_out[512:1024, :].rearrange("(c p) d -> p c d", p=128))])
        rvo = rslot(s_o, [4, 1024])
        lnb = {}

        def ln1(ti):
            s, e = TILES[ti]
            n = 512
            for cc in range(4):
                P.add("act", act(ZSQ[:, cc, :], Z[:, cc, s:e], AF.Square), reads=[("Z", cc, ti)], writes=[("ZSQ", cc)])
            bm, bv = (0, 1) if ti % 2 == 0 else (2, 3)
            lnb[ti] = (bm, bv)
            for cc in range(4):
                P.add("pe", mm(ps[bm][:, 0:n], ONES[:], Z[:, cc, s:e], cc == 0, cc == 3), reads=[("Z", cc, ti), "ONES"], writes=[("ps", bm)])
            for cc in range(4):
                P.add("pe", mm(ps[bv][:, 0:n], ONES[:], ZSQ[:, cc, :], cc == 0, cc == 3), reads=[("ZSQ", cc), "ONES"], writes=[("ps", bv)])

        def ln2(ti):
            n = 512
            bm, bv = lnb[ti]
            P.add("act", act(MEAN[:, 0:n], ps[bm][:, 0:n], AF.Copy, scale=1.0 / 512), reads=[("ps", bm)], writes=["MEAN"])
            P.add("dve", tt(VAR[:, 0:n], MEAN[:, 0:n], MEAN[:, 0:n], ALU.mult), reads=["MEAN"], writes=["VAR"])
            P.add("dve", stt(VAR[:, 0:n], ps[bv][:, 0:n], 1.0 / 512, VAR[:, 0:n], ALU.mult, ALU.subtract), reads=[("ps", bv), "VAR"], writes=["VAR"])
            P.add("act", act(VAR[:, 0:n], VAR[:, 0:n], AF.Ln, bias=C("EPSV")), reads=["VAR"], writes=["VAR"])
            P.add("act", act(RS[:, 0:n], VAR[:, 0:n], AF.Exp, scale=-0.5), reads=["VAR"], writes=["RS1"])

        def ln3(ti):
            s, e = TILES[ti]
            n = 512
            for cc in range(4):
                ta = TS[tsk[0] % 2]
                ka = ("TS", tsk[0] % 2)
                tsk[0] += 1
                P.add("dve", tt(ta[:, 0:n], Z[:, cc, s:e], MEAN[:, 0:n], ALU.subtract), reads=[("Z", cc, ti), "MEAN"], writes=[ka])
                P.add("dve", tt(ta[:, 0:n], ta[:, 0:n], RS[:, 0:n], ALU.mult), reads=[ka, "RS1"], writes=[ka])
                P.add("act", act(Z[:, cc, s:e], ta[:, 0:n], AF.Silu, scale=C("OLG", 1, cc), bias=C("OLB", 1, cc)), reads=[ka, "CONST"], writes=[("Z", cc, ti)])
            for dc in range(8):
                bo = P.bank()
                for cc in range(4):
                    P.add("pe", mm(ps[bo][:, 0:n], rvo[:, cc, dc * 128:(dc + 1) * 128], Z[:, cc, s:e], cc == 0, cc == 3), reads=rkeys(s_o) + [("Z", cc, ti)], writes=[("ps", bo)])
                h_update(bo, ti, dc, 1, s)

        P.bank_lo, P.bank_n = 4, 4
        ln1(0)
        for ti in range(4):
            if ti + 1 < 4:
                ln1(ti + 1)
            ln2(ti)
            ln3(ti)
        P.bank_lo, P.bank_n = 2, 6
        P.barrier()
        areset(q1_end)
        K1f = aalloc("K1", [128, NKEYS], BF16)[0]
        V1 = aalloc("V1", [128, NT, 2, 65], BF16)[0]
        QP = [aalloc(f"QP{i}", [128, T], BF16)[0] for i in range(2)]
        O = [aalloc(f"O1{i}", [128, T], BF16)[0] for i in range(2)]
        WO = [aalloc(f"WO1{i}", [128, 1024], BF16)[0] for i in range(2)]
        c2 = [ABASE]

        def a2(name, shape, dt):
            t_, _, end = aalloc(name, shape, dt, at=c2[0])
            c2[0] = end
            assert end <= a_end
            return t_
        VST = a2("VST", [128, NT, 128], BF16)
        E = [a2(f"E1{i}", [128, 2, 512], BF16) for i in range(4)]
        REC = [a2(f"REC1{i}", [128, 512], F32) for i in range(2)]
        OU = [a2(f"OU1{i}", [128, 512], F32) for i in range(2)]
        kv1 = kg1.ap().rearrange("(r q) t -> q r t", q=128)
        P.add("sp", dma(K1f[:, 0:4 * T].rearrange("p (r t) -> p r t", r=4), kv1[:, :, 0:T]), reads=["kg1"], writes=[("K1", 0)], kind="d")
        P.add("sp", dma(K1f[:, 4 * T:NKEYS].rearrange("p (r t) -> p r t", r=4), kv1[:, :, T:NKR]), reads=["kg1"], writes=[("K1", 1)], kind="d")
        vv1 = vg1.ap().rearrange("(r p) x -> p r x", p=128)
        P.add("sp", dma(VST[:, 0:64, :].rearrange("p (r t) f -> p r (t f)", r=4), vv1[:, :, 0:16 * 128]), reads=["vg1"], writes=[("VST", 0)], kind="d")
        for r in range(4):
            P.add("sp", dma(VST[(r % 2) * 64:(r % 2) * 64 + 64, 64 + r // 2, :], vg1[r * 128:r * 128 + 64, 16 * 128:17 * 128]), reads=["vg1"], writes=[("VST", 1 + r)], kind="d")
        for i in range(2):
            P.add("dve", (lambda t: (lambda e: e.memset(t[:], 0.0)))(WO[i]), writes=[("WO", i)])
            P.add("dve", (lambda t: (lambda e: e.memset(t[:], 0.0)))(O[i]), writes=[("O", i, t_) for t_ in range(5)])
            P.add("dve", (lambda t: (lambda e: e.memset(t[:], 0.0)))(QP[i]), writes=[("QP", i)])
            P.add("dve", (lambda t: (lambda e: e.memset(t[:], 0.0)))(REC[i]), writes=[("REC", i)])
        vst_all = [("VST", i) for i in range(5)]
        P.add("dve", cp(V1[:, :, 0, 0:64], VST[:, :, 0:64]), reads=vst_all, writes=[("V1", 0)])
        P.add("act", act(V1[:, :, 1, 0:64], VST[:, :, 64:128], AF.Copy), reads=vst_all, writes=[("V1", 1)])
        if stop == "gath1":
            return
        P.add("dve", lambda e: e.memset(V1[:, :, :, 64:65], 1.0), writes=["VONE1"])
        attention(8, 64, 0.125, None, None, lambda h, qb: qgen1(h, qb, QP, Q1),
                  kt_of=lambda h: (K1f, 0, 128), v_of=lambda h, j: V1[:, j, h // 4, :],
                  q_of=lambda h, qb, s, e: QP[qb][:, s:e],
                  qkeys=lambda h, qb, ti: [("QP", qb)], kkeys=lambda ks, nk: [("K1", 0), ("K1", 1)],
                  vkeys=lambda j: [("V1", 0), ("V1", 1)],
                  E=E, O=O, REC=REC, OU=OU, WO=WO, wo_src=lambda h: o_w_out[h * 64:(h + 1) * 64, :],
                  qtiles=[(s, e, ti, False) for ti, (s, e) in enumerate(TILES[:4])], out_tiles=range(4), extra_reads=["VONE1"])

    use_layer(1)
    ffn_block(1, 0, 0)
    if stop == "ffn1_1":
        dump_H()
        return nc, P
    mixer1()
    if stop in ("conf1", "mix1", "qkv1", "gath1", "pb_q", "pb_v", "pb_only"):
        dump_H()
        return nc, P
    ffn_block(1, 1, 2, tiles=range(4))
    areset()
    OUTB = [aalloc(f"OUTB{i}", [128, 8, 512], F32)[0] for i in range(2)]
    tmp = adaln_tmp(at=ABASE + 55616)
    a_keys = [("A", c_, t_) for c_ in range(8) for t_ in range(5)]
    def store_tile(ti):
        s, e = TILES[ti]
        P.add("sp", dma(out_d[:, :, s:e], OUTB[ti % 2][:]), reads=[("OUTB", c, ti % 2) for c in range(8)], writes=[("out", ti)], kind="d")
    adaln(lambda c, s, e: OUTB[(s // 512) % 2][:, c, 0:e - s], lambda c, ti: [("OUTB", c, ti % 2)] + a_keys, lambda g, c: C("FG", 1, c), lambda g, c: FZERO[:, 0:1], range(4), tmp, post=store_tile)
    P.add("sp", None, reads=[("out", ti) for ti in range(4)], kind="w")
    return nc, P


NDS = 16


def emit(nc, P):
    from contextlib import ExitStack
    for q, ops in P.q.items():
        for op in ops:
            for d in op.deps:
                if d.kind == "c" and d.eng == "pe" and q == "pe":
                    continue
                d.needed = True
    idx = {}
    ncc = 0
    for q, ops in P.q.items():
        nd = 0
        for op in ops:
            if op.kind == "c":
                if op.needed:
                    idx[q] = idx.get(q, 0) + 1
                    op.sem = q
                    op.val = idx[q]
            elif op.kind == "d":
                op.sem = (q, nd % NDS)
                op.val = 16 * (nd // NDS + 1)
                nd += 1
            elif op.kind == "cc":
                op.sem = ("cc", ncc)
                op.val = 1
                ncc += 1
    with ExitStack() as st:
        semobj = {}
        for q in ("pe", "act", "dve", "pool"):
            semobj[q] = st.enter_context(nc.semaphore("s_" + q))
        for q in ("pool", "sp"):
            for i in range(NDS):
                semobj[(q, i)] = st.enter_context(nc.semaphore(f"d_{q}{i}"))
        for i in range(ncc):
            semobj[("cc", i)] = st.enter_context(nc.semaphore(f"cc{i}"))
        block = st.enter_context(nc.Block())

        def run(q):
            def body(e):
                known = {}
                for op in P.q[q]:
                    waits = {}
                    for d in op.deps:
                        if d.kind == "c" and d.eng == "pe" and q == "pe":
                            continue
                        if d.sem is None:
                            continue
                        if waits.get(d.sem, 0) < d.val:
                            waits[d.sem] = d.val
                    if op.kind == "d" and op.val > 16:
                        if waits.get(op.sem, 0) < op.val - 16:
                            waits[op.sem] = op.val - 16
                    for s_, v in waits.items():
                        if known.get(s_, 0) < v:
                            e.wait_ge(semobj[s_], v)
                            known[s_] = v
                    if op.kind == "w":
                        continue
                    ins = op.fn(e)
                    if op.kind == "c":
                        if op.needed:
                            ins.then_inc(semobj[op.sem], 1)
                    elif op.kind == "d":
                        ins.then_inc(semobj[op.sem], 16)
                    else:
                        ins.then_inc(semobj[op.sem])
            return body

        block.tensor(run("pe"))
        block.scalar(run("act"))
        block.vector(run("dve"))
        block.gpsimd(run("pool"))
        block.sync(run("sp"))
    return nc


def _fm(v):
    v = np.asarray(v, np.float32)
    return v.reshape(-1, 128).T.copy()


def _rope_tab(rot_dim, pos_valid, rows, cols):
    nf = rot_dim // 4
    inv = (np.float32(10000.0) ** (-np.arange(nf, dtype=np.float32) / np.float32(nf))).astype(np.float32)
    ang = np.concatenate([rows[:, None].astype(np.float32) * inv, cols[:, None].astype(np.float32) * inv], axis=-1)
    cos = np.cos(ang).astype(np.float32)
    sin = np.sin(ang).astype(np.float32)
    cos = np.where(pos_valid[:, None], cos, 1.0)
    sin = np.where(pos_valid[:, None], sin, 0.0)
    cf = np.repeat(cos, 2, axis=1)
    sf = np.repeat(sin, 2, axis=1)
    sf[:, 0::2] *= -1.0
    return cf.T, sf.T


def _swap_pairs(w):
    idx = np.arange(w.shape[-1]) ^ 1
    return w[..., idx]


def host_prep(inp):
    g = {k: np.asarray(v) for k, v in inp.items()}
    x, ctx, c, c_ctx = g["x"], g["ctx"], g["c"], g["c_ctx"]
    shared = {}
    shared["w_mod"] = g["w_mod"]
    shared["w_ffn_in"] = g["w_ffn_in"]
    shared["w_ffn_out"] = g["w_ffn_out"]
    ewi = g["e_w_in"][0]
    shared["e_w_in"] = ewi
    kr = ewi[:, 1920:1952]
    pad = np.zeros((D, 2, 96), np.float32)
    pad[:, 0, 64:] = kr
    pad[:, 1, 64:] = _swap_pairs(kr)
    shared["e_wkr_pad"] = pad
    wqb = g["e_w_q_b"][0]
    shared["e_wqb"] = wqb
    sw = np.zeros((256, 8, 96), np.float32)
    for h in range(8):
        sw[:, h, 64:] = _swap_pairs(wqb[:, h * 96 + 64:(h + 1) * 96])
    shared["e_wqb_swp"] = sw
    shared["e_wkvb"] = g["e_w_kv_b"][0]
    shared["e_w_out"] = g["e_w_out"][0]
    owi = g["o_w_in"][0]
    shared["o_w_in"] = owi
    order = [0, 4, 1, 5, 2, 6, 3, 7]
    oq = np.zeros((D, 2, 512), np.float32)
    for s_, h in enumerate(order):
        blk = owi[:, h * 64:(h + 1) * 64]
        oq[:, 0, s_ * 64:(s_ + 1) * 64] = blk
        oq[:, 1, s_ * 64:(s_ + 1) * 64] = _swap_pairs(blk)
    shared["o_wq"] = oq
    ok = np.zeros((D, 2, 128), np.float32)
    for h in range(2):
        blk = owi[:, 512 + h * 64:512 + (h + 1) * 64]
        ok[:, 0, h * 64:(h + 1) * 64] = blk
        ok[:, 1, h * 64:(h + 1) * 64] = _swap_pairs(blk)
    shared["o_wk"] = ok
    shared["o_w_out"] = g["o_w_out"][0]
    shared = {k: np.ascontiguousarray(v, dtype=np.float32) for k, v in shared.items()}

    base = np.zeros((128, NCONST), np.float32)

    def put(name, arr):
        arr = np.asarray(arr, np.float32)
        base[:, _off[name]:_off[name] + arr.shape[1]] = arr
    put("BM", np.concatenate([_fm(g["b_mod"][l]) for l in range(2)], axis=1))
    put("NG", np.concatenate([_fm(g["norm_g"][l, j]) for l in range(2) for j in range(3)], axis=1))
    put("FG", _fm(g["final_g"]))
    put("ECW", np.stack([g["e_conv_w"][0][:, cc * 128:(cc + 1) * 128].T for cc in range(4)], axis=1).reshape(128, 12))
    put("EQN", _fm(g["e_q_norm"][0]))
    put("EKN", _fm(g["e_kv_norm"][0]))
    p64 = np.arange(128) % 64
    put("OQN", g["o_q_norm"][0][p64][:, None])
    put("OQNS", g["o_q_norm"][0][p64 ^ 1][:, None])
    put("OKN", g["o_k_norm"][0][p64][:, None])
    put("OKNS", g["o_k_norm"][0][p64 ^ 1][:, None])
    put("OCW", np.stack([g["o_conv_w"][0][:, cc * 128:(cc + 1) * 128].T for cc in range(4)], axis=1).reshape(128, 124))
    put("OCB", _fm(g["o_conv_b"][0]))
    put("OLG", _fm(g["o_ln_g"][0]))
    put("OLB", _fm(g["o_ln_b"][0]))

    maps = []
    for r in range(8):
        b, qd = r // 4, r % 4
        t0, c0 = qd * T, qd * 64
        tok = np.concatenate([np.arange(t0, t0 + T), np.arange(t0 - HALO, t0), np.arange(t0 + T, t0 + T + HALO)])
        valid = (tok >= 0) & (tok < SEQ)
        xs = np.zeros((TT, D), np.float32)
        xs[:LAT_END][valid] = x[b][tok[valid]]
        ctok = np.array([c0 - 1, c0 + 64] + list(range(c0, c0 + 64)))
        cvalid = (ctok >= 0) & (ctok < 256)
        xs[LAT_END:][cvalid] = ctx[b][ctok[cvalid]]
        xTh = np.ascontiguousarray(xs.reshape(TT, 8, 128).transpose(2, 1, 0))
        cs = base.copy()
        cv = np.stack([_fm(c[b]), _fm(c_ctx)], axis=2).reshape(128, 16)
        cs[:, _off["CV"]:_off["CV"] + 16] = cv
        cs[:, _off["MKL"]:_off["MKL"] + 32] = valid[T:].astype(np.float32)[None, :]
        cs[:, _off["MKC"]:_off["MKC"] + 2] = cvalid[:2].astype(np.float32)[None, :]
        tokc = np.clip(tok, 0, SEQ - 1)
        rows, cols = tokc // 64, tokc % 64
        tabs = []
        for rot, rowsel in ((32, lambda p: (p >= 64) & (p < 96)), (64, None)):
            cf, sf = _rope_tab(rot, valid, rows, cols)
            tab = np.zeros((128, 2, TT), np.float32)
            tab[:, 0, :] = 1.0
            if rot == 32:
                tab[64:96, 0, :LAT_END] = cf
                tab[64:96, 1, :LAT_END] = sf
            else:
                tab[:, 0, :LAT_END] = np.concatenate([cf, cf], axis=0)
                tab[:, 1, :LAT_END] = np.concatenate([sf, sf], axis=0)
            tabs.append(tab.astype(ml_dtypes.bfloat16))
        m = dict(shared)
        m["xT"] = xTh
        m["consts"] = cs
        m["ident"] = np.eye(128, dtype=np.float32).astype(ml_dtypes.bfloat16)
        m["tab0"] = tabs[0]
        m["tab1"] = tabs[1]
        maps.append(m)
    return maps


_CACHE = {}


def kernel(**inputs):
    maps = host_prep(inputs)
    if "nc" not in _CACHE:
        nc, P = build_nc()
        emit(nc, P)
        _CACHE["nc"] = nc
    nc = _CACHE["nc"]
    res = run_bass_kernel_spmd(nc, maps, core_ids=list(range(8)))
    out = np.zeros((2, SEQ, D), np.float32)
    for r in range(8):
        b, qd = r // 4, r % 4
        o = np.asarray(res.results[r]["out"])
        out[b, qd * T:(qd + 1) * T, :] = o.transpose(2, 1, 0).reshape(T, D)
    return out
```

```python
import numpy as np
import ml_dtypes
import concourse.bass as bass
import concourse.mybir as mybir
from concourse.bass_utils import run_bass_kernel_spmd

F32 = mybir.dt.float32
BF16 = mybir.dt.bfloat16
AF = mybir.ActivationFunctionType
ALU = mybir.AluOpType

D = 1024
SEQ = 8192
T = 2048
HALO = 16
TT = 2146
LAT_END = 2080
CTX0 = 2082
NKR = 2112
NKEYS = 4 * NKR
TILES = [(0, 512), (512, 1024), (1024, 1536), (1536, 2048), (2048, TT)]
EPS = 1e-6
DFF = 2816
NSLOT = 4
SLOT = 4096

_off = {}
_n = 0
for _name, _w in [("CV", 16), ("BM", 144), ("NG", 48), ("FG", 8), ("ECW", 12), ("EQN", 2), ("EKN", 1),
                  ("OQN", 1), ("OQNS", 1), ("OKN", 1), ("OKNS", 1), ("OCW", 124), ("OCB", 4), ("OLG", 4),
                  ("OLB", 4), ("MKL", 32), ("MKC", 2)]:
    _off[_name] = _n
    _n += _w
NCONST = _n


class Op:
    __slots__ = ("eng", "fn", "deps", "kind", "needed", "sem", "val", "inc", "nobar")

    def __init__(self, eng, fn, kind):
        self.eng = eng
        self.fn = fn
        self.kind = kind
        self.deps = ()
        self.needed = False
        self.sem = None
        self.val = 0
        self.inc = 1


class Prog:
    CE = ("pe", "act", "dve", "pool")

    def __init__(self):
        self.q = {"pe": [], "act": [], "dve": [], "pool": [], "sp": []}
        self.last_w = {}
        self.readers = {}
        self.bar = []
        self.rr = 0
        self.ring_i = 0
        self.bar_pos = {}
        self.bank_lo, self.bank_n = 2, 6

    def add(self, eng, fn, reads=(), writes=(), kind="c", nobar=False):
        op = Op(eng, fn, kind)
        op.nobar = nobar
        deps = set() if nobar else set(self.bar)
        lw = self.last_w
        rd = self.readers
        for r in reads:
            w = lw.get(r)
            if w is not None:
                deps.add(w)
            if type(r) is tuple and r[0] == "ps":
                rs = rd.get(r)
                if rs:
                    for k_, o_ in rs.items():
                        if k_ != eng:
                            deps.add(o_)
        for w_ in writes:
            w = lw.get(w_)
            if w is not None:
                deps.add(w)
            rs = rd.get(w_)
            if rs:
                deps.update(rs.values())
        op.deps = deps
        key = eng if kind == "c" else id(op)
        for r in reads:
            d = rd.get(r)
            if d is None:
                d = rd[r] = {}
            d[key] = op
        for w_ in writes:
            lw[w_] = op
            rd[w_] = {}
        self.q[eng].append(op)
        return op

    def barrier(self):
        bar = []
        for e in self.CE:
            for op in reversed(self.q[e]):
                if op.kind == "c":
                    bar.append(op)
                    break
        for e in ("pool", "sp"):
            for op in self.q[e][self.bar_pos.get(e, 0):]:
                if op.kind == "d" and not op.nobar:
                    bar.append(op)
            self.bar_pos[e] = len(self.q[e])
        self.bar = bar

    def bank(self):
        b = self.bank_lo + self.rr % self.bank_n
        self.rr += 1
        return b

    def ring(self):
        s = self.ring_i % NSLOT
        self.ring_i += 1
        return s


def build_nc(stop=None):
    nc = bass.Bass("TRN2", target_bir_lowering=False)
    P = Prog()

    def din(name, shape, dt=F32):
        return nc.dram_tensor(name, list(shape), dt, kind="ExternalInput").ap()

    xT = din("xT", [128, 8, TT])
    consts_d = din("consts", [128, NCONST])
    ident_d = din("ident", [128, 128], BF16)
    tab0_d = din("tab0", [128, 2, TT], BF16)
    tab1_d = din("tab1", [128, 2, TT], BF16)
    w_mod = din("w_mod", [2, D, 9 * D])
    w_ffn_in = din("w_ffn_in", [2, 2, D, 2 * DFF])
    w_ffn_out = din("w_ffn_out", [2, 2, DFF, D])
    e_w_in = din("e_w_in", [D, 1952])
    e_wkr_pad = din("e_wkr_pad", [D, 2, 96])
    e_wqb = din("e_wqb", [256, 768])
    e_wqb_swp = din("e_wqb_swp", [256, 8, 96])
    e_wkvb = din("e_wkvb", [128, 1024])
    e_w_out = din("e_w_out", [D, D])
    o_wq = din("o_wq", [D, 2, 512])
    o_wk = din("o_wk", [D, 2, 128])
    o_w_in = din("o_w_in", [D, 1792])
    o_w_out = din("o_w_out", [D, D])
    out_d = nc.dram_tensor("out", [128, 8, TT if stop else T], F32, kind="ExternalOutput").ap()
    kvb0 = nc.dram_tensor("kvb0", [160, NKR], BF16)
    kvg0 = nc.dram_tensor("kvg0", [640, NKR], BF16)
    kb1 = nc.dram_tensor("kb1", [128, NKR], BF16)
    kg1 = nc.dram_tensor("kg1", [512, NKR], BF16)
    vb1 = nc.dram_tensor("vb1", [128, 17 * 128], BF16)
    vg1 = nc.dram_tensor("vg1", [512, 17 * 128], BF16)

    H = nc.alloc_sbuf_tensor("H", [128, 8, TT], F32)
    RING = nc.alloc_sbuf_tensor("RING", [128, NSLOT, SLOT], BF16)
    CONST = nc.alloc_sbuf_tensor("CONST", [128, NCONST], F32)
    SCB = nc.alloc_sbuf_tensor("SCB", [128, 16], BF16)
    MODL = nc.alloc_sbuf_tensor("MODL", [128, 72], F32)
    MODC = nc.alloc_sbuf_tensor("MODC", [128, 72], F32)
    MODS = [{"L": MODL, "C": MODC}, {"L": nc.alloc_sbuf_tensor("MODL1", [128, 72], F32), "C": nc.alloc_sbuf_tensor("MODC1", [128, 72], F32)}]
    MOD = dict(MODS[0])
    IDENT = nc.alloc_sbuf_tensor("IDENT", [128, 128], BF16)
    GSS = [{"L": nc.alloc_sbuf_tensor("GSL", [128, 3, 8], F32), "C": nc.alloc_sbuf_tensor("GSC", [128, 3, 8], F32)},
           {"L": nc.alloc_sbuf_tensor("GSL1", [128, 3, 8], F32), "C": nc.alloc_sbuf_tensor("GSC1", [128, 3, 8], F32)}]
    GATES = [{"L": nc.alloc_sbuf_tensor("GTL", [128, 3, 8], F32), "C": nc.alloc_sbuf_tensor("GTC", [128, 3, 8], F32)},
             {"L": nc.alloc_sbuf_tensor("GTL1", [128, 3, 8], F32), "C": nc.alloc_sbuf_tensor("GTC1", [128, 3, 8], F32)}]
    GS = dict(GSS[0])
    GATE = dict(GATES[0])
    LS = [0]

    def mk(name, g):
        return f"{name}{g}{LS[0]}"

    def use_layer(l):
        LS[0] = l
        MOD.update(MODS[l])
        GS.update(GSS[l])
        GATE.update(GATES[l])
    ONES = nc.alloc_sbuf_tensor("ONES", [128, 128], BF16)
    BD = nc.alloc_sbuf_tensor("BD", [128, 128], BF16)
    ONES32 = nc.alloc_sbuf_tensor("ONES32", [128, 128], F32)
    FONE = nc.alloc_sbuf_tensor("FONE", [128, 8], F32)
    FZERO = nc.alloc_sbuf_tensor("FZERO", [128, 8], F32)
    EPSV = nc.alloc_sbuf_tensor("EPSV", [128, 1], F32)
    ABASE = nc.sbuf_base
    ATOP = nc.sbuf_top
    acur = [ABASE]

    def aalloc(name, shape, dt, at=None):
        esz = 4 if dt == F32 else 2
        nbytes = int(np.prod(shape[1:])) * esz
        off = acur[0] if at is None else at
        off = (off + 31) // 32 * 32
        assert off + nbytes <= ATOP, (name, off, nbytes, ATOP)
        t = nc.alloc_sbuf_tensor_at(name, list(shape), dt, offset=off)
        if at is None:
            acur[0] = off + nbytes
        return t, off, off + nbytes

    def areset(to=None):
        acur[0] = ABASE if to is None else to

    PSALL = nc.alloc_psum_tensor("psall", [128, 4096], F32)
    ps = [PSALL[:, i * 512:(i + 1) * 512] for i in range(8)]

    def C(name, w=None, o=0):
        if name == "EPSV":
            return EPSV[:, 0:1]
        a = _off[name] + o
        return CONST[:, a:a + (1 if w is None else w)]

    def mm(out, lhsT, rhs, start=True, stop=True):
        return lambda e: e.matmul(out, lhsT=lhsT, rhs=rhs, start=start, stop=stop)

    def act(out, in_, func, scale=1.0, bias=0.0):
        return lambda e: e.activation(out=out, in_=in_, func=func, bias=bias, scale=scale)

    def tt(out, in0, in1, op):
        return lambda e: e.tensor_tensor(out=out, in0=in0, in1=in1, op=op)

    def ts(out, in0, s1, op0, s2=None, op1=None):
        if op1 is None:
            return lambda e: e.tensor_scalar(out=out, in0=in0, scalar1=s1, scalar2=None, op0=op0)
        return lambda e: e.tensor_scalar(out=out, in0=in0, scalar1=s1, scalar2=s2, op0=op0, op1=op1)

    def stt(out, in0, scalar, in1, op0, op1):
        return lambda e: e.scalar_tensor_tensor(out=out, in0=in0, scalar=scalar, in1=in1, op0=op0, op1=op1)

    def cp(out, in_):
        return lambda e: (e.tensor_copy(out=out, in_=in_) if hasattr(e, "tensor_copy") else e.activation(out=out, in_=in_, func=AF.Copy))

    def dma(out, in_):
        return lambda e: e.dma_start(out=out, in_=in_)

    def segs(ti):
        s, e = TILES[ti]
        if ti < 4:
            return [(s, e, "L")]
        return [(2048, LAT_END, "L"), (LAT_END, TT, "C")]

    def Hres(ti):
        return [("H", c, ti) for c in range(8)]

    def rslot(slot, shape):
        v = RING[:, slot, 0:int(np.prod(shape))]
        if len(shape) == 2:
            return v.rearrange("p (a b) -> p a b", a=shape[0])
        if len(shape) == 3:
            return v.rearrange("p (a b c) -> p a b c", a=shape[0], b=shape[1])
        return v

    for c in range(8):
        P.add("sp", dma(H[:, c, :], xT[:, c, :]), writes=[("H", c, ti) for ti in range(5)], kind="d")
    P.add("sp", dma(CONST[:], consts_d), writes=["CONST"], kind="d")
    P.add("sp", dma(IDENT[:], ident_d), writes=["IDENT"], kind="d")
    P.add("dve", lambda e: e.memset(ONES[:], 1.0), writes=["ONES"])
    P.add("dve", lambda e: e.memset(ONES32[:], 0.0), writes=["ONES32"])
    P.add("dve", lambda e: e.memset(ONES32[64:65, :], 1.0), reads=["ONES32"], writes=["ONES32"])
    P.add("dve", lambda e: e.memset(FONE[:], 1.0), writes=["FONE"])
    P.add("dve", lambda e: e.memset(FZERO[:], 0.0), writes=["FZERO"])
    P.add("dve", lambda e: e.memset(BD[:], 0.0), writes=["BD"])
    P.add("dve", lambda e: e.memset(BD[0:64, 0:64], 1.0), reads=["BD"], writes=["BD"])
    P.add("dve", lambda e: e.memset(BD[64:128, 64:128], 1.0), reads=["BD"], writes=["BD"])
    P.add("act", act(SCB[:], C("CV", 16), AF.Silu), reads=["CONST"], writes=["SCB"])

    MODW = [aalloc(f"MODW{i}", [128, 8, 512], BF16, at=ABASE + 55616 + i * 8192)[0] for i in range(6)]
    modw_i = [0]

    def mod_piece(l, p, ksuf=""):
        i = modw_i[0] % 6
        modw_i[0] += 1
        wv = w_mod[l].rearrange("(c p) f -> p c f", p=128)

        def dma_part():
            xr = [("MODW", 3)] if (l == 0 and p == 4) else []
            P.add("pool", dma(MODW[i][:], wv[:, :, p * 512:(p + 1) * 512]), reads=xr, writes=[("MODW", i)], kind="d")

        def comp_part():
            bank = P.bank()
            for fc in range(4):
                for d in range(8):
                    P.add("pe", mm(ps[bank][:, fc * 2:fc * 2 + 2], MODW[i][:, d, fc * 128:(fc + 1) * 128], SCB[:, d * 2:d * 2 + 2], d == 0, d == 7),
                          reads=[("MODW", i), "SCB"], writes=[("ps", bank)])
            psv = ps[bank][:, 0:8].rearrange("p (f two) -> p f two", two=2)
            bm = C("BM", 4, 72 * l + p * 4)
            P.add("dve", tt(MODS[l]["L"][:, p * 4:(p + 1) * 4], psv[:, :, 0], bm, ALU.add), reads=[("ps", bank), "CONST"], writes=[f"MODL{l}" + ksuf])
            P.add("dve", tt(MODS[l]["C"][:, p * 4:(p + 1) * 4], psv[:, :, 1], bm, ALU.add), reads=[("ps", bank), "CONST"], writes=[f"MODC{l}" + ksuf])
        return dma_part, comp_part

    def mod_derive(l, j, parts=("gs", "gate"), ksuf=""):
        for g in ("L", "C"):
            ng = C("NG", 8, (l * 3 + j) * 8)
            if "gs" in parts:
                P.add("dve", stt(GSS[l][g][:, j, :], MODS[l][g][:, (3 * j + 1) * 8:(3 * j + 2) * 8], 1.0, ng, ALU.add, ALU.mult),
                      reads=[f"MOD{g}{l}", "CONST"], writes=[f"GS{g}{l}"])
            if "gate" in parts:
                P.add("dve", ts(GATES[l][g][:, j, :], MODS[l][g][:, (3 * j + 2) * 8:(3 * j + 3) * 8], 1.0 if j == 1 else 0.5, ALU.mult),
                      reads=[f"MOD{g}{l}" + ksuf], writes=[f"GATE{g}{l}"])

    def adaln(dest, dres, scale_of, shift_of, tiles=range(5), tmp=None, post=None):
        SQ, RS, LT, TMP = tmp
        kk = [0]
        banks = {}
        tl = list(tiles)

        def stage_a(ti):
            s, e = TILES[ti]
            n = e - s
            b2 = ti % 2
            P.add("act", act(SQ[b2][:, :, 0:n], H[:, :, s:e], AF.Square), reads=Hres(ti), writes=[("SQ", b2)])
            bank = P.bank()
            banks[ti] = bank
            for c in range(8):
                P.add("pe", mm(ps[bank][:, 0:n], ONES[:], SQ[b2][:, c, 0:n], c == 0, c == 7), reads=[("SQ", b2), "ONES"], writes=[("ps", bank)])

        def stage_b(ti):
            s, e = TILES[ti]
            n = e - s
            b2 = ti % 2
            bank = banks[ti]
            P.add("act", act(LT[b2][:, 0:n], ps[bank][:, 0:n], AF.Ln, scale=1.0 / D, bias=C("EPSV")), reads=[("ps", bank), "CONST"], writes=[("LT", b2)])
            P.add("act", act(RS[b2][:, 0:n], LT[b2][:, 0:n], AF.Exp, scale=-0.5), reads=[("LT", b2)], writes=[("RS", b2)])

        def stage_c(ti):
            s, e = TILES[ti]
            b2 = ti % 2
            for c in range(8):
                for (ss, ee, g) in segs(ti):
                    t4 = kk[0] % 4
                    kk[0] += 1
                    P.add("dve", stt(TMP[t4][:, 0:ee - ss], H[:, c, ss:ee], scale_of(g, c), RS[b2][:, ss - s:ee - s], ALU.mult, ALU.mult),
                          reads=[("H", c, ti), ("RS", b2), mk("GS", g)], writes=[("TMP", t4)])
                    if c % 2 == 0:
                        P.add("act", act(dest(c, ss, ee), TMP[t4][:, 0:ee - ss], AF.Identity, bias=shift_of(g, c)),
                              reads=[("TMP", t4), mk("MOD", g)], writes=(dres(c, ti) if callable(dres) else [(dres, c, ti)]))
                    else:
                        P.add("dve", ts(dest(c, ss, ee), TMP[t4][:, 0:ee - ss], shift_of(g, c), ALU.add),
                              reads=[("TMP", t4), mk("MOD", g)], writes=(dres(c, ti) if callable(dres) else [(dres, c, ti)]))

        n_ = len(tl)
        stage_a(tl[0])
        for i_ in range(n_):
            if i_ + 1 < n_:
                stage_a(tl[i_ + 1])
            stage_b(tl[i_])
            stage_c(tl[i_])
            if post is not None:
                post(tl[i_])

    def adaln_tmp(at=None):
        if at is None:
            SQ = [aalloc(f"SQ{i}", [128, 8, 512], BF16)[0] for i in range(2)]
            RS = [aalloc(f"RS{i}", [128, 512], F32)[0] for i in range(2)]
            LT = [aalloc(f"LT{i}", [128, 512], F32)[0] for i in range(2)]
            TMP = [aalloc(f"TMP{i}", [128, 512], F32)[0] for i in range(4)]
            return SQ, RS, LT, TMP
        cur = [at]

        def a3(name, shape, dt):
            t_, _, end = aalloc(name, shape, dt, at=cur[0])
            cur[0] = end
            return t_
        SQ = [a3(f"SQx{i}", [128, 8, 512], BF16) for i in range(2)]
        RS = [a3(f"RSx{i}", [128, 512], F32) for i in range(2)]
        LT = [a3(f"LTx{i}", [128, 512], F32) for i in range(2)]
        TMP = [a3(f"TMPx{i}", [128, 512], F32) for i in range(4)]
        return SQ, RS, LT, TMP

    def ffn(l, k, j, A, HID, SG, tiles=range(5), sched=None):
        win = w_ffn_in[l, k].rearrange("(c p) f -> p c f", p=128)
        wout = w_ffn_out[l, k]
        groups = [[0, 1], [2, 3], [4, 5], [6, 7], [8, 9], [10]]
        sgk = 0
        carry = []
        for gi, grp in enumerate(groups):
            for pi_l, pi in enumerate(grp):
                slot = P.ring()
                rv = rslot(slot, [8, 2, 256])
                P.add("pool", dma(rv[:, :, 0, :], win[:, :, pi * 256:(pi + 1) * 256]), writes=[("ring", slot, 0)], kind="d", nobar=True)
                P.add("pool", dma(rv[:, :, 1, :], win[:, :, DFF + pi * 256:DFF + (pi + 1) * 256]), writes=[("ring", slot, 1)], kind="d", nobar=True)
                for sub in range(2):
                    fl = pi_l * 2 + sub
                    for ti in tiles:
                        s, e = TILES[ti]
                        n = e - s
                        bg = P.bank()
                        for d in range(8):
                            P.add("pe", mm(ps[bg][:, 0:n], rv[:, d, 0, sub * 128:(sub + 1) * 128], A[:, d, s:e], d == 0, d == 7),
                                  reads=[("ring", slot, 0), ("A", d, ti)], writes=[("ps", bg)])
                        bu = P.bank()
                        for d in range(8):
                            P.add("pe", mm(ps[bu][:, 0:n], rv[:, d, 1, sub * 128:(sub + 1) * 128], A[:, d, s:e], d == 0, d == 7),
                                  reads=[("ring", slot, 1), ("A", d, ti)], writes=[("ps", bu)])
                        sg = sgk % 2
                        sgk += 1
                        P.add("act", act(SG[sg][:, 0:n], ps[bg][:, 0:n], AF.Silu), reads=[("ps", bg)], writes=[("SG", sg)])
                        P.add("dve", tt(HID[:, fl, s:e], SG[sg][:, 0:n], ps[bu][:, 0:n], ALU.mult), reads=[("SG", sg), ("ps", bu)], writes=[("HID", fl, ti)])
            oslots = []
            for pi in grp:
                slot = P.ring()
                rv = rslot(slot, [2, 1024])
                P.add("pool", dma(rv, wout[pi * 256:(pi + 1) * 256, :].rearrange("(s p) d -> p s d", p=128)),
                      writes=[("ring", slot, 0), ("ring", slot, 1)], kind="d", nobar=True)
                oslots.append((slot, rv))
            pend = []
            for task in (sched[gi] if sched else []):
                if task[0] == "piece":
                    d_, c_ = mod_piece(task[1], task[2])
                    d_()
                    pend.append(c_)
                else:
                    pend.append((lambda t_=task: mod_derive(t_[1], t_[2])))
            nf = 2 * len(grp)
            for ti in tiles:
                s, e = TILES[ti]
                n = e - s
                for dc in range(8):
                    bo = P.bank()
                    for fl in range(nf):
                        slot, rv = oslots[fl // 2]
                        P.add("pe", mm(ps[bo][:, 0:n], rv[:, fl % 2, dc * 128:(dc + 1) * 128], HID[:, fl, s:e], fl == 0, fl == nf - 1),
                              reads=[("ring", slot, 0), ("ring", slot, 1), ("HID", fl, ti)], writes=[("ps", bo)])
                    for (ss, ee, g) in segs(ti):
                        P.add("dve", stt(H[:, dc, ss:ee], ps[bo][:, ss - s:ee - s], GATE[g][:, j, dc:dc + 1], H[:, dc, ss:ee], ALU.mult, ALU.add),
                              reads=[("ps", bo), mk("GATE", g), ("H", dc, ti)], writes=[("H", dc, ti)])
            for f_ in carry:
                f_()
            carry = pend
        for f_ in carry:
            f_()

    def ffn_block(l, k, j, tiles=range(5), sched=None):
        P.barrier()
        areset()
        A, _, a_end = aalloc("A", [128, 8, TT], BF16)
        tmp = adaln_tmp()
        adaln(lambda c, s, e: A[:, c, s:e], "A", lambda g, c: GS[g][:, j, c:c + 1], lambda g, c: MOD[g][:, 3 * j * 8 + c:3 * j * 8 + c + 1], tiles, tmp)
        P.barrier()
        if stop == "adaln0":
            for c in range(8):
                P.add("act", cp(H[:, c, :], A[:, c, :]), writes=[("H", c, ti) for ti in range(5)])
            return
        areset(a_end)
        HID = aalloc("HID", [128, 4, TT], BF16)[0]
        SG = [aalloc(f"SG{i}", [128, 512], F32)[0] for i in range(2)]
        ffn(l, k, j, A, HID, SG, tiles, sched)

    def dump_H():
        P.barrier()
        for c in range(8):
            P.add("sp", dma(out_d[:, c, :], H[:, c, :]), reads=[("H", c, ti) for ti in range(5)], writes=[("out", c)], kind="d")
        P.add("sp", None, reads=[("out", c) for c in range(8)], kind="w")

    P.add("dve", lambda e: e.memset(EPSV[:], EPS), writes=["CONST2"])

    comps = []
    for p_ in range(6):
        d_, c_ = mod_piece(0, p_, ksuf="" if p_ < 4 else "g")
        d_()
        comps.append(c_)
    for p_ in range(4):
        comps[p_]()
    mod_derive(0, 0, parts=("gs",))
    for p_ in range(4, 6):
        comps[p_]()
    mod_derive(0, 0, parts=("gate",), ksuf="g")
    sched0 = [[("piece", 0, 6 + 2 * gi), ("piece", 0, 7 + 2 * gi)] for gi in range(6)]
    sched0[2].append(("derive", 0, 1))
    sched0[5].append(("derive", 0, 2))
    ffn_block(0, 0, 0, sched=sched0)
    if stop in ("ffn1_0", "adaln0"):
        dump_H()
        return nc, P

    RG = [[0, 1, 2, 3], [4, 5, 6, 7]]
    NT = 66
    KEYT = [(j * 128, 128, j >= 64) for j in range(NT)]

    def ring_load(dmas):
        slot = P.ring()
        for i, (dst, src) in enumerate(dmas):
            keys = [("ring", slot, i)]
            if i == 0:
                keys += [("ring", slot, k) for k in range(len(dmas), 3)]
            P.add("pool", dma(dst(slot), src), writes=keys, kind="d", nobar=True)
        return slot

    def rkeys(slot):
        return [("ring", slot, k) for k in range(3)]

    def h_update(bank, ti, dc, j, s):
        for (ss, ee, g) in segs(ti):
            P.add("dve", stt(H[:, dc, ss:ee], ps[bank][:, ss - s:ee - s], GATE[g][:, j, dc:dc + 1], H[:, dc, ss:ee], ALU.mult, ALU.add),
                  reads=[("ps", bank), mk("GATE", g), ("H", dc, ti)], writes=[("H", dc, ti)])

    def mixer0():
        P.barrier()
        areset()
        TAB0 = aalloc("TAB0", [128, 2, TT], BF16)[0]
        QAN, _, pers_end = aalloc("QAN", [128, 2, TT], BF16)
        P.add("sp", dma(TAB0[:], tab0_d), writes=["TAB0"], kind="d")
        A, _, a_end = aalloc("A0", [128, 8, TT], BF16)
        tmp = adaln_tmp()
        adaln(lambda c, s, e: A[:, c, s:e], "A", lambda g, c: GS[g][:, 1, c:c + 1], lambda g, c: MOD[g][:, 24 + c:25 + c], range(5), tmp)
        P.barrier()
        areset(a_end)
        CKVN = aalloc("CKVN", [128, TT], BF16)[0]
        KRR = aalloc("KRR", [128, TT], BF16)[0]
        SQ2 = [aalloc(f"SQ2{i}", [128, 3, 512], BF16)[0] for i in range(1)]
        LT2 = [aalloc(f"LT2{i}", [128, 512], F32)[0] for i in range(2)]
        RS2 = [aalloc(f"RS2{i}", [128, 512], F32)[0] for i in range(2)]
        TS = [aalloc(f"TS{i}", [128, 512], F32)[0] for i in range(4)]
        CIN = aalloc("CIN", [128, 2080], BF16)[0]
        CINC = aalloc("CINC", [128, 66], BF16)[0]
        SBB = aalloc("SBB", [128, TT], BF16)[0]
        DG3 = aalloc("DG3", [128, 3, 128], BF16)[0]
        YA = aalloc("YA", [128, 4, TT], BF16)[0]
        P.add("dve", lambda e: e.memset(YA[:, :, 2048:CTX0], 0.0), writes=[("YA", cc_, 4) for cc_ in range(4)])
        ewv = e_w_in.rearrange("(c p) f -> p c f", p=128)
        s_qk = ring_load([(lambda sl: rslot(sl, [8, 384]), ewv[:, :, 1536:1920])])
        rv_qk = rslot(s_qk, [8, 384])
        s_kr = ring_load([(lambda sl: rslot(sl, [8, 2, 96]), e_wkr_pad.rearrange("(c p) s f -> p c s f", p=128))])
        rv_kr = rslot(s_kr, [8, 2, 96])
        tsk = 0
        for ti, (s, e) in enumerate(TILES):
            n = e - s
            b2 = 0
            bq = [P.bank(), P.bank()]
            for cq in range(2):
                for d in range(8):
                    P.add("pe", mm(ps[bq[cq]][:, 0:n], rv_qk[:, d, cq * 128:(cq + 1) * 128], A[:, d, s:e], d == 0, d == 7),
                          reads=rkeys(s_qk) + [("A", d, ti)], writes=[("ps", bq[cq])])
                P.add("act", act(SQ2[b2][:, cq, 0:n], ps[bq[cq]][:, 0:n], AF.Square), reads=[("ps", bq[cq])], writes=[("SQ2", b2, cq)])
            bs = P.bank()
            for cq in range(2):
                P.add("pe", mm(ps[bs][:, 0:n], ONES[:], SQ2[b2][:, cq, 0:n], cq == 0, cq == 1), reads=[("SQ2", b2, cq)], writes=[("ps", bs)])
            P.add("act", act(LT2[0][:, 0:n], ps[bs][:, 0:n], AF.Ln, scale=1.0 / 256, bias=C("EPSV")), reads=[("ps", bs)], writes=[("LT2", 0)])
            P.add("act", act(RS2[0][:, 0:n], LT2[0][:, 0:n], AF.Exp, scale=-0.5), reads=[("LT2", 0)], writes=[("RS2", 0)])
            for cq in range(2):
                P.add("dve", stt(QAN[:, cq, s:e], ps[bq[cq]][:, 0:n], C("EQN", 1, cq), RS2[0][:, 0:n], ALU.mult, ALU.mult),
                      reads=[("ps", bq[cq]), ("RS2", 0), "CONST"], writes=[("QAN", ti)])
            bk = P.bank()
            for d in range(8):
                P.add("pe", mm(ps[bk][:, 0:n], rv_qk[:, d, 256:384], A[:, d, s:e], d == 0, d == 7), reads=rkeys(s_qk) + [("A", d, ti)], writes=[("ps", bk)])
            P.add("act", act(SQ2[b2][:, 2, 0:n], ps[bk][:, 0:n], AF.Square), reads=[("ps", bk)], writes=[("SQ2", b2, 2)])
            bs2 = P.bank()
            P.add("pe", mm(ps[bs2][:, 0:n], ONES[:], SQ2[b2][:, 2, 0:n]), reads=[("SQ2", b2, 2)], writes=[("ps", bs2)])
            P.add("act", act(LT2[1][:, 0:n], ps[bs2][:, 0:n], AF.Ln, scale=1.0 / 128, bias=C("EPSV")), reads=[("ps", bs2)], writes=[("LT2", 1)])
            P.add("act", act(RS2[1][:, 0:n], LT2[1][:, 0:n], AF.Exp, scale=-0.5), reads=[("LT2", 1)], writes=[("RS2", 1)])
            P.add("dve", stt(CKVN[:, s:e], ps[bk][:, 0:n], C("EKN"), RS2[1][:, 0:n], ALU.mult, ALU.mult),
                  reads=[("ps", bk), ("RS2", 1), "CONST"], writes=[("CKVN", ti)])
            br = [P.bank(), P.bank()]
            for w_ in range(2):
                for d in range(8):
                    P.add("pe", mm(ps[br[w_]][0:96, 0:n], rv_kr[:, d, w_, :], A[:, d, s:e], d == 0, d == 7), reads=rkeys(s_kr) + [("A", d, ti)], writes=[("ps", br[w_])])
            ta, tb = TS[tsk % 4], TS[(tsk + 1) % 4]
            ka, kb_ = ("TS", tsk % 4), ("TS", (tsk + 1) % 4)
            tsk += 2
            P.add("dve", tt(ta[64:96, 0:n], ps[br[0]][64:96, 0:n], TAB0[64:96, 0, s:e], ALU.mult), reads=[("ps", br[0]), "TAB0"], writes=[ka])
            P.add("dve", tt(tb[64:96, 0:n], ps[br[1]][64:96, 0:n], TAB0[64:96, 1, s:e], ALU.mult), reads=[("ps", br[1]), "TAB0"], writes=[kb_])
            P.add("dve", tt(KRR[64:96, s:e], ta[64:96, 0:n], tb[64:96, 0:n], ALU.add), reads=[ka, kb_], writes=[("KRR", ti)])
        P.add("sp", dma(kvb0[0:128, 0:T], CKVN[:, 0:T]), reads=[("CKVN", t_) for t_ in range(4)], writes=[("kvb0", 0)], kind="d")
        P.add("sp", dma(kvb0[0:128, T:NKR], CKVN[:, CTX0:TT]), reads=[("CKVN", 4)], writes=[("kvb0", 1)], kind="d")
        P.add("sp", dma(kvb0[128:160, 0:T], KRR[64:96, 0:T]), reads=[("KRR", t_) for t_ in range(4)], writes=[("kvb0", 2)], kind="d")
        P.add("sp", dma(kvb0[128:160, T:NKR], KRR[64:96, CTX0:TT]), reads=[("KRR", 4)], writes=[("kvb0", 3)], kind="d")
        def conv_load(cc):
            return ring_load([(lambda sl, k=k: rslot(sl, [8, 3, 128])[:, :, k, :], ewv[:, :, k * 512 + cc * 128:k * 512 + (cc + 1) * 128]) for k in range(3)])
        conv_slots = {cc_: conv_load(cc_) for cc_ in (0, 1)}
        P.add("pool", lambda e: e.collective_compute("AllGather", ALU.bypass, replica_groups=RG, ins=[kvb0.ap().opt()], outs=[kvg0.ap().opt()]),
              reads=[("kvb0", i) for i in range(4)], writes=["kvg0"], kind="cc")

        for cc in range(4):
            s_c = conv_slots.pop(cc) if cc in conv_slots else conv_load(cc)
            rv = rslot(s_c, [8, 3, 128])
            for ti, (s, e) in enumerate(TILES):
                n = e - s
                bb = [P.bank(), P.bank(), P.bank()]
                for k in range(3):
                    for d in range(8):
                        P.add("pe", mm(ps[bb[k]][:, 0:n], rv[:, d, k, :], A[:, d, s:e], d == 0, d == 7), reads=[("ring", s_c, k), ("A", d, ti)], writes=[("ps", bb[k])])
                ta = TS[tsk % 4]
                ka = ("TS", tsk % 4)
                tsk += 1
                P.add("act", act(ta[:, 0:n], ps[bb[1]][:, 0:n], AF.Copy), reads=[("ps", bb[1])], writes=[ka])
                if ti < 4:
                    pieces = [(CIN[:, 16 + s:16 + e], 0, n, None)]
                else:
                    pieces = [(CIN[:, 0:16], 0, 16, C("MKL", 16, 0)), (CIN[:, 2064:2080], 16, 32, C("MKL", 16, 16)),
                              (CINC[:, 0:1], 32, 33, C("MKC", 1, 0)), (CINC[:, 65:66], 33, 34, C("MKC", 1, 1)), (CINC[:, 1:65], 34, 98, None)]
                for (dst, a0, a1, mk) in pieces:
                    P.add("dve", tt(dst, ta[:, a0:a1], ps[bb[2]][:, a0:a1], ALU.mult), reads=[ka, ("ps", bb[2])], writes=[("CIN", ti)])
                    if mk is not None:
                        P.add("dve", tt(dst, dst, mk, ALU.mult), reads=[("CIN", ti), "CONST"], writes=[("CIN", ti)])
                P.add("act", act(SBB[:, s:e], ps[bb[0]][:, 0:n], AF.Copy), reads=[("ps", bb[0])], writes=[("SBB", ti)])
            w = [C("ECW", 1, cc * 3 + k) for k in range(3)]
            cin_all = [("CIN", t_) for t_ in range(5)]
            for k in range(3):
                P.add("dve", ts(DG3[:, k, :], IDENT[:], w[k], ALU.mult), reads=["IDENT", "CONST"], writes=["DG3"])
            for ti in range(4):
                s, e = TILES[ti]
                bz = P.bank()
                for k in range(3):
                    P.add("pe", mm(ps[bz][:, 0:512], DG3[:, k, :], CIN[:, 15 + s + k:15 + s + k + 512], k == 0, k == 2), reads=["DG3"] + cin_all, writes=[("ps", bz)])
                P.add("dve", tt(YA[:, cc, s:e], SBB[:, s:e], ps[bz][:, 0:512], ALU.mult), reads=[("SBB", ti), ("ps", bz)], writes=[("YA", cc, ti)])
            bz = P.bank()
            for (c0, n_, srcf) in [(0, 15, lambda k: CIN[:, k:k + 15]), (15, 15, lambda k: CIN[:, 2063 + k:2063 + k + 15]), (30, 64, lambda k: CINC[:, k:k + 64])]:
                for k in range(3):
                    P.add("pe", mm(ps[bz][:, c0:c0 + n_], DG3[:, k, :], srcf(k), k == 0, k == 2), reads=["DG3"] + cin_all, writes=[("ps", bz)])
            P.add("dve", tt(YA[:, cc, 2049:2064], SBB[:, 2049:2064], ps[bz][:, 0:15], ALU.mult), reads=[("SBB", 4), ("ps", bz)], writes=[("YA", cc, 4)])
            P.add("dve", tt(YA[:, cc, 2064:2079], SBB[:, 2064:2079], ps[bz][:, 15:30], ALU.mult), reads=[("SBB", 4), ("ps", bz), ("YA", cc, 4)], writes=[("YA", cc, 4)])
            P.add("dve", tt(YA[:, cc, CTX0:TT], SBB[:, CTX0:TT], ps[bz][:, 30:94], ALU.mult), reads=[("SBB", 4), ("ps", bz), ("YA", cc, 4)], writes=[("YA", cc, 4)])
        s_o = ring_load([(lambda sl: rslot(sl, [4, 1024]), e_w_out[0:512, :].rearrange("(c p) d -> p c d", p=128))])
        rvo = rslot(s_o, [4, 1024])
        for ti, (s, e) in enumerate(TILES):
            n = e - s
            for dc in range(8):
                bo = P.bank()
                for cc in range(4):
                    P.add("pe", mm(ps[bo][:, 0:n], rvo[:, cc, dc * 128:(dc + 1) * 128], YA[:, cc, s:e], cc == 0, cc == 3), reads=rkeys(s_o) + [("YA", cc, ti)], writes=[("ps", bo)])
                h_update(bo, ti, dc, 1, s)
        if stop == "conv0":
            return
        P.barrier()
        areset(pers_end)
        CKVf = aalloc("CKV", [128, NKEYS], BF16)[0]
        KT = aalloc("KT", [128, NKEYS], BF16)[0]
        V = aalloc("V", [128, NT, 65], BF16)[0]
        Q = [aalloc(f"Q{i}", [128, TT], BF16)[0] for i in range(2)]
        E = [aalloc(f"E{i}", [128, 2, 512], BF16)[0] for i in range(4)]
        O = [aalloc(f"O{i}", [128, TT], BF16)[0] for i in range(2)]
        REC = [aalloc(f"REC{i}", [128, 512], F32)[0] for i in range(1)]
        OU = [aalloc(f"OU{i}", [128, 512], F32)[0] for i in range(2)]
        TS2 = [aalloc(f"TSB{i}", [128, 512], F32)[0] for i in range(1)]
        WQB = aalloc("WQB", [128, 2, 768], BF16)[0]
        WQBS = aalloc("WQBS", [128, 2, 8, 96], BF16)[0]
        WKVB = aalloc("WKVB", [128, 1024], BF16)[0]
        WO = [aalloc(f"WO{i}", [128, 1024], BF16)[0] for i in range(2)]
        kview = kvg0.ap().rearrange("(r q) t -> q r t", q=160)
        P.add("sp", dma(CKVf[:, 0:4 * T].rearrange("p (r t) -> p r t", r=4), kview[0:128, :, 0:T]), reads=["kvg0"], writes=[("CKV", 0)], kind="d")
        P.add("sp", dma(CKVf[:, 4 * T:NKEYS].rearrange("p (r t) -> p r t", r=4), kview[0:128, :, T:NKR]), reads=["kvg0"], writes=[("CKV", 1)], kind="d")
        P.add("sp", dma(KT[64:96, 0:4 * T].rearrange("p (r t) -> p r t", r=4), kview[128:160, :, 0:T]), reads=["kvg0"], writes=[("KTR", 0)], kind="d")
        P.add("sp", dma(KT[64:96, 4 * T:NKEYS].rearrange("p (r t) -> p r t", r=4), kview[128:160, :, T:NKR]), reads=["kvg0"], writes=[("KTR", 1)], kind="d")
        for i in range(2):
            P.add("dve", (lambda t: (lambda e: e.memset(t[:], 0.0)))(WO[i]), writes=[("WO", i)])
        P.add("pool", dma(WQB[:], e_wqb.rearrange("(c p) f -> p c f", p=128)), writes=["WQB"], kind="d")
        P.add("pool", dma(WQBS[:], e_wqb_swp.rearrange("(c p) h f -> p c h f", p=128)), writes=["WQBS"], kind="d")
        P.add("pool", dma(WKVB[:], e_wkvb), writes=["WKVB"], kind="d")
        P.add("dve", lambda e: e.memset(V[:, :, 64:65], 1.0), writes=["VONE"])
        for i in range(2):
            P.add("dve", (lambda t: (lambda e: e.memset(t[:], 0.0)))(O[i]), writes=[("O", i, t_) for t_ in range(5)])
        P.add("dve", lambda e: e.memset(REC[0][:], 0.0), writes=[("REC", 0)])
        attention(8, 96, 96 ** -0.5,
                  kgen=lambda h: kgen0(h, KT, CKVf, WKVB), vgen=lambda h: vgen0(h, V, CKVf, WKVB),
                  qgen=lambda h, qb: qgen0(h, qb, Q[qb], QAN, WQB, WQBS, TAB0, TS2),
                  kt_of=lambda h: (KT, 0, 96), v_of=lambda h, j: V[:, j, :], q_of=lambda h, qb, s, e: Q[qb][0:96, s:e],
                  qkeys=lambda h, qb, ti: [("Q", qb, ti)], kkeys=lambda ks, nk: [("KT", ks // 512), ("KT", (ks + nk - 1) // 512)],
                  vkeys=lambda j: [("V", j // 8)],
                  E=E, O=O, REC=REC, OU=OU, WO=WO, wo_src=lambda h: e_w_out[512 + h * 64:512 + (h + 1) * 64, :],
                  qtiles=[(0, 512, 0, False), (512, 1024, 1, False), (1024, 1536, 2, False), (1536, 2048, 3, False), (2048, LAT_END, 4, False), (CTX0, TT, 4, True)],
                  out_tiles=range(5), extra_reads=[("KTR", 0), ("KTR", 1), "VONE"])

    def kgen0(h, KT, CKVf, WKVB):
        for g in range(17):
            ks, ke = g * 512, min(NKEYS, (g + 1) * 512)
            n = ke - ks
            b = P.bank()
            P.add("pe", mm(ps[b][:, 0:n], WKVB[:, h * 128:(h + 1) * 128], CKVf[:, ks:ke]), reads=["WKVB", ("CKV", 0), ("CKV", 1)], writes=[("ps", b)])
            if g % 3 == 0:
                P.add("act", act(KT[0:64, ks:ke], ps[b][0:64, 0:n], AF.Copy), reads=[("ps", b)], writes=[("KT", g)])
            else:
                P.add("dve", cp(KT[0:64, ks:ke], ps[b][0:64, 0:n]), reads=[("ps", b)], writes=[("KT", g)])

    def vgen0(h, V, CKVf, WKVB):
        for j0 in range(0, NT, 8):
            cnt = min(8, NT - j0)
            b = P.bank()
            for j in range(j0, j0 + cnt):
                ks, nk, _ = KEYT[j]
                P.add("pe", mm(ps[b][0:nk, (j - j0) * 64:(j - j0 + 1) * 64], CKVf[:, ks:ks + nk], WKVB[:, h * 128 + 64:(h + 1) * 128]),
                      reads=["WKVB", ("CKV", 0), ("CKV", 1)], writes=[("ps", b)])
            src = ps[b][:, 0:cnt * 64].rearrange("p (j f) -> p j f", f=64)
            if (j0 // 8) % 2 == 0:
                P.add("dve", cp(V[:, j0:j0 + cnt, 0:64], src), reads=[("ps", b)], writes=[("V", j0 // 8)])
            else:
                P.add("act", act(V[:, j0:j0 + cnt, 0:64], src, AF.Copy), reads=[("ps", b)], writes=[("V", j0 // 8)])

    def qgen0(h, qb, Qb, QAN, WQB, WQBS, TAB0, TS2):
        for ti, (s, e) in enumerate(TILES):
            n = e - s
            b0, b1 = P.bank(), P.bank()
            for c in range(2):
                P.add("pe", mm(ps[b0][0:96, 0:n], WQB[:, c, h * 96:(h + 1) * 96], QAN[:, c, s:e], c == 0, c == 1), reads=["WQB", ("QAN", ti)], writes=[("ps", b0)])
            for c in range(2):
                P.add("pe", mm(ps[b1][0:96, 0:n], WQBS[:, c, h, :], QAN[:, c, s:e], c == 0, c == 1), reads=["WQBS", ("QAN", ti)], writes=[("ps", b1)])
            qk = ("Q", qb, ti)
            P.add("act", act(Qb[0:64, s:e], ps[b0][0:64, 0:n], AF.Copy), reads=[("ps", b0)], writes=[qk])
            P.add("dve", tt(TS2[0][64:96, 0:n], ps[b0][64:96, 0:n], TAB0[64:96, 0, s:e], ALU.mult), reads=[("ps", b0), "TAB0"], writes=[("TSB", 0)])
            P.add("dve", tt(Qb[64:96, s:e], ps[b1][64:96, 0:n], TAB0[64:96, 1, s:e], ALU.mult), reads=[("ps", b1), "TAB0", qk], writes=[qk])
            P.add("dve", tt(Qb[64:96, s:e], TS2[0][64:96, 0:n], Qb[64:96, s:e], ALU.add), reads=[("TSB", 0), qk], writes=[qk])


    def qgen1(h, qb, QP, Q1):
        g = h // 4
        if h == 4 or h == 5:
            P.add("dve", (lambda t: (lambda e: e.memset(t[0:64, :], 0.0)))(QP[qb]), reads=[("QP", qb)], writes=[("QP", qb)])
        src = Q1[g * 64:(g + 1) * 64, h % 4, :]
        dst = QP[qb][g * 64:(g + 1) * 64, :]
        P.add("dve", cp(dst, src), reads=[("Q1", h % 4, t_) for t_ in range(4)] + [("QP", qb)], writes=[("QP", qb)])

    def attention(nheads, kdim, scale, kgen, vgen, qgen, kt_of, v_of, q_of, qkeys, kkeys, vkeys, E, O, REC, OU, WO, wo_src, qtiles, out_tiles, extra_reads):
        ek = 0
        npair = 0
        NE = len(E)
        LA = max(1, NE - 2)
        NSP = LA + 1
        ob = 0
        pend_fin = []
        pend_out = {}
        nfin = [0]

        pend_fin2 = []

        def finalize(h, qb, s, e, ti):
            nq = e - s

            def run():
                ou = OU[nfin[0] % len(OU)]
                ouk = ("OU", nfin[0] % len(OU))
                rc = REC[nfin[0] % len(REC)]
                rck = ("REC", nfin[0] % len(REC))
                nfin[0] += 1
                P.add("act", act(ou[0:65, 0:nq], ps[ob][0:65, 0:nq], AF.Copy), reads=[("ps", ob)], writes=[ouk])
                P.add("dve", lambda e_: e_.reciprocal(out=rc[64:65, 0:nq], in_=ou[64:65, 0:nq]), reads=[ouk], writes=[rck])

                def run2():
                    bb = P.bank()
                    P.add("pe", mm(ps[bb][:, 0:nq], ONES32[:], rc[:, 0:nq]), reads=[rck, "ONES32"], writes=[("ps", bb)])
                    P.add("dve", tt(O[qb][0:64, s:e], ou[0:64, 0:nq], ps[bb][0:64, 0:nq], ALU.mult), reads=[ouk, ("ps", bb)], writes=[("O", qb, ti)])
                pend_fin2.append(run2)
            return run

        def outproj(qb, ti):
            s, e = TILES[ti]
            n = e - s

            def one(dc):
                def run():
                    bo = P.bank()
                    P.add("pe", mm(ps[bo][:, 0:n], WO[qb][:, dc * 128:(dc + 1) * 128], O[qb][:, s:e]), reads=[("WO", qb), ("O", qb, ti)], writes=[("ps", bo)])
                    h_update(bo, ti, dc, 1, s)
                return run
            return [one(dc) for dc in range(8)]

        for h in range(nheads):
            qb = h % 2
            P.bank_lo, P.bank_n = 2, 6
            if kgen is not None:
                kgen(h)
                vgen(h)
            qgen(h, qb)
            P.bank_lo, P.bank_n = 1, 1
            P.add("pool", dma(WO[qb][0:64, :], wo_src(h)), writes=[("WO", qb)], kind="d")
            KTt, kp0, kp1 = kt_of(h)
            for qi, (s, e, ti, ctx_only) in enumerate(qtiles):
                nq = e - s
                keys = [(j, kt) for j, kt in enumerate(KEYT) if (kt[2] or not ctx_only)]
                pairs = [keys[i_:i_ + 2] for i_ in range(0, len(keys), 2)]
                slots = {}
                nk_tot = len(keys)
                done = 0
                for pi in range(len(pairs) + LA):
                    if pi < len(pairs):
                        pr = pairs[pi]
                        pb = 2 + 2 * (npair % NSP)
                        npair += 1
                        eb = ek % NE
                        ek += 1
                        rows = max(kt[1] for (_, kt) in pr)
                        for t_, (j, (ks, nk, _)) in enumerate(pr):
                            P.add("pe", mm(ps[pb + t_][0:nk, 0:nq], KTt[kp0:kp1, ks:ks + nk], q_of(h, qb, s, e)),
                                  reads=kkeys(ks, nk) + qkeys(h, qb, ti) + extra_reads, writes=[("ps", pb + t_)])
                        np_ = len(pr)
                        src = PSALL[0:rows, pb * 512:(pb + np_) * 512].rearrange("p (t c) -> p t c", t=np_)[:, :, 0:nq]
                        P.add("act", act(E[eb][0:rows, 0:np_, 0:nq], src, AF.Exp, scale=scale),
                              reads=[("ps", pb + t_) for t_ in range(np_)], writes=[("E", eb)])
                        slots[pi] = eb
                    if pi >= LA:
                        pr = pairs[pi - LA]
                        eb = slots[pi - LA]
                        for t_, (j, (ks, nk, _)) in enumerate(pr):
                            P.add("pe", mm(ps[ob][0:65, 0:nq], v_of(h, j)[0:nk, :], E[eb][0:nk, t_, 0:nq], done == 0, done == nk_tot - 1),
                                  reads=[("E", eb)] + vkeys(j) + extra_reads, writes=[("ps", ob)])
                            done += 1
                    last = pi == len(pairs) + LA - 1
                    if pi == 0 or last:
                        while pend_fin:
                            pend_fin.pop(0)()
                    if pi == 5 or last:
                        while pend_fin2:
                            pend_fin2.pop(0)()
                    fl = pend_out.get(qi)
                    if fl and (pi >= 6 and pi % 3 == 0 or last):
                        fl.pop(0)()
                        if last:
                            while fl:
                                fl.pop(0)()
                pend_fin.append(finalize(h, qb, s, e, ti))
            for k_ in sorted(pend_out):
                for f in pend_out.pop(k_):
                    f()
            for qi, ti in enumerate(out_tiles):
                pend_out[qi] = outproj(qb, ti)
        P.bank_lo, P.bank_n = 2, 6
        while pend_fin:
            pend_fin.pop(0)()
        while pend_fin2:
            pend_fin2.pop(0)()
        for k_ in sorted(pend_out):
            for f in pend_out.pop(k_):
                f()

    mixer0()
    if stop in ("conv0", "mix0"):
        dump_H()
        return nc, P
    sched1 = [[("piece", 1, 3 * gi + t_) for t_ in range(3)] for gi in range(6)]
    sched1[1].append(("derive", 1, 0))
    sched1[3].append(("derive", 1, 1))
    sched1[5].append(("derive", 1, 2))
    ffn_block(0, 1, 2, sched=sched1)
    if stop == "l0":
        dump_H()
        return nc, P

    def mixer1():
        areset()
        A, _, a_end = aalloc("A1", [128, 8, TT], BF16)
        tmp = adaln_tmp(at=ABASE + 55616)
        adaln(lambda c, s, e: A[:, c, s:e], "A", lambda g, c: GS[g][:, 1, c:c + 1], lambda g, c: MOD[g][:, 24 + c:25 + c], range(5), tmp)
        P.barrier()
        areset(a_end)
        owv = o_w_in.rearrange("(c p) f -> p c f", p=128)
        Q1, _, q1_end = aalloc("Q1", [128, 4, T], BF16)
        TAB1 = aalloc("TAB1", [128, 2, TT], BF16)[0]
        K1O = aalloc("K1O", [128, NKR], BF16)[0]
        VOWN = aalloc("VOWN", [128, 17, 128], BF16)[0]
        TS = [aalloc(f"TS2{i}", [128, 512], F32)[0] for i in range(4)]
        SQ = [aalloc(f"SQB{i}", [128, 512], BF16)[0] for i in range(2)]
        LTb = [aalloc(f"LTB{i}", [128, 512], F32)[0] for i in range(2)]
        RSb = [aalloc(f"RSB{i}", [128, 512], F32)[0] for i in range(2)]
        P.add("sp", dma(TAB1[:], tab1_d), writes=["TAB1"], kind="d")
        wqv = o_wq.rearrange("(c p) w f -> p c w f", p=128)
        wkv = o_wk.rearrange("(c p) w f -> p c w f", p=128)
        it = 0
        work = [("q", jq) for jq in range(4)] + [("k", 0)]
        for (kind, jq) in work:
            src = wqv[:, :, :, jq * 128:(jq + 1) * 128] if kind == "q" else wkv
            s_w = ring_load([(lambda sl, w_=w_: rslot(sl, [8, 2, 128])[:, :, w_, :], src[:, :, w_, :]) for w_ in range(2)])
            rv = rslot(s_w, [8, 2, 128])
            gn, gns = ("OQN", "OQNS") if kind == "q" else ("OKN", "OKNS")
            cols = [(s, e, ti, s) for ti, (s, e) in enumerate(TILES[:4])]
            if kind == "k":
                cols.append((CTX0, TT, 4, T))
            for (s, e, ti, dcol) in cols:
                n = e - s
                b2 = it % 2
                it += 1
                bb = [P.bank(), P.bank()]
                for w_ in range(2):
                    for d in range(8):
                        P.add("pe", mm(ps[bb[w_]][:, 0:n], rv[:, d, w_, :], A[:, d, s:e], d == 0, d == 7), reads=rkeys(s_w) + [("A", d, ti)], writes=[("ps", bb[w_])])
                P.add("act", act(SQ[b2][:, 0:n], ps[bb[0]][:, 0:n], AF.Square), reads=[("ps", bb[0])], writes=[("SQB", b2)])
                bs = P.bank()
                P.add("pe", mm(ps[bs][:, 0:n], BD[:], SQ[b2][:, 0:n]), reads=[("SQB", b2), "BD"], writes=[("ps", bs)])
                P.add("act", act(LTb[b2][:, 0:n], ps[bs][:, 0:n], AF.Ln, scale=1.0 / 64, bias=C("EPSV")), reads=[("ps", bs)], writes=[("LTB", b2)])
                P.add("act", act(RSb[b2][:, 0:n], LTb[b2][:, 0:n], AF.Exp, scale=-0.5), reads=[("LTB", b2)], writes=[("RSB", b2)])
                t1, t2 = TS[(2 * it) % 4], TS[(2 * it + 1) % 4]
                k1, k2 = ("TS", (2 * it) % 4), ("TS", (2 * it + 1) % 4)
                P.add("dve", stt(t1[:, 0:n], ps[bb[0]][:, 0:n], C(gn), TAB1[:, 0, s:e], ALU.mult, ALU.mult), reads=[("ps", bb[0]), "TAB1", "CONST"], writes=[k1])
                P.add("dve", stt(t2[:, 0:n], ps[bb[1]][:, 0:n], C(gns), TAB1[:, 1, s:e], ALU.mult, ALU.mult), reads=[("ps", bb[1]), "TAB1", "CONST"], writes=[k2])
                P.add("dve", tt(t1[:, 0:n], t1[:, 0:n], t2[:, 0:n], ALU.add), reads=[k1, k2], writes=[k1])
                if kind == "q":
                    P.add("dve", tt(Q1[:, jq, s:e], t1[:, 0:n], RSb[b2][:, 0:n], ALU.mult), reads=[k1, ("RSB", b2)], writes=[("Q1", jq, ti)])
                else:
                    P.add("dve", tt(K1O[:, dcol:dcol + n], t1[:, 0:n], RSb[b2][:, 0:n], ALU.mult), reads=[k1, ("RSB", b2)], writes=[("K1O", ti)])
        if stop in ("pb_q", "pb_only"):
            return
        s_v = ring_load([(lambda sl: rslot(sl, [8, 128]), owv[:, :, 640:768])])
        rvv = rslot(s_v, [8, 128])
        for t0_ in range(0, 17, 4):
            cnt = min(4, 17 - t0_)
            b = P.bank()
            for tq in range(t0_, t0_ + cnt):
                cs, nk = (tq * 128, 128) if tq < 16 else (CTX0, 64)
                ti = min(tq // 4, 4)
                for d in range(8):
                    P.add("pe", mm(ps[b][0:nk, (tq - t0_) * 128:(tq - t0_ + 1) * 128], A[:, d, cs:cs + nk], rvv[:, d, :], d == 0, d == 7),
                          reads=rkeys(s_v) + [("A", d, ti)], writes=[("ps", b)])
            P.add("act", act(VOWN[:, t0_:t0_ + cnt, :], ps[b][:, 0:cnt * 128].rearrange("p (j f) -> p j f", f=128), AF.Copy), reads=[("ps", b)], writes=[("VOWN", t0_ // 4)])
        if stop == "pb_v":
            return
        P.add("sp", dma(kb1[:, :], K1O[:]), reads=[("K1O", t_) for t_ in range(5)], writes=["kb1"], kind="d")
        P.add("sp", dma(vb1[:, :], VOWN[:].rearrange("p t f -> p (t f)")), reads=[("VOWN", t_) for t_ in range(5)], writes=[("vb1", 0), ("vb1", 1)], kind="d")
        if stop == "qkv1":
            return
        def u_load(cc):
            return ring_load([(lambda sl, k=k: rslot(sl, [8, 2, 128])[:, :, k, :], owv[:, :, 768 + k * 512 + cc * 128:768 + k * 512 + (cc + 1) * 128]) for k in range(2)])
        u_slots = {cc_: u_load(cc_) for cc_ in (0, 1)}
        P.add("pool", lambda e: e.collective_compute("AllGather", ALU.bypass, replica_groups=RG, ins=[kb1.ap().opt()], outs=[kg1.ap().opt()]),
              reads=["kb1"], writes=["kg1"], kind="cc")
        P.add("pool", lambda e: e.collective_compute("AllGather", ALU.bypass, replica_groups=RG, ins=[vb1.ap().opt()], outs=[vg1.ap().opt()]),
              reads=[("vb1", 0), ("vb1", 1)], writes=["vg1"], kind="cc")
        P.barrier()
        areset(q1_end)
        CIN1 = [aalloc(f"CIN1{i}", [128, 2080], BF16)[0] for i in range(2)]
        Z = aalloc("Z", [128, 4, T], BF16)[0]
        DG = [aalloc(f"DG{i}", [128, 31, 128], BF16)[0] for i in range(2)]
        TS = [aalloc(f"TS1{i}", [128, 512], F32)[0] for i in range(2)]
        ZSQ = aalloc("ZSQ", [128, 4, 512], BF16)[0]
        MEAN = aalloc("MEAN", [128, 512], F32)[0]
        VAR = aalloc("VAR", [128, 512], F32)[0]
        RS = aalloc("RS1", [128, 512], F32)[0]
        owv = o_w_in.rearrange("(c p) f -> p c f", p=128)
        tsk = [0]

        def uproj(cc):
            cb = cc % 2
            s_u = u_slots.pop(cc) if cc in u_slots else u_load(cc)
            rv = rslot(s_u, [8, 2, 128])
            for ti, (s, e) in enumerate(TILES):
                if ti == 4:
                    e = LAT_END
                n = e - s
                bb = [P.bank(), P.bank()]
                for k in range(2):
                    for d in range(8):
                        P.add("pe", mm(ps[bb[k]][:, 0:n], rv[:, d, k, :], A[:, d, s:e], d == 0, d == 7), reads=[("ring", s_u, k), ("A", d, ti)], writes=[("ps", bb[k])])
                ta = TS[tsk[0] % 2]
                ka = ("TS", tsk[0] % 2)
                tsk[0] += 1
                P.add("act", act(ta[:, 0:n], ps[bb[1]][:, 0:n], AF.Sigmoid), reads=[("ps", bb[1])], writes=[ka])
                if ti < 4:
                    pieces = [(CIN1[cb][:, 16 + s:16 + e], 0, n, None)]
                else:
                    pieces = [(CIN1[cb][:, 0:16], 0, 16, C("MKL", 16, 0)), (CIN1[cb][:, 2064:2080], 16, 32, C("MKL", 16, 16))]
                for (dst, a0, a1, mk) in pieces:
                    P.add("dve", tt(dst, ta[:, a0:a1], ps[bb[0]][:, a0:a1], ALU.mult), reads=[ka, ("ps", bb[0])], writes=[("CIN1", cb, ti)])
                    if mk is not None:
                        P.add("dve", tt(dst, dst, mk, ALU.mult), reads=[("CIN1", cb, ti), "CONST"], writes=[("CIN1", cb, ti)])

        def dgbuild(cc):
            cb = cc % 2
            for k in range(31):
                P.add("dve", ts(DG[cb][:, k, :], IDENT[:], C("OCW", 1, cc * 31 + k), ALU.mult), reads=["IDENT", "CONST"], writes=[("DG", cb)])

        def conv(cc):
            cb = cc % 2
            cin_all = [("CIN1", cb, t_) for t_ in range(5)]
            for ti in range(4):
                s, e = TILES[ti]
                bz = P.bank()
                for k in range(31):
                    P.add("pe", mm(ps[bz][:, 0:512], DG[cb][:, k, :], CIN1[cb][:, 1 + k + s:1 + k + s + 512], k == 0, k == 30), reads=[("DG", cb)] + cin_all, writes=[("ps", bz)])
                P.add("act", act(Z[:, cc, s:e], ps[bz][:, 0:512], AF.Identity, bias=C("OCB", 1, cc)), reads=[("ps", bz), "CONST"], writes=[("Z", cc, ti)])

        dgbuild(0)
        uproj(0)
        for cc in range(4):
            if cc + 1 < 4:
                dgbuild(cc + 1)
                uproj(cc + 1)
            conv(cc)
        s_o = ring_load([(lambda sl: rslot(sl, [4, 1024]), o_w_out[512:1024, :].rearrange("(c p) d -> p c d", p=128))])
        rvo = rslot(s_o, [4, 1024])
        lnb = {}

        def ln1(ti):
            s, e = TILES[ti]
            n = 512
            for cc in range(4):
                P.add("act", act(ZSQ[:, cc, :], Z[:, cc, s:e], AF.Square), reads=[("Z", cc, ti)], writes=[("ZSQ", cc)])
            bm, bv = (0, 1) if ti % 2 == 0 else (2, 3)
            lnb[ti] = (bm, bv)
            for cc in range(4):
                P.add("pe", mm(ps[bm][:, 0:n], ONES[:], Z[:, cc, s:e], cc == 0, cc == 3), reads=[("Z", cc, ti), "ONES"], writes=[("ps", bm)])
            for cc in range(4):
                P.add("pe", mm(ps[bv][:, 0:n], ONES[:], ZSQ[:, cc, :], cc == 0, cc == 3), reads=[("ZSQ", cc), "ONES"], writes=[("ps", bv)])

        def ln2(ti):
            n = 512
            bm, bv = lnb[ti]
            P.add("act", act(MEAN[:, 0:n], ps[bm][:, 0:n], AF.Copy, scale=1.0 / 512), reads=[("ps", bm)], writes=["MEAN"])
            P.add("dve", tt(VAR[:, 0:n], MEAN[:, 0:n], MEAN[:, 0:n], ALU.mult), reads=["MEAN"], writes=["VAR"])
            P.add("dve", stt(VAR[:, 0:n], ps[bv][:, 0:n], 1.0 / 512, VAR[:, 0:n], ALU.mult, ALU.subtract), reads=[("ps", bv), "VAR"], writes=["VAR"])
            P.add("act", act(VAR[:, 0:n], VAR[:, 0:n], AF.Ln, bias=C("EPSV")), reads=["VAR"], writes=["VAR"])
            P.add("act", act(RS[:, 0:n], VAR[:, 0:n], AF.Exp, scale=-0.5), reads=["VAR"], writes=["RS1"])

        def ln3(ti):
            s, e = TILES[ti]
            n = 512
            for cc in range(4):
                ta = TS[tsk[0] % 2]
                ka = ("TS", tsk[0] % 2)
                tsk[0] += 1
                P.add("dve", tt(ta[:, 0:n], Z[:, cc, s:e], MEAN[:, 0:n], ALU.subtract), reads=[("Z", cc, ti), "MEAN"], writes=[ka])
                P.add("dve", tt(ta[:, 0:n], ta[:, 0:n], RS[:, 0:n], ALU.mult), reads=[ka, "RS1"], writes=[ka])
                P.add("act", act(Z[:, cc, s:e], ta[:, 0:n], AF.Silu, scale=C("OLG", 1, cc), bias=C("OLB", 1, cc)), reads=[ka, "CONST"], writes=[("Z", cc, ti)])
            for dc in range(8):
                bo = P.bank()
                for cc in range(4):
                    P.add("pe", mm(ps[bo][:, 0:n], rvo[:, cc, dc * 128:(dc + 1) * 128], Z[:, cc, s:e], cc == 0, cc == 3), reads=rkeys(s_o) + [("Z", cc, ti)], writes=[("ps", bo)])
                h_update(bo, ti, dc, 1, s)

        P.bank_lo, P.bank_n = 4, 4
        ln1(0)
        for ti in range(4):
            if ti + 1 < 4:
                ln1(ti + 1)
            ln2(ti)
            ln3(ti)
        P.bank_lo, P.bank_n = 2, 6
        P.barrier()
        areset(q1_end)
        K1f = aalloc("K1", [128, NKEYS], BF16)[0]
        V1 = aalloc("V1", [128, NT, 2, 65], BF16)[0]
        QP = [aalloc(f"QP{i}", [128, T], BF16)[0] for i in range(2)]
        O = [aalloc(f"O1{i}", [128, T], BF16)[0] for i in range(2)]
        WO = [aalloc(f"WO1{i}", [128, 1024], BF16)[0] for i in range(2)]
        c2 = [ABASE]

        def a2(name, shape, dt):
            t_, _, end = aalloc(name, shape, dt, at=c2[0])
            c2[0] = end
            assert end <= a_end
            return t_
        VST = a2("VST", [128, NT, 128], BF16)
        E = [a2(f"E1{i}", [128, 2, 512], BF16) for i in range(4)]
        REC = [a2(f"REC1{i}", [128, 512], F32) for i in range(2)]
        OU = [a2(f"OU1{i}", [128, 512], F32) for i in range(2)]
        kv1 = kg1.ap().rearrange("(r q) t -> q r t", q=128)
        P.add("sp", dma(K1f[:, 0:4 * T].rearrange("p (r t) -> p r t", r=4), kv1[:, :, 0:T]), reads=["kg1"], writes=[("K1", 0)], kind="d")
        P.add("sp", dma(K1f[:, 4 * T:NKEYS].rearrange("p (r t) -> p r t", r=4), kv1[:, :, T:NKR]), reads=["kg1"], writes=[("K1", 1)], kind="d")
        vv1 = vg1.ap().rearrange("(r p) x -> p r x", p=128)
        P.add("sp", dma(VST[:, 0:64, :].rearrange("p (r t) f -> p r (t f)", r=4), vv1[:, :, 0:16 * 128]), reads=["vg1"], writes=[("VST", 0)], kind="d")
        for r in range(4):
            P.add("sp", dma(VST[(r % 2) * 64:(r % 2) * 64 + 64, 64 + r // 2, :], vg1[r * 128:r * 128 + 64, 16 * 128:17 * 128]), reads=["vg1"], writes=[("VST", 1 + r)], kind="d")
        for i in range(2):
            P.add("dve", (lambda t: (lambda e: e.memset(t[:], 0.0)))(WO[i]), writes=[("WO", i)])
            P.add("dve", (lambda t: (lambda e: e.memset(t[:], 0.0)))(O[i]), writes=[("O", i, t_) for t_ in range(5)])
            P.add("dve", (lambda t: (lambda e: e.memset(t[:], 0.0)))(QP[i]), writes=[("QP", i)])
            P.add("dve", (lambda t: (lambda e: e.memset(t[:], 0.0)))(REC[i]), writes=[("REC", i)])
        vst_all = [("VST", i) for i in range(5)]
        P.add("dve", cp(V1[:, :, 0, 0:64], VST[:, :, 0:64]), reads=vst_all, writes=[("V1", 0)])
        P.add("act", act(V1[:, :, 1, 0:64], VST[:, :, 64:128], AF.Copy), reads=vst_all, writes=[("V1", 1)])
        if stop == "gath1":
            return
        P.add("dve", lambda e: e.memset(V1[:, :, :, 64:65], 1.0), writes=["VONE1"])
        attention(8, 64, 0.125, None, None, lambda h, qb: qgen1(h, qb, QP, Q1),
                  kt_of=lambda h: (K1f, 0, 128), v_of=lambda h, j: V1[:, j, h // 4, :],
                  q_of=lambda h, qb, s, e: QP[qb][:, s:e],
                  qkeys=lambda h, qb, ti: [("QP", qb)], kkeys=lambda ks, nk: [("K1", 0), ("K1", 1)],
                  vkeys=lambda j: [("V1", 0), ("V1", 1)],
                  E=E, O=O, REC=REC, OU=OU, WO=WO, wo_src=lambda h: o_w_out[h * 64:(h + 1) * 64, :],
                  qtiles=[(s, e, ti, False) for ti, (s, e) in enumerate(TILES[:4])], out_tiles=range(4), extra_reads=["VONE1"])

    use_layer(1)
    ffn_block(1, 0, 0)
    if stop == "ffn1_1":
        dump_H()
        return nc, P
    mixer1()
    if stop in ("conf1", "mix1", "qkv1", "gath1", "pb_q", "pb_v", "pb_only"):
        dump_H()
        return nc, P
    ffn_block(1, 1, 2, tiles=range(4))
    areset()
    OUTB = [aalloc(f"OUTB{i}", [128, 8, 512], F32)[0] for i in range(2)]
    tmp = adaln_tmp(at=ABASE + 55616)
    a_keys = [("A", c_, t_) for c_ in range(8) for t_ in range(5)]
    def store_tile(ti):
        s, e = TILES[ti]
        P.add("sp", dma(out_d[:, :, s:e], OUTB[ti % 2][:]), reads=[("OUTB", c, ti % 2) for c in range(8)], writes=[("out", ti)], kind="d")
    adaln(lambda c, s, e: OUTB[(s // 512) % 2][:, c, 0:e - s], lambda c, ti: [("OUTB", c, ti % 2)] + a_keys, lambda g, c: C("FG", 1, c), lambda g, c: FZERO[:, 0:1], range(4), tmp, post=store_tile)
    P.add("sp", None, reads=[("out", ti) for ti in range(4)], kind="w")
    return nc, P


NDS = 16


def emit(nc, P):
    from contextlib import ExitStack
    for q, ops in P.q.items():
        for op in ops:
            for d in op.deps:
                if d.kind == "c" and d.eng == "pe" and q == "pe":
                    continue
                d.needed = True
    idx = {}
    ncc = 0
    for q, ops in P.q.items():
        nd = 0
        for op in ops:
            if op.kind == "c":
                if op.needed:
                    idx[q] = idx.get(q, 0) + 1
                    op.sem = q
                    op.val = idx[q]
            elif op.kind == "d":
                op.sem = (q, nd % NDS)
                op.val = 16 * (nd // NDS + 1)
                nd += 1
            elif op.kind == "cc":
                op.sem = ("cc", ncc)
                op.val = 1
                ncc += 1
    with ExitStack() as st:
        semobj = {}
        for q in ("pe", "act", "dve", "pool"):
            semobj[q] = st.enter_context(nc.semaphore("s_" + q))
        for q in ("pool", "sp"):
            for i in range(NDS):
                semobj[(q, i)] = st.enter_context(nc.semaphore(f"d_{q}{i}"))
        for i in range(ncc):
            semobj[("cc", i)] = st.enter_context(nc.semaphore(f"cc{i}"))
        block = st.enter_context(nc.Block())

        def run(q):
            def body(e):
                known = {}
                for op in P.q[q]:
                    waits = {}
                    for d in op.deps:
                        if d.kind == "c" and d.eng == "pe" and q == "pe":
                            continue
                        if d.sem is None:
                            continue
                        if waits.get(d.sem, 0) < d.val:
                            waits[d.sem] = d.val
                    if op.kind == "d" and op.val > 16:
                        if waits.get(op.sem, 0) < op.val - 16:
                            waits[op.sem] = op.val - 16
                    for s_, v in waits.items():
                        if known.get(s_, 0) < v:
                            e.wait_ge(semobj[s_], v)
                            known[s_] = v
                    if op.kind == "w":
                        continue
                    ins = op.fn(e)
                    if op.kind == "c":
                        if op.needed:
                            ins.then_inc(semobj[op.sem], 1)
                    elif op.kind == "d":
                        ins.then_inc(semobj[op.sem], 16)
                    else:
                        ins.then_inc(semobj[op.sem])
            return body

        block.tensor(run("pe"))
        block.scalar(run("act"))
        block.vector(run("dve"))
        block.gpsimd(run("pool"))
        block.sync(run("sp"))
    return nc


def _fm(v):
    v = np.asarray(v, np.float32)
    return v.reshape(-1, 128).T.copy()


def _rope_tab(rot_dim, pos_valid, rows, cols):
    nf = rot_dim // 4
    inv = (np.float32(10000.0) ** (-np.arange(nf, dtype=np.float32) / np.float32(nf))).astype(np.float32)
    ang = np.concatenate([rows[:, None].astype(np.float32) * inv, cols[:, None].astype(np.float32) * inv], axis=-1)
    cos = np.cos(ang).astype(np.float32)
    sin = np.sin(ang).astype(np.float32)
    cos = np.where(pos_valid[:, None], cos, 1.0)
    sin = np.where(pos_valid[:, None], sin, 0.0)
    cf = np.repeat(cos, 2, axis=1)
    sf = np.repeat(sin, 2, axis=1)
    sf[:, 0::2] *= -1.0
    return cf.T, sf.T


def _swap_pairs(w):
    idx = np.arange(w.shape[-1]) ^ 1
    return w[..., idx]


def host_prep(inp):
    g = {k: np.asarray(v) for k, v in inp.items()}
    x, ctx, c, c_ctx = g["x"], g["ctx"], g["c"], g["c_ctx"]
    shared = {}
    shared["w_mod"] = g["w_mod"]
    shared["w_ffn_in"] = g["w_ffn_in"]
    shared["w_ffn_out"] = g["w_ffn_out"]
    ewi = g["e_w_in"][0]
    shared["e_w_in"] = ewi
    kr = ewi[:, 1920:1952]
    pad = np.zeros((D, 2, 96), np.float32)
    pad[:, 0, 64:] = kr
    pad[:, 1, 64:] = _swap_pairs(kr)
    shared["e_wkr_pad"] = pad
    wqb = g["e_w_q_b"][0]
    shared["e_wqb"] = wqb
    sw = np.zeros((256, 8, 96), np.float32)
    for h in range(8):
        sw[:, h, 64:] = _swap_pairs(wqb[:, h * 96 + 64:(h + 1) * 96])
    shared["e_wqb_swp"] = sw
    shared["e_wkvb"] = g["e_w_kv_b"][0]
    shared["e_w_out"] = g["e_w_out"][0]
    owi = g["o_w_in"][0]
    shared["o_w_in"] = owi
    order = [0, 4, 1, 5, 2, 6, 3, 7]
    oq = np.zeros((D, 2, 512), np.float32)
    for s_, h in enumerate(order):
        blk = owi[:, h * 64:(h + 1) * 64]
        oq[:, 0, s_ * 64:(s_ + 1) * 64] = blk
        oq[:, 1, s_ * 64:(s_ + 1) * 64] = _swap_pairs(blk)
    shared["o_wq"] = oq
    ok = np.zeros((D, 2, 128), np.float32)
    for h in range(2):
        blk = owi[:, 512 + h * 64:512 + (h + 1) * 64]
        ok[:, 0, h * 64:(h + 1) * 64] = blk
        ok[:, 1, h * 64:(h + 1) * 64] = _swap_pairs(blk)
    shared["o_wk"] = ok
    shared["o_w_out"] = g["o_w_out"][0]
    shared = {k: np.ascontiguousarray(v, dtype=np.float32) for k, v in shared.items()}

    base = np.zeros((128, NCONST), np.float32)

    def put(name, arr):
        arr = np.asarray(arr, np.float32)
        base[:, _off[name]:_off[name] + arr.shape[1]] = arr
    put("BM", np.concatenate([_fm(g["b_mod"][l]) for l in range(2)], axis=1))
    put("NG", np.concatenate([_fm(g["norm_g"][l, j]) for l in range(2) for j in range(3)], axis=1))
    put("FG", _fm(g["final_g"]))
    put("ECW", np.stack([g["e_conv_w"][0][:, cc * 128:(cc + 1) * 128].T for cc in range(4)], axis=1).reshape(128, 12))
    put("EQN", _fm(g["e_q_norm"][0]))
    put("EKN", _fm(g["e_kv_norm"][0]))
    p64 = np.arange(128) % 64
    put("OQN", g["o_q_norm"][0][p64][:, None])
    put("OQNS", g["o_q_norm"][0][p64 ^ 1][:, None])
    put("OKN", g["o_k_norm"][0][p64][:, None])
    put("OKNS", g["o_k_norm"][0][p64 ^ 1][:, None])
    put("OCW", np.stack([g["o_conv_w"][0][:, cc * 128:(cc + 1) * 128].T for cc in range(4)], axis=1).reshape(128, 124))
    put("OCB", _fm(g["o_conv_b"][0]))
    put("OLG", _fm(g["o_ln_g"][0]))
    put("OLB", _fm(g["o_ln_b"][0]))

    maps = []
    for r in range(8):
        b, qd = r // 4, r % 4
        t0, c0 = qd * T, qd * 64
        tok = np.concatenate([np.arange(t0, t0 + T), np.arange(t0 - HALO, t0), np.arange(t0 + T, t0 + T + HALO)])
        valid = (tok >= 0) & (tok < SEQ)
        xs = np.zeros((TT, D), np.float32)
        xs[:LAT_END][valid] = x[b][tok[valid]]
        ctok = np.array([c0 - 1, c0 + 64] + list(range(c0, c0 + 64)))
        cvalid = (ctok >= 0) & (ctok < 256)
        xs[LAT_END:][cvalid] = ctx[b][ctok[cvalid]]
        xTh = np.ascontiguousarray(xs.reshape(TT, 8, 128).transpose(2, 1, 0))
        cs = base.copy()
        cv = np.stack([_fm(c[b]), _fm(c_ctx)], axis=2).reshape(128, 16)
        cs[:, _off["CV"]:_off["CV"] + 16] = cv
        cs[:, _off["MKL"]:_off["MKL"] + 32] = valid[T:].astype(np.float32)[None, :]
        cs[:, _off["MKC"]:_off["MKC"] + 2] = cvalid[:2].astype(np.float32)[None, :]
        tokc = np.clip(tok, 0, SEQ - 1)
        rows, cols = tokc // 64, tokc % 64
        tabs = []
        for rot, rowsel in ((32, lambda p: (p >= 64) & (p < 96)), (64, None)):
            cf, sf = _rope_tab(rot, valid, rows, cols)
            tab = np.zeros((128, 2, TT), np.float32)
            tab[:, 0, :] = 1.0
            if rot == 32:
                tab[64:96, 0, :LAT_END] = cf
                tab[64:96, 1, :LAT_END] = sf
            else:
                tab[:, 0, :LAT_END] = np.concatenate([cf, cf], axis=0)
                tab[:, 1, :LAT_END] = np.concatenate([sf, sf], axis=0)
            tabs.append(tab.astype(ml_dtypes.bfloat16))
        m = dict(shared)
        m["xT"] = xTh
        m["consts"] = cs
        m["ident"] = np.eye(128, dtype=np.float32).astype(ml_dtypes.bfloat16)
        m["tab0"] = tabs[0]
        m["tab1"] = tabs[1]
        maps.append(m)
    return maps


_CACHE = {}


def kernel(**inputs):
    maps = host_prep(inputs)
    if "nc" not in _CACHE:
        nc, P = build_nc()
        emit(nc, P)
        _CACHE["nc"] = nc
    nc = _CACHE["nc"]
    res = run_bass_kernel_spmd(nc, maps, core_ids=list(range(8)))
    out = np.zeros((2, SEQ, D), np.float32)
    for r in range(8):
        b, qd = r // 4, r % 4
        o = np.asarray(res.results[r]["out"])
        out[b, qd * T:(qd + 1) * T, :] = o.transpose(2, 1, 0).reshape(T, D)
    return out
```
